# Optimizing a Trainium2 kernel written in Bass

```python
import math
import jax, jax.numpy as jnp
from jax import lax
import numpy as np

D_MODEL = 1024
BATCH = 8
SEQ = 4096
DEPTH = 4

N_MIXERS = 3
N_META = 16
EPS = 1e-6
S5_WIDTH = D_MODEL
S5_GROUP = 16
S5_GROUPS = S5_WIDTH // S5_GROUP
S5_STATE = 64
DT_MIN = 1e-3
DT_MAX = 1e-1
CONV_E = 2 * D_MODEL
CONV_K = 3
POOL_E = 2 * D_MODEL
POOL_WINDOWS = (2, 4, 8, 16)
POOL_GROUP = POOL_E // len(POOL_WINDOWS)

kernel_name = "hybrid_s5_shortconv_pool_interleaved"


def rmsnorm(h, g):
    hf = h.astype(jnp.float32)
    y = hf * lax.rsqrt(jnp.mean(hf * hf, axis=-1, keepdims=True) + EPS)
    return (y * g.astype(jnp.float32)).astype(h.dtype)


def s5_branch(n, w_in, lam_re, lam_im, log_dt, b_re, b_im, c_re, c_im, d_skip, w_glu, b_glu, w_out):
    f32 = jnp.float32
    bsz, L, _ = n.shape
    u, z = jnp.split(n @ w_in, 2, axis=-1)
    uf = u.astype(f32).reshape(bsz, L, S5_GROUPS, S5_GROUP)
    lr = lam_re.astype(f32)
    li = lam_im.astype(f32)
    dt = jnp.exp(log_dt.astype(f32))[:, None]
    mag = jnp.exp(lr * dt)
    ar = mag * jnp.cos(li * dt)
    ai = mag * jnp.sin(li * dt)
    den = lr * lr + li * li
    kr = ((ar - 1.0) * lr + ai * li) / den
    ki = (ai * lr - (ar - 1.0) * li) / den
    br = b_re.astype(f32)
    bi = b_im.astype(f32)
    bbr = kr[..., None] * br - ki[..., None] * bi
    bbi = kr[..., None] * bi + ki[..., None] * br
    xr = jnp.einsum("blgi,gpi->blgp", uf, bbr)
    xi = jnp.einsum("blgi,gpi->blgp", uf, bbi)
    a_r = jnp.broadcast_to(ar[None, None], (1, L, S5_GROUPS, S5_STATE))
    a_i = jnp.broadcast_to(ai[None, None], (1, L, S5_GROUPS, S5_STATE))

    def combine(e1, e2):
        a1r, a1i, b1r, b1i = e1
        a2r, a2i, b2r, b2i = e2
        return (a2r * a1r - a2i * a1i,
                a2r * a1i + a2i * a1r,
                a2r * b1r - a2i * b1i + b2r,
                a2r * b1i + a2i * b1r + b2i)

    _, _, sr, si = lax.associative_scan(combine, (a_r, a_i, xr, xi), axis=1)
    y = (jnp.einsum("blgp,gip->blgi", sr, c_re.astype(f32))
         - jnp.einsum("blgp,gip->blgi", si, c_im.astype(f32))
         + d_skip.astype(f32).reshape(S5_GROUPS, S5_GROUP) * uf)
    y = jax.nn.gelu(y.reshape(bsz, L, S5_WIDTH))
    y = y * jax.nn.sigmoid(y @ w_glu.astype(f32) + b_glu.astype(f32))
    y = y.astype(n.dtype) * jax.nn.silu(z)
    return y @ w_out


def shortconv_branch(n, w_in, conv_w, conv_b, w_out):
    bg, cg, v, z = jnp.split(n @ w_in, 4, axis=-1)
    hc = cg * v
    conv = lax.conv_general_dilated(
        hc, conv_w[:, None, :], window_strides=(1,), padding=[(CONV_K - 1, 0)],
        dimension_numbers=("NWC", "WIO", "NWC"), feature_group_count=CONV_E) + conv_b
    y = bg * conv
    return (y * jax.nn.silu(z)) @ w_out


def pool_branch(n, w_in, w_grp, b_grp, scale, w_out):
    f32 = jnp.float32
    bsz, L, _ = n.shape
    u, z = jnp.split(n @ w_in, 2, axis=-1)
    ug = u.astype(f32).reshape(bsz, L, len(POOL_WINDOWS), POOL_GROUP)
    cs = jnp.cumsum(ug, axis=1)
    t = jnp.arange(1, L + 1, dtype=f32)[:, None]
    outs = []
    for k, w in enumerate(POOL_WINDOWS):
        c = cs[:, :, k]
        lag = jnp.concatenate([jnp.zeros_like(c[:, :w]), c[:, :L - w]], axis=1)
        mixed = (c - lag) / jnp.minimum(t, float(w)) - ug[:, :, k]
        outs.append(mixed @ w_grp[k].astype(f32) + b_grp[k].astype(f32))
    y = jnp.concatenate(outs, axis=-1) * scale.astype(f32)
    y = y.astype(n.dtype) * jax.nn.silu(z)
    return y @ w_out


def _normal(key, shape, std):
    return jax.random.normal(key, shape, jnp.float32) * std


def _s5_params(key, p):
    ks = jax.random.split(key, 12)
    n_idx = jnp.arange(S5_STATE, dtype=jnp.float32)
    return {
        p + "w_in": _normal(ks[0], (D_MODEL, 2 * S5_WIDTH), D_MODEL ** -0.5),
        p + "lam_re": -0.5 + _normal(ks[1], (S5_GROUPS, S5_STATE), 0.01),
        p + "lam_im": math.pi * n_idx[None, :] + _normal(ks[2], (S5_GROUPS, S5_STATE), 0.01),
        p + "log_dt": jax.random.uniform(ks[3], (S5_GROUPS,), jnp.float32,
                                         math.log(DT_MIN), math.log(DT_MAX)),
        p + "b_re": _normal(ks[4], (S5_GROUPS, S5_STATE, S5_GROUP), (2 * S5_GROUP) ** -0.5),
        p + "b_im": _normal(ks[5], (S5_GROUPS, S5_STATE, S5_GROUP), (2 * S5_GROUP) ** -0.5),
        p + "c_re": _normal(ks[6], (S5_GROUPS, S5_GROUP, S5_STATE), (2 * S5_STATE) ** -0.5),
        p + "c_im": _normal(ks[7], (S5_GROUPS, S5_GROUP, S5_STATE), (2 * S5_STATE) ** -0.5),
        p + "d_skip": _normal(ks[8], (S5_WIDTH,), 1.0),
        p + "w_glu": _normal(ks[9], (S5_WIDTH, S5_WIDTH), S5_WIDTH ** -0.5),
        p + "b_glu": _normal(ks[10], (S5_WIDTH,), 0.01),
        p + "w_out": _normal(ks[11], (S5_WIDTH, D_MODEL), S5_WIDTH ** -0.5),
    }


def _conv_params(key, p):
    ks = jax.random.split(key, 4)
    return {
        p + "w_in": _normal(ks[0], (D_MODEL, 4 * CONV_E), D_MODEL ** -0.5),
        p + "conv_w": _normal(ks[1], (CONV_K, CONV_E), CONV_K ** -0.5),
        p + "conv_b": _normal(ks[2], (CONV_E,), 0.01),
        p + "w_out": _normal(ks[3], (CONV_E, D_MODEL), CONV_E ** -0.5),
    }


def _pool_params(key, p):
    ks = jax.random.split(key, 5)
    ng = len(POOL_WINDOWS)
    return {
        p + "w_in": _normal(ks[0], (D_MODEL, 2 * POOL_E), D_MODEL ** -0.5),
        p + "w_grp": _normal(ks[1], (ng, POOL_GROUP, POOL_GROUP), POOL_GROUP ** -0.5),
        p + "b_grp": _normal(ks[2], (ng, POOL_GROUP), 0.01),
        p + "scale": 1.0 + _normal(ks[3], (POOL_E,), 0.02),
        p + "w_out": _normal(ks[4], (POOL_E, D_MODEL), POOL_E ** -0.5),
    }


def setup_inputs(seed: int = 0) -> dict:
    key = jax.random.key(seed)
    ks = jax.random.split(key, 3 + 2 * DEPTH)
    out = {
        "x": _normal(ks[0], (BATCH, SEQ, D_MODEL), 1.0),
        "meta_tokens": _normal(ks[1], (N_META, D_MODEL), 1.0),
    }
    builders = (_s5_params, _conv_params, _pool_params)
    for i in range(DEPTH):
        out["norm%d_g" % i] = 1.0 + _normal(ks[3 + 2 * i], (D_MODEL,), 0.02)
        out.update(builders[i % N_MIXERS](ks[4 + 2 * i], "l%d_" % i))
    out["final_g"] = 1.0 + _normal(ks[2], (D_MODEL,), 0.02)
    return out


def reference(x, meta_tokens,
              norm0_g, l0_w_in, l0_lam_re, l0_lam_im, l0_log_dt, l0_b_re, l0_b_im, l0_c_re, l0_c_im,
              l0_d_skip, l0_w_glu, l0_b_glu, l0_w_out,
              norm1_g, l1_w_in, l1_conv_w, l1_conv_b, l1_w_out,
              norm2_g, l2_w_in, l2_w_grp, l2_b_grp, l2_scale, l2_w_out,
              norm3_g, l3_w_in, l3_lam_re, l3_lam_im, l3_log_dt, l3_b_re, l3_b_im, l3_c_re, l3_c_im,
              l3_d_skip, l3_w_glu, l3_b_glu, l3_w_out,
              final_g):
    bsz = x.shape[0]
    meta = jnp.broadcast_to(meta_tokens[None].astype(x.dtype), (bsz, N_META, D_MODEL))
    h = jnp.concatenate([meta, x], axis=1)
    layers = [
        (norm0_g, (l0_w_in, l0_lam_re, l0_lam_im, l0_log_dt, l0_b_re, l0_b_im, l0_c_re, l0_c_im,
                   l0_d_skip, l0_w_glu, l0_b_glu, l0_w_out)),
        (norm1_g, (l1_w_in, l1_conv_w, l1_conv_b, l1_w_out)),
        (norm2_g, (l2_w_in, l2_w_grp, l2_b_grp, l2_scale, l2_w_out)),
        (norm3_g, (l3_w_in, l3_lam_re, l3_lam_im, l3_log_dt, l3_b_re, l3_b_im, l3_c_re, l3_c_im,
                   l3_d_skip, l3_w_glu, l3_b_glu, l3_w_out)),
    ]
    mixers = (s5_branch, shortconv_branch, pool_branch)
    for i in range(DEPTH):
        g, params = layers[i]
        h = h + mixers[i % N_MIXERS](rmsnorm(h, g), *params)
    return rmsnorm(h[:, N_META:], final_g)
```

```python
import math
from contextlib import ExitStack
import numpy as np
import concourse.bass as bass
import concourse.mybir as mybir
from concourse.bass_utils import run_bass_kernel_spmd

F32 = mybir.dt.float32
BF16 = mybir.dt.bfloat16
I32 = mybir.dt.int32
AF = mybir.ActivationFunctionType
ALU = mybir.AluOpType
P = 128
NMETA = 16
SEQ = 4096
DM = 1024
EPS = 1e-6
PI = math.pi


class Res:
    __slots__ = ("name", "w", "rs", "sem", "ndma", "grp", "multi", "excl")

    def __init__(self, name, grp=None, excl=False):
        self.name = name; self.w = None; self.rs = {}; self.sem = None; self.ndma = 0; self.grp = grp; self.multi = None
        self.excl = excl


class SemGroup:
    def __init__(self, name):
        self.name = name; self.sem = None; self.total = 0


class Op:
    __slots__ = ("eng", "fn", "deps", "dma", "semres", "needs_inc", "ev", "phase")

    def __init__(self, eng, fn, dma):
        self.eng = eng; self.fn = fn; self.deps = []; self.dma = dma; self.semres = None
        self.needs_inc = False; self.ev = None


class Sched:
    ENG = ("pe", "act", "dve", "pool", "sp")

    def __init__(self, nc, stack):
        self.nc = nc; self.stack = stack
        self.ops = {e: [] for e in self.ENG}
        self.all_dma = []
        self.nsem = 0

    def new_sem(self, name):
        self.nsem += 1
        return self.stack.enter_context(self.nc.semaphore(name))

    frozen = False
    phase = ""

    def add(self, eng, fn, reads=(), writes=(), dma=False, semres=None, part_of=None):
        if self.frozen:
            return None
        op = Op(eng, fn, dma)
        op.phase = self.phase
        deps = []
        rr = []
        for r in reads:
            if r.multi is not None: rr.extend(r.multi)
            else: rr.append(r)
        reads = rr
        for r in reads:
            if r.w is not None: deps.append(r.w)
            if r.excl:
                for k_, v_ in r.rs.items():
                    if k_ != eng: deps.append(v_)
        for w in writes:
            if w.w is not None: deps.append(w.w)
            deps.extend(w.rs.values())
        seen = set(); out = []
        for d in deps:
            if id(d) in seen or d is op or d is part_of: continue
            seen.add(id(d))
            if not d.dma and not dma and d.eng == eng:
                if eng == "pe" or not self.ops[eng] or self.ops[eng][-1] is not d:
                    continue
                self.n_adj = getattr(self, "n_adj", 0) + 1
            out.append(d); d.needs_inc = True
        op.deps = out
        for r in reads: r.rs[eng if not dma else ("dma", id(op))] = op
        for w in writes: w.w = op; w.rs = {}
        if dma:
            op.semres = semres if semres is not None else writes[0]
            self.all_dma.append(op)
            op.needs_inc = True
        self.ops[eng].append(op)
        return op

    def realias(self, old, new):
        users = []
        for o in old:
            if o.w is not None: users.append(o.w)
            users.extend(list(o.rs.values()))
        for n in new:
            for v in users: n.rs[("r", id(v))] = v

    def emit(self):
        nc = self.nc
        MAXV = 30000
        nes = 0
        for op in self.all_dma:
            r = op.semres
            if r.grp is not None: r.grp.total += 1
        for e in self.ENG:
            cur = None; cnt = 0
            for op in self.ops[e]:
                if op.dma:
                    r = op.semres
                    if r.grp is not None:
                        g = r.grp
                        if g.sem is None: g.sem = self.new_sem("g_" + g.name)
                        op.ev = (g.sem, 16 * g.total)
                    else:
                        if r.sem is None: r.sem = {}
                        if e not in r.sem: r.sem[e] = [self.new_sem("d_%s_%s" % (r.name, e)), 0]
                        r.sem[e][1] += 1
                        op.ev = (r.sem[e][0], 16 * r.sem[e][1])
                elif op.needs_inc:
                    if cur is None or cnt >= MAXV:
                        cur = self.new_sem("e_%s_%d" % (e, nes)); nes += 1; cnt = 0
                    cnt += 1
                    op.ev = (cur, cnt)
        last = {}
        for op in self.all_dma: last[op.ev[0].name] = op.ev
        final_waits = list(last.values())
        engobj = {"pe": "tensor", "act": "scalar", "dve": "vector", "pool": "gpsimd", "sp": "sync"}
        sched = self
        with nc.Block() as block:
            def mk(ename):
                def body(eng):
                    known = {}
                    for op in sched.ops[ename]:
                        for d in op.deps:
                            sem, val = d.ev
                            if known.get(sem.name, 0) >= val: continue
                            known[sem.name] = val
                            eng.wait_ge(sem, val)
                        ins = op.fn(eng)
                        if _DBG_TAGS is not None:
                            try: _DBG_TAGS[ins.ins.name] = (ename, op.phase)
                            except Exception: pass
                        if op.ev is not None:
                            ins.then_inc(op.ev[0], 16 if op.dma else 1)
                    if ename == "sp":
                        for sem, val in final_waits:
                            eng.wait_ge(sem, val)
                return body
            for ename in self.ENG:
                getattr(block, engobj[ename])(mk(ename))


_DBG_TAGS = None
_DBG_OFFS = None
CHUNKS = [(0, 1040), (1040, 1024), (2064, 1024), (3088, 1024)]
TMAX = 1040
PARAM_NAMES = [
    "meta_tokens", "norm0_g", "l0_w_in", "l0_lam_re", "l0_lam_im", "l0_log_dt", "l0_b_re", "l0_b_im",
    "l0_c_re", "l0_c_im", "l0_d_skip", "l0_w_glu", "l0_b_glu", "l0_w_out",
    "norm1_g", "l1_w_in", "l1_conv_w", "l1_conv_b", "l1_w_out",
    "norm2_g", "l2_w_in", "l2_w_grp", "l2_b_grp", "l2_scale", "l2_w_out",
    "norm3_g", "l3_w_in", "l3_lam_re", "l3_lam_im", "l3_log_dt", "l3_b_re", "l3_b_im",
    "l3_c_re", "l3_c_im", "l3_d_skip", "l3_w_glu", "l3_b_glu", "l3_w_out", "final_g"]
PARAM_SHAPES = {
    "meta_tokens": [16, 1024], "norm0_g": [1024], "l0_w_in": [1024, 2048], "l0_lam_re": [64, 64], "l0_lam_im": [64, 64],
    "l0_log_dt": [64], "l0_b_re": [64, 64, 16], "l0_b_im": [64, 64, 16], "l0_c_re": [64, 16, 64], "l0_c_im": [64, 16, 64],
    "l0_d_skip": [1024], "l0_w_glu": [1024, 1024], "l0_b_glu": [1024], "l0_w_out": [1024, 1024],
    "norm1_g": [1024], "l1_w_in": [1024, 8192], "l1_conv_w": [3, 2048], "l1_conv_b": [2048], "l1_w_out": [2048, 1024],
    "norm2_g": [1024], "l2_w_in": [1024, 4096], "l2_w_grp": [4, 512, 512], "l2_b_grp": [4, 512], "l2_scale": [2048],
    "l2_w_out": [2048, 1024], "norm3_g": [1024], "l3_w_in": [1024, 2048], "l3_lam_re": [64, 64], "l3_lam_im": [64, 64],
    "l3_log_dt": [64], "l3_b_re": [64, 64, 16], "l3_b_im": [64, 64, 16], "l3_c_re": [64, 16, 64], "l3_c_im": [64, 16, 64],
    "l3_d_skip": [1024], "l3_w_glu": [1024, 1024], "l3_b_glu": [1024], "l3_w_out": [1024, 1024], "final_g": [1024]}


def host_consts():
    c = {}
    c["c_ident"] = np.eye(128, dtype=np.float32)
    idx = np.arange(128)
    c["c_mask"] = (idx[:, None] // 16 <= idx[None, :] // 16).astype(np.float32)
    selC = np.zeros((128, 2, 64), np.float32)
    for gl in range(8):
        for o in range(16):
            selC[gl * 16 + o, gl % 2, (gl // 2) * 16 + o] = 1.0
    c["c_selC"] = selC
    selG = np.zeros((64, 2, 32), np.float32)
    for g in range(64):
        selG[g, g % 2, g // 2] = 1.0
    c["c_selG"] = selG
    c["c_invc"] = np.tile((1.0 / np.arange(1, 17, dtype=np.float32))[None, :], (128, 1)).astype(np.float32)
    c["c_ones"] = np.ones((128, 128), np.float32)
    return c


CONST_SHAPES = {"c_ident": [128, 128], "c_mask": [128, 128], "c_selC": [128, 2, 64], "c_selG": [64, 2, 32],
                "c_invc": [128, 16], "c_ones": [128, 128]}


def chunk_tiles(ci):
    return [(0, 16), (16, 512), (528, 512)] if ci == 0 else [(0, 512), (512, 512)]


def chunk_pieces(ci):
    return [(0, 2), (2, 128)] if ci == 0 else [(0, 128)]


def build_program(layers=(0, 1, 2, 3), nchunks=4):
    nc = bass.Bass("TRN2", target_bir_lowering=False)
    D = {}
    D["x"] = nc.dram_tensor("x", [SEQ, DM], F32, kind="ExternalInput").ap()
    for n in PARAM_NAMES:
        D[n] = nc.dram_tensor(n, PARAM_SHAPES[n], F32, kind="ExternalInput").ap()
    for n, s in CONST_SHAPES.items():
        D[n] = nc.dram_tensor(n, s, F32, kind="ExternalInput").ap()
    out_d = nc.dram_tensor("out", [SEQ, DM], F32, kind="ExternalOutput").ap()
    scr = {}
    for l in [l_ for l_ in (0, 3) if l_ in layers]:
        scr[("T", l)] = nc.dram_tensor("scrT%d" % l, [128, 64 * 128], BF16, kind="Internal").ap()
        scr[("B", l)] = nc.dram_tensor("scrB%d" % l, [128, 64 * 128], BF16, kind="Internal").ap()
        scr[("C", l)] = nc.dram_tensor("scrC%d" % l, [128, 64 * 128], BF16, kind="Internal").ap()
        scr[("B2", l)] = nc.dram_tensor("scrB2%d" % l, [128, 64 * 128], BF16, kind="Internal").ap()

    with ExitStack() as st:
        S = Sched(nc, st)
        import os as _os
        ARENA_BYTES = int(_os.environ.get('KARENA', '207872'))
        arena = st.enter_context(nc.sbuf_tensor("arena", [128, ARENA_BYTES // 4], F32))
        mem_top = [0]

        def view_at(off, shape, dt):
            n = 1
            for s_ in shape: n *= s_
            nb = n * (2 if dt == BF16 else 4)
            assert off % 4 == 0 and nb % 4 == 0 and off + nb <= ARENA_BYTES, (off, nb)
            ap = arena[:, off // 4:(off + nb) // 4]
            if dt != F32: ap = ap.bitcast(dt)
            if len(shape) == 2:
                ap = ap.rearrange("p (a b) -> p a b", b=shape[1])
            elif len(shape) == 3:
                ap = ap.rearrange("p (a b c) -> p a b c", b=shape[1], c=shape[2])
            elif len(shape) == 4:
                ap = ap.rearrange("p (a b c d) -> p a b c d", b=shape[1], c=shape[2], d=shape[3])
            return ap

        def alloc(shape, dt):
            n = 1
            for s_ in shape: n *= s_
            nb = n * (2 if dt == BF16 else 4)
            nb = (nb + 31) // 32 * 32
            off = mem_top[0]; mem_top[0] += nb
            return view_at(off, shape, dt), off

        kint_t = st.enter_context(nc.sbuf_tensor("kint_t", [128, 32], I32))
        psum = [st.enter_context(nc.psum_tensor("ps%d" % i, [128, 512], F32)) for i in range(8)]
        psres = [Res("ps%d" % i, excl=True) for i in range(8)]
        pidx = [0]

        def bank():
            i = pidx[0]; pidx[0] = (i + 1) % 8
            return psum[i], psres[i]

        def op(eng, method, reads, writes, *args, **kw):
            return S.add(eng, lambda e: getattr(e, method)(*args, **kw), reads, writes)

        import os
        SKIP = os.environ.get("KSKIP", "")

        def dma(eng, out, in_, reads, writes, semres=None, slow=False, part_of=None):
            if slow and SKIP == "slow":
                return None
            if len(writes) == 1 and writes[0].multi is not None:
                nr_ = Res("c%d" % len(writes[0].multi)); writes[0].multi.append(nr_); writes = [nr_]
            if slow:
                return S.add(eng, lambda e: e.dma_start(out=out, in_=in_, allow_slow_non_contiguous=True), reads, writes, dma=True, semres=semres, part_of=part_of)
            return S.add(eng, lambda e: e.dma_start(out=out, in_=in_), reads, writes, dma=True, semres=semres, part_of=part_of)

        def mm(out, lhsT, rhs, start, stop, reads, writes):
            return S.add("pe", lambda e: e.matmul(out, lhsT, rhs, start=start, stop=stop), reads, writes)

        def tr(out, in_, ident, reads, writes):
            return S.add("pe", lambda e: e.transpose(out, in_, ident), reads, writes)

        h, _ = alloc([8, TMAX], F32); Rhh = [[Res("h%d_%d" % (b, t)) for t in range(3)] for b in range(8)]
        Rh_all = [r for rr_ in Rhh for r in rr_]
        cur_ci = [0]

        def rh(b, t0):
            for ti_, (a0, an) in enumerate(chunk_tiles(cur_ci[0])):
                if a0 <= t0 < a0 + an: return Rhh[b][ti_]
            raise AssertionError(t0)
        hn, off_hn = alloc([8, TMAX], BF16); Rhn = Res("hn")
        y3, off_y3 = alloc([16, TMAX], BF16); Ry3 = [Res("y3_%d" % b) for b in range(16)]
        NSLOT = int(_os.environ.get('KNSLOT', '5'))
        ring = []; Rring = []
        for i in range(NSLOT):
            v, o_ = alloc([4096], BF16); ring.append((v, o_)); Rring.append(Res("ring%d" % i))
        ridx = [0]
        WORK_BYTES = 49920
        work0 = mem_top[0]; mem_top[0] += WORK_BYTES
        stage = []; Rstage = []
        off_stage = mem_top[0]
        for i in range(2):
            v, _ = alloc([1024], F32); stage.append(v); Rstage.append(Res("stage%d" % i))
        sq = []; Rsq = []
        for i in range(2):
            v, _ = alloc([512], BF16); sq.append(v); Rsq.append(Res("sq%d" % i))
        rsb = []; Rrs = []
        for i in range(2):
            v, _ = alloc([512], F32); rsb.append(v); Rrs.append(Res("rs%d" % i))
        pg = SemGroup("params")
        identf, _ = alloc([128], F32); identb, _ = alloc([128], BF16); onesb, _ = alloc([128], BF16)
        maskf, _ = alloc([128], F32); invc, _ = alloc([16], F32)
        Rc = Res("consts"); Rc.multi = []
        gains, _ = alloc([5, 8], F32)
        bglu, _ = alloc([2, 8], F32)
        cw, _ = alloc([3, 16], F32); cb, _ = alloc([16], F32)
        pscale, _ = alloc([16], F32); pbg, _ = alloc([16], F32); pbs, _ = alloc([16], F32)
        Dg, _ = alloc([2, 64], F32)
        ArAr, _ = alloc([2, 32, 2], F32); AiS, _ = alloc([2, 32, 2], F32)
        A2rr, _ = alloc([2, 32, 2], F32); A2is, _ = alloc([2, 32, 2], F32)
        Rtab = [Res("tab0"), Res("tab1")]
        chist, _ = alloc([16, 2], F32); Rchist = Res("chist")
        phist, _ = alloc([16, 16], F32); Rphist = Res("phist")
        Scarry = None
        assert mem_top[0] <= ARENA_BYTES, mem_top[0]

        def wslot(nelem_shape):
            i = ridx[0]; ridx[0] = (i + 1) % NSLOT
            v, o_ = ring[i]
            return view_at(o_, nelem_shape, BF16), Rring[i]

        if _os.environ.get("KDUMP", ""):
            Rinit = Res("init")
            for i_ in range(0, ARENA_BYTES // 4, 8192):
                op("dve", "memset", [], [Rinit], arena[:, i_:min(i_ + 8192, ARENA_BYTES // 4)], 0.0)
            op("act", "copy", [Rinit], [Rinit], out=arena[:, 0:8], in_=arena[:, 0:8])
            op("pool", "tensor_copy", [Rinit], [Rinit], out=arena[:, 0:8], in_=arena[:, 0:8])
            S.add("pe", lambda e: e.matmul(psum[0][:, 0:8], arena[:, 0:128], arena[:, 0:8], start=True, stop=True), [Rinit], [psres[0]])
            S.add("sp", lambda e: e.dma_start(out=arena[:, 0:8], in_=arena[:, 8:16]), [Rinit], [Rinit], dma=True)
        dma("sp", identf, D["c_ident"], [], [Rc])
        dma("pool", identb, D["c_ident"], [], [Rc])
        dma("pool", onesb, D["c_ones"], [], [Rc])
        dma("sp", maskf, D["c_mask"], [], [Rc])
        dma("sp", invc, D["c_invc"], [], [Rc])
        for i, nm in enumerate(["norm0_g", "norm1_g", "norm2_g", "norm3_g", "final_g"]):
            dma("sp", gains[:, i, :], D[nm].rearrange("(b p) -> p b", p=128), [], [Rc], slow=True)
        for i, nm in enumerate(["l0_b_glu", "l3_b_glu"]):
            dma("sp", bglu[:, i, :], D[nm].rearrange("(b p) -> p b", p=128), [], [Rc], slow=True)
        for k in range(3):
            dma("sp", cw[:, k, :], D["l1_conv_w"][k].rearrange("(b p) -> p b", p=128), [], [Rc], slow=True)
        dma("sp", cb, D["l1_conv_b"].rearrange("(b p) -> p b", p=128), [], [Rc], slow=True)
        dma("sp", pscale, D["l2_scale"].rearrange("(b p) -> p b", p=128), [], [Rc], slow=True)
        for k_ in range(4):
            dma("sp", pbg[:, 4 * k_:4 * k_ + 4], D["l2_b_grp"][k_].rearrange("(b p) -> p b", p=128), [], [Rc], slow=True)
        for li, nm in enumerate(["l0_d_skip", "l3_d_skip"]):
            for tau in range(8):
                dma("sp", Dg[tau * 16:(tau + 1) * 16, li, :], D[nm].rearrange("(g i) -> i g", i=16), [], [Rc], slow=True)
        Rpbs = Res("pbs")
        op("dve", "tensor_tensor", [Rc], [Rpbs], out=pbs, in0=pbg, in1=pscale, op=ALU.mult)
        op("dve", "memset", [], [Rphist], phist, 0.0)
        op("dve", "memset", [], [Rchist], chist, 0.0)

        KSTOP = _os.environ.get("KSTOP", "")
        if KSTOP == "consts": S.frozen = True
        s5layers = [l for l in layers if l in (0, 3)]
        Rscr = {k: Res("scr%s%d" % k) for k in scr}
        if SKIP == "arena":
            pass

        PRO_RES = {}

        def s5_prologue(l):
            S.phase = "pro%d" % l
            li = 0 if l == 0 else 1
            pre = "l%d_" % l
            base = [0]

            def A(shape, dt=F32):
                n = 1
                for s_ in shape: n *= s_
                nb = (n * (2 if dt == BF16 else 4) + 31) // 32 * 32
                off = base[0]; base[0] += nb
                assert base[0] <= work0 + WORK_BYTES
                return view_at(off, shape, dt)
            Rp = PRO_RES

            def R(n):
                if n not in Rp: Rp[n] = Res("p_%s" % n)
                return Rp[n]
            lamR = A([64]); lamI = A([64]); ldt = A([1]); ldtb = A([64]); selG = A([2, 32]); selC = A([2, 64])
            dma("sp", lamR[0:64], D[pre + "lam_re"], [], [R("lamR")])
            dma("sp", lamI[0:64], D[pre + "lam_im"], [], [R("lamI")])
            dma("sp", ldt[0:64], D[pre + "log_dt"].rearrange("(g o) -> g o", o=1), [], [R("ldt")])
            dma("sp", selG[0:64], D["c_selG"], [], [R("selG")])
            dma("sp", selC, D["c_selC"], [], [R("selC")])
            op("dve", "tensor_copy", [R("ldt")], [R("ldtb")], out=ldtb[0:64], in_=ldt[0:64, 0:1].to_broadcast([64, 64]))
            pb_, pr_ = bank()
            for par in range(2):
                rows = slice(par * 64, par * 64 + 64)
                mm(pb_[rows, 0:32], lamR[0:64], selG[0:64, par, :], True, True, [R("lamR"), R("selG")], [pr_])
                mm(pb_[rows, 32:64], lamI[0:64], selG[0:64, par, :], True, True, [R("lamI"), R("selG")], [pr_])
                mm(pb_[rows, 64:96], ldtb[0:64], selG[0:64, par, :], True, True, [R("ldtb"), R("selG")], [pr_])
            sm = A([24, 32])
            Rsm = R("sm")
            lr, li_, ld, dt, x1, mag, ang, v_, kf, r_, m_, sn, cs, ar, ai, am1, den, t_, kr, ki = [sm[:, i, :] for i in range(20)]
            kint = kint_t[:, :]; _ = A([32], I32)
            op("act", "copy", [pr_], [Rsm], out=sm[:, 0:3, :], in_=pb_[:, 0:96].rearrange("p (a b) -> p a b", b=32))

            def dv(method, *a, **k):
                return op("dve", method, [Rsm], [Rsm], *a, **k)

            def ac(*a, **k):
                return op("act", "activation", [Rsm], [Rsm], *a, **k)
            ac(out=dt, in_=ld, func=AF.Exp)
            dv("tensor_tensor", out=x1, in0=lr, in1=dt, op=ALU.mult)
            ac(out=mag, in_=x1, func=AF.Exp)
            dv("tensor_tensor", out=ang, in0=li_, in1=dt, op=ALU.mult)
            ac(out=sn, in_=ang, func=AF.Sin, scale=1.0 / 8)
            ac(out=v_, in_=ang, func=AF.Sin, scale=1.0 / 16)
            dv("tensor_tensor", out=v_, in0=v_, in1=v_, op=ALU.mult)
            dv("tensor_scalar", out=cs, in0=v_, scalar1=-2.0, scalar2=1.0, op0=ALU.mult, op1=ALU.add)
            for _d in range(3):
                dv("tensor_tensor", out=kf, in0=cs, in1=cs, op=ALU.mult)
                dv("tensor_tensor", out=r_, in0=sn, in1=sn, op=ALU.mult)
                dv("scalar_tensor_tensor", out=sn, in0=cs, scalar=2.0, in1=sn, op0=ALU.mult, op1=ALU.mult)
                dv("tensor_tensor", out=cs, in0=kf, in1=r_, op=ALU.subtract)
            dv("tensor_tensor", out=ar, in0=mag, in1=cs, op=ALU.mult)
            dv("tensor_tensor", out=ai, in0=mag, in1=sn, op=ALU.mult)
            dv("tensor_scalar", out=am1, in0=ar, scalar1=-1.0, scalar2=None, op0=ALU.add)
            dv("tensor_tensor", out=den, in0=lr, in1=lr, op=ALU.mult)
            dv("tensor_tensor", out=t_, in0=li_, in1=li_, op=ALU.mult)
            dv("tensor_tensor", out=den, in0=den, in1=t_, op=ALU.add)
            dv("reciprocal", out=den, in_=den)
            dv("tensor_tensor", out=kr, in0=am1, in1=lr, op=ALU.mult)
            dv("tensor_tensor", out=t_, in0=ai, in1=li_, op=ALU.mult)
            dv("tensor_tensor", out=kr, in0=kr, in1=t_, op=ALU.add)
            dv("tensor_tensor", out=kr, in0=kr, in1=den, op=ALU.mult)
            dv("tensor_tensor", out=ki, in0=ai, in1=lr, op=ALU.mult)
            dv("tensor_tensor", out=t_, in0=am1, in1=li_, op=ALU.mult)
            dv("tensor_tensor", out=ki, in0=ki, in1=t_, op=ALU.subtract)
            dv("tensor_tensor", out=ki, in0=ki, in1=den, op=ALU.mult)
            EPr = A([32, 9]); EPi = A([32, 9]); ERr = A([32, 8]); ERi = A([32, 8])
            dv("memset", EPr[:, :, 0], 1.0); dv("memset", EPi[:, :, 0], 0.0)
            dv("tensor_copy", out=EPr[:, :, 1], in_=ar); dv("tensor_copy", out=EPi[:, :, 1], in_=ai)
            for q in range(2, 9):
                dv("tensor_tensor", out=EPr[:, :, q], in0=EPr[:, :, q - 1], in1=ar, op=ALU.mult)
                dv("tensor_tensor", out=t_, in0=EPi[:, :, q - 1], in1=ai, op=ALU.mult)
                dv("tensor_tensor", out=EPr[:, :, q], in0=EPr[:, :, q], in1=t_, op=ALU.subtract)
                dv("tensor_tensor", out=EPi[:, :, q], in0=EPr[:, :, q - 1], in1=ai, op=ALU.mult)
                dv("tensor_tensor", out=t_, in0=EPi[:, :, q - 1], in1=ar, op=ALU.mult)
                dv("tensor_tensor", out=EPi[:, :, q], in0=EPi[:, :, q], in1=t_, op=ALU.add)
            for tau in range(8):
                dv("tensor_copy", out=ERr[:, :, tau], in_=EPr[:, :, 7 - tau])
                dv("tensor_copy", out=ERi[:, :, tau], in_=EPi[:, :, 7 - tau])
            Ir = sm[:, 20, :]; Ii = sm[:, 21, :]; n8 = sm[:, 22, :]
            dv("tensor_tensor", out=n8, in0=EPr[:, :, 8], in1=EPr[:, :, 8], op=ALU.mult)
            dv("tensor_tensor", out=t_, in0=EPi[:, :, 8], in1=EPi[:, :, 8], op=ALU.mult)
            dv("tensor_tensor", out=n8, in0=n8, in1=t_, op=ALU.add)
            dv("reciprocal", out=n8, in_=n8)
            dv("tensor_tensor", out=Ir, in0=EPr[:, :, 8], in1=n8, op=ALU.mult)
            dv("scalar_tensor_tensor", out=Ii, in0=EPi[:, :, 8], scalar=-1.0, in1=n8, op0=ALU.mult, op1=ALU.mult)
            op("dve", "tensor_copy", [Rsm], [Rtab[li]], out=ArAr[:, li, :, 0], in_=EPr[:, :, 8])
            op("dve", "tensor_copy", [Rsm], [Rtab[li]], out=ArAr[:, li, :, 1], in_=EPr[:, :, 8])
            op("dve", "tensor_scalar", [Rsm], [Rtab[li]], out=AiS[:, li, :, 0], in0=EPi[:, :, 8], scalar1=-1.0, scalar2=None, op0=ALU.mult)
            op("dve", "tensor_copy", [Rsm], [Rtab[li]], out=AiS[:, li, :, 1], in_=EPi[:, :, 8])
            dv("tensor_tensor", out=kf, in0=EPr[:, :, 8], in1=EPr[:, :, 8], op=ALU.mult)
            dv("tensor_tensor", out=r_, in0=EPi[:, :, 8], in1=EPi[:, :, 8], op=ALU.mult)
            dv("tensor_tensor", out=kf, in0=kf, in1=r_, op=ALU.subtract)
            dv("scalar_tensor_tensor", out=r_, in0=EPr[:, :, 8], scalar=2.0, in1=EPi[:, :, 8], op0=ALU.mult, op1=ALU.mult)
            op("dve", "tensor_copy", [Rsm], [Rtab[li]], out=A2rr[:, li, :, 0], in_=kf)
            op("dve", "tensor_copy", [Rsm], [Rtab[li]], out=A2rr[:, li, :, 1], in_=kf)
            op("dve", "tensor_scalar", [Rsm], [Rtab[li]], out=A2is[:, li, :, 0], in0=r_, scalar1=-1.0, scalar2=None, op0=ALU.mult)
            op("dve", "tensor_copy", [Rsm], [Rtab[li]], out=A2is[:, li, :, 1], in_=r_)
            Bre = A([32, 16]); Bim = A([32, 16]); bbr = A([32, 16]); bbi = A([32, 16]); tb = A([32, 16])
            Cre = A([32, 16]); Cim = A([32, 16])
            for par in range(2):
                rows = slice(par * 64, par * 64 + 64)
                for nm, dst in (("b_re", Bre), ("b_im", Bim)):
                    src = D[pre + nm].rearrange("(g2 q) p i -> q p g2 i", q=2)[par]
                    dma("sp", dst[rows], src, [], [R("Bsrc")])
            Xt = A([8, 64])
            for nm, dst in (("c_re", Cre), ("c_im", Cim)):
                srcv = D[pre + nm].rearrange("(a gl) o p -> a (gl o) p", gl=8)
                for a in range(8):
                    dma("sp", Xt[:, a, :], srcv[a], [], [R("Xt%d" % a)])
                pb_, pr_ = bank()
                for a in range(8):
                    for par in range(2):
                        rows = slice(par * 64, par * 64 + 64)
                        mm(pb_[rows, a * 64:(a + 1) * 64], Xt[:, a, :], selC[:, par, :], True, True, [R("Xt%d" % a), R("selC")], [pr_])
                op("act", "copy", [pr_], [R("Csrc")], out=dst, in_=pb_[:, 0:512].rearrange("p (a b) -> p a b", b=16))
            RB = R("bb")
            krb = kr.unsqueeze(2).to_broadcast([128, 32, 16]); kib = ki.unsqueeze(2).to_broadcast([128, 32, 16])
            op("dve", "tensor_tensor", [Rsm, R("Bsrc")], [RB], out=bbr, in0=Bre, in1=krb, op=ALU.mult)
            op("dve", "tensor_tensor", [Rsm, R("Bsrc")], [RB], out=tb, in0=Bim, in1=kib, op=ALU.mult)
            op("dve", "tensor_tensor", [RB], [RB], out=bbr, in0=bbr, in1=tb, op=ALU.subtract)
            op("dve", "tensor_tensor", [Rsm, R("Bsrc")], [RB], out=bbi, in0=Bim, in1=krb, op=ALU.mult)
            op("dve", "tensor_tensor", [Rsm, R("Bsrc")], [RB], out=tb, in0=Bre, in1=kib, op=ALU.mult)
            op("dve", "tensor_tensor", [RB], [RB], out=bbi, in0=bbi, in1=tb, op=ALU.add)
            Bqr = A([16, 8, 16]); Bqi = A([16, 8, 16]); Bmr = A([16, 8, 16]); Bmi = A([16, 8, 16])
            C1r = A([16, 8, 16]); C1i = A([16, 8, 16]); t1 = A([16, 8, 16]); t2 = A([16, 8, 16])
            Tsb = A([32, 128], BF16); Bsb = A([32, 128], BF16); Csb = A([16, 2, 128], BF16)
            B2r = A([16, 8, 16]); B2i = A([16, 8, 16]); B2sb = A([32, 128], BF16)
            tmpT = [A([128]), A([128])]
            RT = [R("tmpT0"), R("tmpT1")]
            for hf in range(2):
                gs = slice(hf * 16, hf * 16 + 16)
                sh = [128, 16, 8, 16]
                RA = R("big")
                ErB = ERr[:, gs, :].unsqueeze(3).to_broadcast(sh); EiB = ERi[:, gs, :].unsqueeze(3).to_broadcast(sh)
                brB = bbr[:, gs, :].unsqueeze(2).to_broadcast(sh); biB = bbi[:, gs, :].unsqueeze(2).to_broadcast(sh)

                def big(eng, method, *a, **k):
                    return op(eng, method, [Rsm, RB, R("Csrc"), RA], [RA], *a, **k)
                big("dve", "tensor_tensor", out=Bqr, in0=ErB, in1=brB, op=ALU.mult)
                big("dve", "tensor_tensor", out=t1, in0=EiB, in1=biB, op=ALU.mult)
                big("dve", "tensor_tensor", out=Bqr, in0=Bqr, in1=t1, op=ALU.subtract)
                big("dve", "tensor_tensor", out=Bqi, in0=ErB, in1=biB, op=ALU.mult)
                big("dve", "tensor_tensor", out=t1, in0=EiB, in1=brB, op=ALU.mult)
                big("dve", "tensor_tensor", out=Bqi, in0=Bqi, in1=t1, op=ALU.add)
                A8r = EPr[:, gs, 8].unsqueeze(2).unsqueeze(3).to_broadcast(sh); A8i = EPi[:, gs, 8].unsqueeze(2).unsqueeze(3).to_broadcast(sh)
                big("dve", "tensor_tensor", out=t1, in0=Bqr, in1=A8r, op=ALU.mult)
                big("dve", "tensor_tensor", out=t2, in0=Bqi, in1=A8i, op=ALU.mult)
                big("dve", "tensor_tensor", out=B2r, in0=t1, in1=t2, op=ALU.subtract)
                big("dve", "tensor_tensor", out=t1, in0=Bqi, in1=A8r, op=ALU.mult)
                big("dve", "tensor_tensor", out=t2, in0=Bqr, in1=A8i, op=ALU.mult)
                big("dve", "tensor_tensor", out=B2i, in0=t1, in1=t2, op=ALU.add)
                IrB = Ir[:, gs].unsqueeze(2).unsqueeze(3).to_broadcast(sh); IiB = Ii[:, gs].unsqueeze(2).unsqueeze(3).to_broadcast(sh)
                big("dve", "tensor_tensor", out=Bmr, in0=Bqr, in1=IrB, op=ALU.mult)
                big("dve", "tensor_tensor", out=t1, in0=Bqi, in1=IiB, op=ALU.mult)
                big("dve", "tensor_tensor", out=Bmr, in0=Bmr, in1=t1, op=ALU.subtract)
                big("dve", "tensor_tensor", out=Bmi, in0=Bqi, in1=IrB, op=ALU.mult)
                big("dve", "tensor_tensor", out=t1, in0=Bqr, in1=IiB, op=ALU.mult)
                big("dve", "tensor_tensor", out=Bmi, in0=Bmi, in1=t1, op=ALU.add)
                E1r = EPr[:, gs, 1:9].unsqueeze(3).to_broadcast(sh); E1i = EPi[:, gs, 1:9].unsqueeze(3).to_broadcast(sh)
                crB = Cre[:, gs, :].unsqueeze(2).to_broadcast(sh); ciB = Cim[:, gs, :].unsqueeze(2).to_broadcast(sh)
                big("dve", "tensor_tensor", out=C1r, in0=E1r, in1=crB, op=ALU.mult)
                big("dve", "tensor_tensor", out=t1, in0=E1i, in1=ciB, op=ALU.mult)
                big("dve", "tensor_tensor", out=C1r, in0=C1r, in1=t1, op=ALU.subtract)
                big("dve", "tensor_tensor", out=t1, in0=E1i, in1=crB, op=ALU.mult)
                big("dve", "tensor_tensor", out=t2, in0=E1r, in1=ciB, op=ALU.mult)
                big("dve", "scalar_tensor_tensor", out=C1i, in0=t1, scalar=-1.0, in1=t2, op0=ALU.mult, op1=ALU.subtract)
                Rout = R("outsb")
                op("act", "copy", [RA], [Rout], out=Csb[:, :, 0, :], in_=C1r.rearrange("p a b c -> p a (b c)"))
                op("act", "copy", [RA], [Rout], out=Csb[:, :, 1, :], in_=C1i.rearrange("p a b c -> p a (b c)"))
                for gl in range(32):
                    g = hf * 32 + gl; g2l = gl // 2; par = gl % 2
                    rows = slice(par * 64, par * 64 + 64)
                    pb_, pr_ = bank()
                    mm(pb_[:, 0:128], Bmr[rows, g2l].rearrange("p a b -> p (a b)"), C1r[rows, g2l].rearrange("p a b -> p (a b)"), True, False, [RA], [pr_])
                    mm(pb_[:, 0:128], Bmi[rows, g2l].rearrange("p a b -> p (a b)"), C1i[rows, g2l].rearrange("p a b -> p (a b)"), False, True, [RA], [pr_])
                    tt = tmpT[gl % 2]; rt = RT[gl % 2]
                    op("dve", "tensor_tensor", [pr_, Rc], [rt], out=tt, in0=pb_[:, 0:128], in1=maskf, op=ALU.mult)
                    op("dve", "scalar_tensor_tensor", [rt, Rc], [Rout], out=Tsb[:, gl, :], in0=identf, scalar=Dg[:, li, g:g + 1], in1=tt, op0=ALU.mult, op1=ALU.add)
                    pb2, pr2 = bank()
                    tr(pb2[:, 0:64], Bqr[rows, g2l].rearrange("p a b -> p (a b)"), identf[rows, par * 64:par * 64 + 64], [RA, Rc], [pr2])
                    tr(pb2[:, 64:128], Bqi[rows, g2l].rearrange("p a b -> p (a b)"), identf[rows, par * 64:par * 64 + 64], [RA, Rc], [pr2])
                    op("act", "copy", [pr2], [Rout], out=Bsb[:, gl, :], in_=pb2[:, 0:128])
                    pb3, pr3 = bank()
                    tr(pb3[:, 0:64], B2r[rows, g2l].rearrange("p a b -> p (a b)"), identf[rows, par * 64:par * 64 + 64], [RA, Rc], [pr3])
                    tr(pb3[:, 64:128], B2i[rows, g2l].rearrange("p a b -> p (a b)"), identf[rows, par * 64:par * 64 + 64], [RA, Rc], [pr3])
                    op("act", "copy", [pr3], [Rout], out=B2sb[:, gl, :], in_=pb3[:, 0:128])
                cols = slice(hf * 4096, hf * 4096 + 4096)
                dma("sp", scr[("T", l)][:, cols], Tsb.rearrange("p a b -> p (a b)"), [Rout], [Rscr[("T", l)]])
                dma("sp", scr[("B", l)][:, cols], Bsb.rearrange("p a b -> p (a b)"), [Rout], [Rscr[("B", l)]])
                dma("sp", scr[("B2", l)][:, cols], B2sb.rearrange("p a b -> p (a b)"), [Rout], [Rscr[("B2", l)]])
                dma("sp", scr[("C", l)][:, cols], Csb.rearrange("p a b c -> p (a b c)"), [Rout], [Rscr[("C", l)]])

        for l in s5layers:
            s5_prologue(l)
        if KSTOP == "pro": S.frozen = True

        u_tm = view_at(work0, [8, 1024], BF16); Rutm = [Res("u_tm%d" % i) for i in range(8)]
        u_tmu = view_at(work0, [64, 8, 16], BF16)
        Sst = view_at(work0 + 16384, [32, 2, 131], F32); RSh = [Res("Sst0"), Res("Sst1")]
        Uv = view_at(off_y3 + 16640, [64, 128], BF16); RU = [Res("U%d" % i) for i in range(16)]
        yfm = view_at(off_y3 + 16640, [8, TMAX], BF16)
        Sb = view_at(off_stage, [16, 2, 128], BF16); RSb = Res("Sb")
        s5work = Rutm + RSh
        UA, _ = alloc([64, 2], BF16); RUA = Res("UA")
        yfmA, _ = alloc([8, 16], BF16); RyA = Res("yfmA")
        SbA, _ = alloc([32, 2, 2], BF16); RSbA = Res("SbA")
        scan_t1, _ = alloc([32, 2], F32); scan_t2, _ = alloc([32, 2], F32)
        scan_t1b, _ = alloc([32, 2], F32); scan_t2b, _ = alloc([32, 2], F32)
        Rscan1 = [Res("scan1_0"), Res("scan1_1")]; Rscan2 = [Res("scan2_0"), Res("scan2_1")]
        sig = []; Rsig = []
        for i in range(2):
            v, _ = alloc([512], F32); sig.append(v); Rsig.append(Res("sig%d" % i))
        carry, _ = alloc([2, 32, 2], F32); Rcarry = [Res("carry0"), Res("carry1")]
        assert mem_top[0] <= ARENA_BYTES, mem_top[0]
        o = work0
        tcg = [view_at(o + i * 2048, [512], F32) for i in range(2)]; o += 4096
        hcb = [view_at(o + i * 4192, [1048], F32) for i in range(2)]; o += 8384
        c1b = [view_at(o + i * 4160, [TMAX], F32) for i in range(2)]; o += 8320
        szb = [view_at(o + i * 2048, [512], F32) for i in range(2)]; o += 4096
        yvb = [view_at(o + i * 2048, [512], F32) for i in range(2)]; o += 4096
        Rtcg = [Res("tcg%d" % i) for i in range(2)]; Rhc = [Res("hc%d" % i) for i in range(2)]
        Rc1 = [Res("c1%d" % i) for i in range(2)]; Rsz = [Res("sz%d" % i) for i in range(2)]; Ryv = [Res("yv%d" % i) for i in range(2)]
        Rhch = [Res("hch%d" % i) for i in range(2)]
        convwork = Rtcg + Rhc + Rhch + Rc1 + Rsz + Ryv
        o = work0
        ubb = [view_at(o + i * 4224, [1056], F32) for i in range(2)]; o += 8448
        lvb = [view_at(o + i * 4224, [1056], F32) for i in range(4)]; o += 16896
        mixb = [view_at(o + i * 8320, [4, TMAX], BF16) for i in range(2)]; o += 16640
        yvp = [view_at(o + i * 2048, [512], F32) for i in range(2)]; o += 4096
        tfix = view_at(o, [16], F32); o += 64
        assert o <= work0 + WORK_BYTES
        Rub = [Res("ub%d" % i) for i in range(2)]; Rlv = [Res("lv%d" % i) for i in range(4)]
        Rmix = [Res("mix%d" % i) for i in range(2)]; Ryvp = [Res("yvp%d" % i) for i in range(2)]; Rtfix = Res("tfix")
        poolwork = Rub + Rlv + Rmix + Ryvp + [Rtfix]
        cur_work = [None]
        global _DBG_OFFS
        _DBG_OFFS = dict(off_hn=off_hn, off_y3=off_y3, work0=work0, off_stage=off_stage)
        S.realias(list(PRO_RES.values()), Rh_all + [Rhn] + Ry3 + Rring + s5work + convwork + poolwork)

        def use_work(kind):
            new = {"s5": s5work, "conv": convwork, "pool": poolwork}[kind]
            if cur_work[0] is not None and cur_work[0] is not new:
                S.realias(cur_work[0], new)
            cur_work[0] = new

        def load_w(eng, src_ap, shape, reads=()):
            v, r = wslot(shape)
            dma(eng, v, src_ap, list(reads), [r])
            return v, r

        def load_w_parts(eng, shape, partfn, nparts, reads=()):
            v, r = wslot(shape)
            prev = None
            for i in range(nparts):
                d_, s_ = partfn(v, i)
                prev = dma(eng, d_, s_, list(reads), [r], part_of=prev)
            return v, r

        def rmsnorm_stats(t0, tn, k):
            pb_, pr_ = bank()
            for b in range(8):
                j = b % 2
                op("act", "activation", [rh(b, t0)], [Rsq[j]], out=sq[j][:, 0:tn], in_=h[:, b, t0:t0 + tn], func=AF.Square)
                mm(pb_[:, 0:tn], onesb, sq[j][:, 0:tn], b == 0, b == 7, [Rsq[j], Rc], [pr_])
            op("act", "activation", [pr_], [Rrs[k]], out=rsb[k][:, 0:tn], in_=pb_[:, 0:tn], func=AF.Sqrt, bias=EPS, scale=1.0 / DM)
            op("dve", "reciprocal", [Rrs[k]], [Rrs[k]], out=rsb[k][:, 0:tn], in_=rsb[k][:, 0:tn])

        nrm_k = [0]

        def norm_to_hn(ci, gi):
            for (t0, tn) in chunk_tiles(ci):
                k = nrm_k[0]; nrm_k[0] ^= 1
                rmsnorm_stats(t0, tn, k)
                for b in range(8):
                    op("dve", "scalar_tensor_tensor", [rh(b, t0), Rrs[k], Rc], [Rhn], out=hn[:, b, t0:t0 + tn], in0=h[:, b, t0:t0 + tn],
                       scalar=gains[:, gi, b:b + 1], in1=rsb[k][:, 0:tn], op0=ALU.mult, op1=ALU.mult)

        def out_proj(ci, wname, nk):
            colw = 4096 // nk
            nslots = DM // colw
            slots = []
            for s_ in range(nslots):
                slots.append(load_w("pool", D[wname][:, s_ * colw:(s_ + 1) * colw].rearrange("(k p) c -> p k c", p=128), [nk, colw]))
            for (t0, tn) in chunk_tiles(ci):
                for s_ in range(nslots):
                    wv, wr = slots[s_]
                    for db in range(colw // 128):
                        dblk = s_ * (colw // 128) + db
                        pb_, pr_ = bank()
                        for k in range(nk):
                            mm(pb_[:, 0:tn], wv[:, k, db * 128:(db + 1) * 128], y3[:, k, t0:t0 + tn], k == 0, k == nk - 1, [wr, Ry3[k]], [pr_])
                        op("dve", "tensor_tensor", [pr_, rh(dblk, t0)], [rh(dblk, t0)], out=h[:, dblk, t0:t0 + tn], in0=pb_[:, 0:tn], in1=h[:, dblk, t0:t0 + tn], op=ALU.add)

        stg_i = [0]

        def load_chunk(ci):
            S.phase = "load"
            c0, T = CHUNKS[ci]
            tiles = []
            if ci == 0:
                tiles.append(("meta", 0, 16, 0))
                for j in range(8): tiles.append(("x", j * 128, 128, 16 + j * 128))
            else:
                for j in range(8): tiles.append(("x", c0 - NMETA + j * 128, 128, j * 128))
            for ti_, (kind, r0, nr, col) in enumerate(tiles):
                if KSTOP == "load%d" % ti_: S.frozen = True
                si = stg_i[0]; stg_i[0] ^= 1
                src = D["meta_tokens"] if kind == "meta" else D["x"][r0:r0 + nr, :]
                dma("sp", stage[si][0:nr, :], src, [], [Rstage[si]])
                for half in range(2):
                    pb_, pr_ = bank()
                    for q in range(4):
                        b = half * 4 + q
                        tr(pb_[:, q * 128:q * 128 + nr], stage[si][0:nr, b * 128:(b + 1) * 128], identf[0:nr, 0:nr], [Rstage[si], Rc], [pr_])
                    for q in range(4):
                        b = half * 4 + q
                        op("act" if half == 0 else "dve", "copy" if half == 0 else "tensor_copy", [pr_], [rh(b, col)],
                           out=h[:, b, col:col + nr], in_=pb_[:, q * 128:q * 128 + nr])

        def final_store(ci):
            S.phase = "final"
            c0, T = CHUNKS[ci]
            hf_, _o = None, None
            for (t0, tn) in chunk_tiles(ci):
                if ci == 0 and t0 == 0: continue
                k = nrm_k[0]; nrm_k[0] ^= 1
                rmsnorm_stats(t0, tn, k)
                if KSTOP == "stats": S.frozen = True
                hf = view_at(off_y3, [8, 512], F32)
                for b in range(8):
                    op("dve", "scalar_tensor_tensor", [rh(b, t0), Rrs[k], Rc], Ry3[0:8], out=hf[:, b, 0:tn], in0=h[:, b, t0:t0 + tn],
                       scalar=gains[:, 4, b:b + 1], in1=rsb[k][:, 0:tn], op0=ALU.mult, op1=ALU.mult)
                for sub in range(tn // 128):
                    si = stg_i[0]; stg_i[0] ^= 1
                    for half in range(2):
                        pb_, pr_ = bank()
                        for q in range(4):
                            b = half * 4 + q
                            tr(pb_[:, q * 128:(q + 1) * 128], hf[:, b, sub * 128:(sub + 1) * 128], identf, Ry3[0:8] + [Rc], [pr_])
                        op("act" if si == 0 else "dve", "copy" if si == 0 else "tensor_copy", [pr_], [Rstage[si]],
                           out=stage[si][:, half * 512:(half + 1) * 512], in_=pb_[:, 0:512])
                    row0 = c0 + t0 + sub * 128 - NMETA
                    dma("sp", out_d[row0:row0 + 128, :], stage[si], [Rstage[si]], [Res("outd")], semres=Rstage[si])

        def layer_conv(ci):
            c0, T = CHUNKS[ci]
            use_work("conv")
            norm_to_hn(ci, 1)
            for e in range(16):
                srcw = D["l1_w_in"].rearrange("(k p) (q c) -> p k q c", p=128, q=4)
                wv, wr = load_w_parts("pool", [8, 4, 128], lambda v, i, e=e, srcw=srcw: (v[:, :, i, :], srcw[:, :, i, e * 128:(e + 1) * 128]), 4)
                j = e % 2
                hc = hcb[j]; c1 = c1b[j]
                op("act", "copy", [Rchist], [Rhch[j]], out=hc[:, 0:2], in_=chist[:, e, :])
                for (t0, tn) in chunk_tiles(ci):
                    banks = []
                    for part in (1, 2, 0, 3):
                        pb_, pr_ = bank()
                        for b in range(8):
                            mm(pb_[:, 0:tn], wv[:, b, part, :], hn[:, b, t0:t0 + tn], b == 0, b == 7, [wr, Rhn], [pr_])
                        banks.append((pb_, pr_))
                    (pcg, rcg), (pv, rv), (pbg_, rbg), (pz, rz) = banks
                    op("act", "copy", [rcg], [Rtcg[j]], out=tcg[j][:, 0:tn], in_=pcg[:, 0:tn])
                    op("dve", "tensor_tensor", [Rtcg[j], rv], [Rhc[j]], out=hc[:, 2 + t0:2 + t0 + tn], in0=pv[:, 0:tn], in1=tcg[j][:, 0:tn], op=ALU.mult)
                    op("act", "activation", [rz], [Rsz[j]], out=szb[j][:, 0:tn], in_=pz[:, 0:tn], func=AF.Silu)
                    op("act", "activation", [Rhc[j], Rc], [Rc1[j]], out=c1[:, t0:t0 + tn], in_=hc[:, 2 + t0:2 + t0 + tn], func=AF.Identity,
                       bias=cb[:, e:e + 1], scale=cw[:, 2, e:e + 1])
                    op("dve", "scalar_tensor_tensor", [Rhc[j], Rhch[j], Rc1[j], Rc], [Rc1[j]], out=c1[:, t0:t0 + tn], in0=hc[:, 1 + t0:1 + t0 + tn],
                       scalar=cw[:, 1, e:e + 1], in1=c1[:, t0:t0 + tn], op0=ALU.mult, op1=ALU.add)
                    op("dve", "scalar_tensor_tensor", [Rhc[j], Rhch[j], Rc1[j], Rc], [Rc1[j]], out=c1[:, t0:t0 + tn], in0=hc[:, t0:t0 + tn],
                       scalar=cw[:, 0, e:e + 1], in1=c1[:, t0:t0 + tn], op0=ALU.mult, op1=ALU.add)
                    op("dve", "tensor_tensor", [Rc1[j], rbg], [Ryv[j]], out=yvb[j][:, 0:tn], in0=pbg_[:, 0:tn], in1=c1[:, t0:t0 + tn], op=ALU.mult)
                    op("dve", "tensor_tensor", [Ryv[j], Rsz[j]], [Ry3[e]], out=y3[:, e, t0:t0 + tn], in0=yvb[j][:, 0:tn], in1=szb[j][:, 0:tn], op=ALU.mult)
                op("act", "copy", [Rhc[j]], [Rchist], out=chist[:, e, :], in_=hc[:, T:T + 2])
            out_proj(ci, "l1_w_out", 16)

        def layer_pool(ci):
            c0, T = CHUNKS[ci]
            use_work("pool")
            norm_to_hn(ci, 2)
            grp_calls = []
            def GRP(k):
                mix = mixb[k % 2]; rmix = Rmix[k % 2]
                wv, wr = load_w("pool", D["l2_w_grp"][k].rearrange("(kk p) c -> p kk c", p=128), [4, 512])
                for eo in range(4):
                    e = 4 * k + eo
                    for (t0, tn) in chunk_tiles(ci):
                        pb_, pr_ = bank()
                        for ei in range(4):
                            mm(pb_[:, 0:tn], wv[:, ei, eo * 128:(eo + 1) * 128], mix[:, ei, t0:t0 + tn], ei == 0, ei == 3, [wr, rmix], [pr_])
                        jj = eo % 2
                        op("act", "activation", [pr_, Rc, Rpbs], [Ryvp[jj]], out=yvp[jj][:, 0:tn], in_=pb_[:, 0:tn], func=AF.Identity,
                           bias=pbs[:, e:e + 1], scale=pscale[:, e:e + 1])
                        op("dve", "tensor_tensor", [Ryvp[jj], Ry3[e]], [Ry3[e]], out=y3[:, e, t0:t0 + tn], in0=yvp[jj][:, 0:tn], in1=y3[:, e, t0:t0 + tn], op=ALU.mult)

            for k in range(4):
                w = 2 << k
                mix = mixb[k % 2]; rmix = Rmix[k % 2]
                for epair in range(2):
                    e0 = 4 * k + 2 * epair
                    srcw = D["l2_w_in"].rearrange("(kk p) (q c) -> p kk q c", p=128, q=2)
                    wv, wr = load_w_parts("pool", [8, 2, 256], lambda v, i, e0=e0, srcw=srcw: (v[:, :, i, :], srcw[:, :, i, e0 * 128:(e0 + 2) * 128]), 2)
                    for el in range(2):
                        e = e0 + el; ei = 2 * epair + el
                        j = e % 2
                        ub = ubb[j]
                        op("act", "copy", [Rphist], [Rub[j]], out=ub[:, 0:16], in_=phist[:, e, :])
                        for (t0, tn) in chunk_tiles(ci):
                            pb_, pr_ = bank()
                            for b in range(8):
                                mm(pb_[:, 0:tn], wv[:, b, 0, el * 128:(el + 1) * 128], hn[:, b, t0:t0 + tn], b == 0, b == 7, [wr, Rhn], [pr_])
                            op("act", "copy", [pr_], [Rub[j]], out=ub[:, 16 + t0:16 + t0 + tn], in_=pb_[:, 0:tn])
                            pb2, pr2 = bank()
                            for b in range(8):
                                mm(pb2[:, 0:tn], wv[:, b, 1, el * 128:(el + 1) * 128], hn[:, b, t0:t0 + tn], b == 0, b == 7, [wr, Rhn], [pr2])
                            op("act", "activation", [pr2], [Ry3[e]], out=y3[:, e, t0:t0 + tn], in_=pb2[:, 0:tn], func=AF.Silu)
                        op("act", "copy", [Rub[j]], [Rphist], out=phist[:, e, :], in_=ub[:, T:T + 16])
                        cur = ub; rcur = Rub[j]; lo = 0
                        NE = 16 + T
                        for lev in range(k + 1):
                            sft = 1 << lev
                            nxt = lvb[lev % 2 + 2 * j]
                            rn = Rlv[lev % 2 + 2 * j]
                            nlo = lo + sft
                            op("dve", "tensor_tensor", [rcur], [rn], out=nxt[:, nlo:NE], in0=cur[:, nlo:NE], in1=cur[:, nlo - sft:NE - sft], op=ALU.add)
                            cur = nxt; rcur = rn; lo = nlo
                        op("dve", "scalar_tensor_tensor", [rcur, Rub[j]], [rmix], out=mix[:, ei, 0:T], in0=cur[:, 16:16 + T], scalar=1.0 / w,
                           in1=ub[:, 16:16 + T], op0=ALU.mult, op1=ALU.subtract)
                        if ci == 0:
                            op("dve", "tensor_tensor", [rcur, Rc], [Rtfix], out=tfix[:, 0:w - 1], in0=cur[:, 16:16 + w - 1], in1=invc[:, 0:w - 1], op=ALU.mult)
                            op("dve", "tensor_tensor", [Rtfix, Rub[j]], [rmix], out=mix[:, ei, 0:w - 1], in0=tfix[:, 0:w - 1], in1=ub[:, 16:16 + w - 1], op=ALU.subtract)
                if k >= 1:
                    grp_calls.append(k - 1); GRP(k - 1)
            GRP(3)
            out_proj(ci, "l2_w_out", 16)

        def layer_s5_full(ci, l):
            c0, T = CHUNKS[ci]
            li = 0 if l == 0 else 1
            pre = "l%d_" % l
            S.phase = "s5norm"
            norm_to_hn(ci, l)
            pieces = chunk_pieces(ci)
            def zgate():
                S.phase = "s5zgate"
                for cs_ in range(2):
                    wv, wr = load_w("pool", D[pre + "w_in"][:, 1024 + cs_ * 512:1024 + (cs_ + 1) * 512].rearrange("(k p) c -> p k c", p=128), [8, 512])
                    for cbk in range(4):
                        cblk = cs_ * 4 + cbk
                        for (t0, tn) in chunk_tiles(ci):
                            pb_, pr_ = bank()
                            for b in range(8):
                                mm(pb_[:, 0:tn], wv[:, b, cbk * 128:(cbk + 1) * 128], hn[:, b, t0:t0 + tn], b == 0, b == 7, [wr, Rhn], [pr_])
                            op("act", "activation", [pr_], [Ry3[cblk]], out=y3[:, cblk, t0:t0 + tn], in_=pb_[:, 0:tn], func=AF.Silu)
            if KSTOP == "s5gate": S.frozen = True
            ycols = []; ncol = 0
            for (n0_, nn_) in pieces:
                ycols.append(ncol); ncol += nn_

            def piece_stage(pi_, n0, nn, stage):
                ycol = ycols[pi_]
                last_piece = (pi_ == len(pieces) - 1)
                small = nn < 128
                Uc = UA if small else Uv
                RUc = [RUA] * 16 if small else RU
                Sbc = SbA if small else Sb
                RSbc = RSbA if small else RSb
                if stage == 1:
                    S.phase = "s5u"
                    for half in range(2):
                        wv, wr = load_w("pool", D[pre + "w_in"][:, half * 512:(half + 1) * 512].rearrange("(k p) c -> p k c", p=128), [8, 512])
                        for tau in range(8):
                            pb_, pr_ = bank()
                            for b in range(8):
                                lhs = hn[:, b, 8 * n0:8 * (n0 + nn)].rearrange("p (n t) -> p n t", t=8)[:, :, tau]
                                mm(pb_[0:nn, :], lhs, wv[:, b, :], b == 0, b == 7, [wr, Rhn], [pr_])
                            op("act" if tau % 2 == 0 else "dve", "copy" if tau % 2 == 0 else "tensor_copy", [pr_], [Rutm[tau]],
                               out=u_tmu[0:nn, half * 32:(half + 1) * 32, tau, :], in_=pb_[0:nn, :].rearrange("p (g i) -> p g i", i=16))
                    if not small:
                        S.realias(Ry3[8:16], RU)
                    S.phase = "s5Utr"
                    for gq in range(16):
                        pb_, pr_ = bank()
                        pbb = pb_[:].bitcast(BF16)
                        for gl in range(4):
                            g = 4 * gq + gl
                            tr(pbb[:, gl * 128:gl * 128 + nn], u_tmu[0:nn, g].rearrange("p t i -> p (t i)"), identb[0:nn, 0:nn], Rutm + [Rc], [pr_])
                        op("dve" if gq % 2 == 0 else "act", "tensor_copy" if gq % 2 == 0 else "copy", [pr_], [RUc[gq]],
                           out=Uc[:, 4 * gq:4 * gq + 4, 0:nn], in_=pbb[:, 0:512].rearrange("p (g n) -> p g n", n=128)[:, :, 0:nn])
                    if KSTOP == "s5U" and last_piece: S.frozen = True
                    S.phase = "s5z"
                    for hf in range(2):
                        wv, wr = load_w("sp", scr[("B", l)][:, hf * 4096:(hf + 1) * 4096].rearrange("p (a b) -> p a b", b=128), [32, 128], reads=[Rscr[("B", l)]])
                        if nn > 1:
                            wv2, wr2 = load_w("sp", scr[("B2", l)][:, hf * 4096:(hf + 1) * 4096].rearrange("p (a b) -> p a b", b=128), [32, 128], reads=[Rscr[("B2", l)]])
                        for g2p in range(8):
                            pb_, pr_ = bank()
                            for a in range(2):
                                g2 = hf * 16 + g2p * 2 + a
                                for par in range(2):
                                    g = 2 * g2 + par; gl = g - hf * 32
                                    rows = slice(par * 64, par * 64 + 64)
                                    for ri in range(2):
                                        c_ = (a * 2 + ri) * 128
                                        mm(pb_[rows, c_:c_ + nn], wv[:, gl, ri * 64:(ri + 1) * 64], Uc[:, g, 0:nn], True, nn == 1, [wr, RUc[g // 4]], [pr_])
                                        if nn > 1:
                                            mm(pb_[rows, c_ + 1:c_ + nn], wv2[:, gl, ri * 64:(ri + 1) * 64], Uc[:, g, 0:nn - 1], False, True, [wr2, RUc[g // 4]], [pr_])
                            g2a = hf * 16 + g2p * 2
                            op("act", "copy", [pr_], RSh, out=Sst[:, g2a:g2a + 2, :, ycol + 1:ycol + 1 + nn],
                               in_=pb_[:, 0:512].rearrange("p (a r n) -> p a r n", a=2, r=2)[:, :, :, 0:nn])
                    if KSTOP == "s5z" and last_piece: S.frozen = True
                if stage == 2:
                    S.phase = "s5scan"
                    def chain(c, pcol, qcol, Trr, Tis, rd, wr_):
                        t1c = scan_t1 if c == 0 else scan_t1b; t2c = scan_t2 if c == 0 else scan_t2b
                        r1 = Rscan1[c]; r2 = Rscan2[c]
                        return [
                            lambda: op("dve", "tensor_tensor", rd + [Rtab[li]], [r2], out=t2c[:, :, 0], in0=Sst[:, :, 1, pcol], in1=Tis[:, li, :, 0], op=ALU.mult),
                            lambda: op("dve", "tensor_tensor", rd + [Rtab[li]], [r2], out=t2c[:, :, 1], in0=Sst[:, :, 0, pcol], in1=Tis[:, li, :, 1], op=ALU.mult),
                            lambda: op("dve", "tensor_tensor", rd + [Rtab[li]], [r1], out=t1c, in0=Sst[:, :, :, pcol], in1=Trr[:, li], op=ALU.mult),
                            lambda: op("dve", "tensor_tensor", [r2, wr_], [wr_], out=Sst[:, :, :, qcol], in0=Sst[:, :, :, qcol], in1=t2c, op=ALU.add),
                            lambda: op("dve", "tensor_tensor", [r1, wr_], [wr_], out=Sst[:, :, :, qcol], in0=Sst[:, :, :, qcol], in1=t1c, op=ALU.add),
                        ]
                    for f_ in chain(0, ycol, ycol + 1, ArAr, AiS, RSh, RSh[0]): f_()
                    n = 1
                    while n < nn:
                        if n + 1 < nn:
                            rdO = RSh if n == 1 else [RSh[1]]
                            rdE = RSh if n == 1 else [RSh[0]]
                            cO = chain(1, ycol + n - 1, ycol + n + 1, A2rr, A2is, rdO, RSh[1])
                            cE = chain(0, ycol + n, ycol + n + 2, A2rr, A2is, rdE, RSh[0])
                            for fo, fe in zip(cO, cE):
                                fo(); fe()
                            n += 2
                        else:
                            for f_ in chain(1, ycol + n - 1, ycol + n + 1, A2rr, A2is, RSh, RSh[1]): f_()
                            n += 1
                    if KSTOP == "s5scan" and last_piece: S.frozen = True
                if stage == 3:
                    S.phase = "s5Y"
                    if not small:
                        S.realias(Rstage, [RSb])
                    if small:
                        op("act", "copy", RSh, [RSbc], out=Sbc[:, :, :, 0:nn], in_=Sst[:, :, :, ycol:ycol + nn])
                    for hf in range(2):
                        if not small:
                            op("act", "copy", RSh, [RSbc], out=Sbc[:, :, :, 0:nn], in_=Sst[:, hf * 16:hf * 16 + 16, :, ycol:ycol + nn])
                        sbo = 0 if small else hf * 16
                        wT, rT = load_w("sp", scr[("T", l)][:, hf * 4096:(hf + 1) * 4096].rearrange("p (a b) -> p a b", b=128), [32, 128], reads=[Rscr[("T", l)]])
                        wC, rC = load_w_parts("sp", [16, 2, 128], lambda v, i, hf=hf: (v.rearrange("p a r b -> p (a r b)"), scr[("C", l)][:, hf * 4096:(hf + 1) * 4096]), 1, reads=[Rscr[("C", l)]])
                        for gq in range(8):
                            pb_, pr_ = bank()
                            for gl4 in range(4):
                                gl = gq * 4 + gl4; g = hf * 32 + gl; g2 = g // 2; par = g % 2
                                rows = slice(par * 64, par * 64 + 64)
                                o_ = pb_[0:nn, gl4 * 128:(gl4 + 1) * 128]
                                mm(o_, Uc[:, g, 0:nn], wT[:, gl, :], True, False, [RUc[g // 4], rT], [pr_])
                                mm(o_, Sbc[rows, g2 - sbo, 0, 0:nn], wC[rows, g2 - hf * 16, 0, :], False, False, [RSbc, rC], [pr_])
                                mm(o_, Sbc[rows, g2 - sbo, 1, 0:nn], wC[rows, g2 - hf * 16, 1, :], False, True, [RSbc, rC], [pr_])
                            gq_abs = hf * 8 + gq
                            op("act", "activation", [pr_], Rutm, out=u_tm[0:nn, :, 64 * gq_abs:64 * gq_abs + 64].rearrange("p j (g o) -> p g j o", g=4),
                               in_=pb_[0:nn, 0:512].rearrange("p (g j o) -> p g j o", g=4, j=8), func=AF.Gelu_apprx_tanh)
                    if not small:
                        S.realias([RSb], Rstage)
                    if KSTOP == "s5Y" and last_piece: S.frozen = True
                    S.phase = "s5ytr"
                    for cbk in range(8):
                        pb_, pr_ = bank()
                        pbb = pb_[:].bitcast(BF16)
                        for j in range(8):
                            tr(pbb[:, j * 128:j * 128 + nn], u_tm[0:nn, j, cbk * 128:(cbk + 1) * 128], identb[0:nn, 0:nn], [Rutm[j], Rc], [pr_])
                        ydst = yfmA if small else yfm
                        op("dve" if cbk % 2 == 0 else "act", "tensor_copy" if cbk % 2 == 0 else "copy", [pr_], [RyA if small else Ry3[8 + cbk]],
                           out=ydst[:, cbk, 8 * n0:8 * (n0 + nn)].rearrange("p (n j) -> p j n", j=8),
                           in_=pbb[:, 0:1024].rearrange("p (j n) -> p j n", n=128)[:, :, 0:nn])
                    if not small:
                        S.realias(RU, Ry3[8:16])

            for pi_, (n0, nn) in enumerate(pieces): piece_stage(pi_, n0, nn, 1)
            zgate()
            for pi_, (n0, nn) in enumerate(pieces): piece_stage(pi_, n0, nn, 2)
            for pi_, (n0, nn) in enumerate(pieces): piece_stage(pi_, n0, nn, 3)

            if KSTOP == "s5yfm": S.frozen = True
            op("dve", "tensor_copy", RSh, [Rcarry[li]], out=carry[:, li], in_=Sst[:, :, :, ncol])
            if len(pieces) > 1:
                op("dve", "tensor_copy", [RyA], Ry3[8:16], out=yfm[:, :, 0:16], in_=yfmA[:, :, 0:16])
            S.phase = "s5glu"
            for cs_ in range(2):
                wv, wr = load_w("pool", D[pre + "w_glu"][:, cs_ * 512:(cs_ + 1) * 512].rearrange("(k p) c -> p k c", p=128), [8, 512])
                for cbk in range(4):
                    cblk = cs_ * 4 + cbk
                    for (t0, tn) in chunk_tiles(ci):
                        pb_, pr_ = bank()
                        for k in range(8):
                            mm(pb_[:, 0:tn], wv[:, k, cbk * 128:(cbk + 1) * 128], yfm[:, k, t0:t0 + tn], k == 0, k == 7, [wr, Ry3[8 + k]], [pr_])
                        jj = cblk % 2
                        op("act", "activation", [pr_, Rc], [Rsig[jj]], out=sig[jj][:, 0:tn], in_=pb_[:, 0:tn], func=AF.Sigmoid, bias=bglu[:, li, cblk:cblk + 1])
                        op("dve", "tensor_tensor", [Rsig[jj], Ry3[8 + cblk]], [Rsig[jj]], out=sig[jj][:, 0:tn], in0=sig[jj][:, 0:tn], in1=yfm[:, cblk, t0:t0 + tn], op=ALU.mult)
                        op("dve", "tensor_tensor", [Rsig[jj], Ry3[cblk]], [Ry3[cblk]], out=y3[:, cblk, t0:t0 + tn], in0=sig[jj][:, 0:tn], in1=y3[:, cblk, t0:t0 + tn], op=ALU.mult)
            S.phase = "s5out"
            out_proj(ci, pre + "w_out", 8)

        first_s5 = {0: True, 3: True}
        for ci in range(nchunks):
            cur_ci[0] = ci
            if ci == 1:
                S.realias(Rh_all, Rh_all)
            load_chunk(ci)
            if KSTOP == "load": S.frozen = True
            for l in layers:
                if l in (0, 3):
                    li = 0 if l == 0 else 1
                    use_work("s5")
                    if first_s5[l]:
                        op("dve", "memset", [], RSh, Sst[:, :, :, 0], 0.0)
                        first_s5[l] = False
                    else:
                        op("dve", "tensor_copy", [Rcarry[li]], RSh, out=Sst[:, :, :, 0], in_=carry[:, li])
                    layer_s5_full(ci, l)
                elif l == 1:
                    layer_conv(ci)
                else:
                    layer_pool(ci)
            final_store(ci)
        if _os.environ.get("KDUMP", ""):
            S.frozen = False
            regs = [(off_hn, 16640), (off_y3, 33280), (work0, WORK_BYTES), (off_stage, 8192)]
            if _os.environ["KDUMP"] != "1":
                regs = [tuple(int(v_) for v_ in t_.split(":")) for t_ in _os.environ["KDUMP"].split(",")]
            allres = Rh_all + [Rhn] + Ry3 + Rring + s5work + convwork + poolwork + Rstage + RU + [RSb] + RSh + Rutm
            ov = out_d.rearrange("(p r) c -> p (r c)", p=128)
            pos = 0
            Rdump = Res("dump")
            for (o_, n_) in regs:
                dma("sp", ov[:, pos // 4:(pos + n_) // 4], arena[:, o_ // 4:(o_ + n_) // 4], allres, [Rdump], semres=Res("dumpsem%d" % pos))
                pos += n_
        S.emit()
    return nc


_CACHE = {}


def kernel(**inputs):
    x = np.ascontiguousarray(inputs["x"], dtype=np.float32)
    B = x.shape[0]
    if "nc" not in _CACHE:
        _CACHE["nc"] = build_program()
    nc = _CACHE["nc"]
    consts = host_consts()
    shared = {n: np.ascontiguousarray(inputs[n], dtype=np.float32) for n in PARAM_NAMES}
    shared.update(consts)
    in_maps = []
    for b in range(B):
        m = dict(shared)
        m["x"] = x[b]
        in_maps.append(m)
    res = run_bass_kernel_spmd(nc, in_maps, core_ids=list(range(B)))
    out = np.stack([np.asarray(r["out"], dtype=np.float32) for r in res.results], axis=0)
    return out
```

```python
import math
from contextlib import ExitStack
import numpy as np
import concourse.bass as bass
import concourse.mybir as mybir
from concourse.bass_utils import run_bass_kernel_spmd

F32 = mybir.dt.float32
BF16 = mybir.dt.bfloat16
I32 = mybir.dt.int32
AF = mybir.ActivationFunctionType
ALU = mybir.AluOpType
P = 128
NMETA = 16
SEQ = 4096
DM = 1024
EPS = 1e-6
PI = math.pi


class Res:
    __slots__ = ("name", "w", "rs", "sem", "ndma", "grp", "multi", "excl")

    def __init__(self, name, grp=None, excl=False):
        self.name = name; self.w = None; self.rs = {}; self.sem = None; self.ndma = 0; self.grp = grp; self.multi = None
        self.excl = excl


class SemGroup:
    def __init__(self, name):
        self.name = name; self.sem = None; self.total = 0


class Op:
    __slots__ = ("eng", "fn", "deps", "dma", "semres", "needs_inc", "ev", "phase")

    def __init__(self, eng, fn, dma):
        self.eng = eng; self.fn = fn; self.deps = []; self.dma = dma; self.semres = None
        self.needs_inc = False; self.ev = None


class Sched:
    ENG = ("pe", "act", "dve", "pool", "sp")

    def __init__(self, nc, stack):
        self.nc = nc; self.stack = stack
        self.ops = {e: [] for e in self.ENG}
        self.all_dma = []
        self.nsem = 0

    def new_sem(self, name):
        self.nsem += 1
        return self.stack.enter_context(self.nc.semaphore(name))

    frozen = False
    phase = ""

    def add(self, eng, fn, reads=(), writes=(), dma=False, semres=None, part_of=None):
        if self.frozen:
            return None
        op = Op(eng, fn, dma)
        op.phase = self.phase
        deps = []
        rr = []
        for r in reads:
            if r.multi is not None: rr.extend(r.multi)
            else: rr.append(r)
        reads = rr
        for r in reads:
            if r.w is not None: deps.append(r.w)
            if r.excl:
                for k_, v_ in r.rs.items():
                    if k_ != eng: deps.append(v_)
        for w in writes:
            if w.w is not None: deps.append(w.w)
            deps.extend(w.rs.values())
        seen = set(); out = []
        for d in deps:
            if id(d) in seen or d is op or d is part_of: continue
            seen.add(id(d))
            if not d.dma and not dma and d.eng == eng:
                if eng == "pe" or not self.ops[eng] or self.ops[eng][-1] is not d:
                    continue
                self.n_adj = getattr(self, "n_adj", 0) + 1
            out.append(d); d.needs_inc = True
        op.deps = out
        for r in reads: r.rs[eng if not dma else ("dma", id(op))] = op
        for w in writes: w.w = op; w.rs = {}
        if dma:
            op.semres = semres if semres is not None else writes[0]
            self.all_dma.append(op)
            op.needs_inc = True
        self.ops[eng].append(op)
        return op

    def realias(self, old, new):
        users = []
        for o in old:
            if o.w is not None: users.append(o.w)
            users.extend(list(o.rs.values()))
        for n in new:
            for v in users: n.rs[("r", id(v))] = v

    def emit(self):
        nc = self.nc
        MAXV = 30000
        nes = 0
        for op in self.all_dma:
            r = op.semres
            if r.grp is not None: r.grp.total += 1
        for e in self.ENG:
            cur = None; cnt = 0
            for op in self.ops[e]:
                if op.dma:
                    r = op.semres
                    if r.grp is not None:
                        g = r.grp
                        if g.sem is None: g.sem = self.new_sem("g_" + g.name)
                        op.ev = (g.sem, 16 * g.total)
                    else:
                        if r.sem is None: r.sem = {}
                        if e not in r.sem: r.sem[e] = [self.new_sem("d_%s_%s" % (r.name, e)), 0]
                        r.sem[e][1] += 1
                        op.ev = (r.sem[e][0], 16 * r.sem[e][1])
                elif op.needs_inc:
                    if cur is None or cnt >= MAXV:
                        cur = self.new_sem("e_%s_%d" % (e, nes)); nes += 1; cnt = 0
                    cnt += 1
                    op.ev = (cur, cnt)
        last = {}
        for op in self.all_dma: last[op.ev[0].name] = op.ev
        final_waits = list(last.values())
        engobj = {"pe": "tensor", "act": "scalar", "dve": "vector", "pool": "gpsimd", "sp": "sync"}
        sched = self
        with nc.Block() as block:
            def mk(ename):
                def body(eng):
                    known = {}
                    for op in sched.ops[ename]:
                        for d in op.deps:
                            sem, val = d.ev
                            if known.get(sem.name, 0) >= val: continue
                            known[sem.name] = val
                            eng.wait_ge(sem, val)
                        ins = op.fn(eng)
                        if _DBG_TAGS is not None:
                            try: _DBG_TAGS[ins.ins.name] = (ename, op.phase)
                            except Exception: pass
                        if op.ev is not None:
                            ins.then_inc(op.ev[0], 16 if op.dma else 1)
                    if ename == "sp":
                        for sem, val in final_waits:
                            eng.wait_ge(sem, val)
                return body
            for ename in self.ENG:
                getattr(block, engobj[ename])(mk(ename))


_DBG_TAGS = None
_DBG_OFFS = None
CHUNKS = [(0, 1040), (1040, 1024), (2064, 1024), (3088, 1024)]
TMAX = 1040
PARAM_NAMES = [
    "meta_tokens", "norm0_g", "l0_w_in", "l0_lam_re", "l0_lam_im", "l0_log_dt", "l0_b_re", "l0_b_im",
    "l0_c_re", "l0_c_im", "l0_d_skip", "l0_w_glu", "l0_b_glu", "l0_w_out",
    "norm1_g", "l1_w_in", "l1_conv_w", "l1_conv_b", "l1_w_out",
    "norm2_g", "l2_w_in", "l2_w_grp", "l2_b_grp", "l2_scale", "l2_w_out",
    "norm3_g", "l3_w_in", "l3_lam_re", "l3_lam_im", "l3_log_dt", "l3_b_re", "l3_b_im",
    "l3_c_re", "l3_c_im", "l3_d_skip", "l3_w_glu", "l3_b_glu", "l3_w_out", "final_g"]
PARAM_SHAPES = {
    "meta_tokens": [16, 1024], "norm0_g": [1024], "l0_w_in": [1024, 2048], "l0_lam_re": [64, 64], "l0_lam_im": [64, 64],
    "l0_log_dt": [64], "l0_b_re": [64, 64, 16], "l0_b_im": [64, 64, 16], "l0_c_re": [64, 16, 64], "l0_c_im": [64, 16, 64],
    "l0_d_skip": [1024], "l0_w_glu": [1024, 1024], "l0_b_glu": [1024], "l0_w_out": [1024, 1024],
    "norm1_g": [1024], "l1_w_in": [1024, 8192], "l1_conv_w": [3, 2048], "l1_conv_b": [2048], "l1_w_out": [2048, 1024],
    "norm2_g": [1024], "l2_w_in": [1024, 4096], "l2_w_grp": [4, 512, 512], "l2_b_grp": [4, 512], "l2_scale": [2048],
    "l2_w_out": [2048, 1024], "norm3_g": [1024], "l3_w_in": [1024, 2048], "l3_lam_re": [64, 64], "l3_lam_im": [64, 64],
    "l3_log_dt": [64], "l3_b_re": [64, 64, 16], "l3_b_im": [64, 64, 16], "l3_c_re": [64, 16, 64], "l3_c_im": [64, 16, 64],
    "l3_d_skip": [1024], "l3_w_glu": [1024, 1024], "l3_b_glu": [1024], "l3_w_out": [1024, 1024], "final_g": [1024]}


def host_consts():
    c = {}
    c["c_ident"] = np.eye(128, dtype=np.float32)
    idx = np.arange(128)
    c["c_mask"] = (idx[:, None] // 16 <= idx[None, :] // 16).astype(np.float32)
    selC = np.zeros((128, 2, 64), np.float32)
    for gl in range(8):
        for o in range(16):
            selC[gl * 16 + o, gl % 2, (gl // 2) * 16 + o] = 1.0
    c["c_selC"] = selC
    selG = np.zeros((64, 2, 32), np.float32)
    for g in range(64):
        selG[g, g % 2, g // 2] = 1.0
    c["c_selG"] = selG
    c["c_invc"] = np.tile((1.0 / np.arange(1, 17, dtype=np.float32))[None, :], (128, 1)).astype(np.float32)
    c["c_ones"] = np.ones((128, 128), np.float32)
    return c


CONST_SHAPES = {"c_ident": [128, 128], "c_mask": [128, 128], "c_selC": [128, 2, 64], "c_selG": [64, 2, 32],
                "c_invc": [128, 16], "c_ones": [128, 128]}


def chunk_tiles(ci):
    return [(0, 16), (16, 512), (528, 512)] if ci == 0 else [(0, 512), (512, 512)]


def chunk_pieces(ci):
    return [(0, 2), (2, 128)] if ci == 0 else [(0, 128)]


def build_program(layers=(0, 1, 2, 3), nchunks=4):
    nc = bass.Bass("TRN2", target_bir_lowering=False)
    D = {}
    D["x"] = nc.dram_tensor("x", [SEQ, DM], F32, kind="ExternalInput").ap()
    for n in PARAM_NAMES:
        D[n] = nc.dram_tensor(n, PARAM_SHAPES[n], F32, kind="ExternalInput").ap()
    for n, s in CONST_SHAPES.items():
        D[n] = nc.dram_tensor(n, s, F32, kind="ExternalInput").ap()
    out_d = nc.dram_tensor("out", [SEQ, DM], F32, kind="ExternalOutput").ap()
    scr = {}
    for l in [l_ for l_ in (0, 3) if l_ in layers]:
        scr[("T", l)] = nc.dram_tensor("scrT%d" % l, [128, 64 * 128], BF16, kind="Internal").ap()
        scr[("B", l)] = nc.dram_tensor("scrB%d" % l, [128, 64 * 128], BF16, kind="Internal").ap()
        scr[("C", l)] = nc.dram_tensor("scrC%d" % l, [128, 64 * 128], BF16, kind="Internal").ap()
        scr[("B2", l)] = nc.dram_tensor("scrB2%d" % l, [128, 64 * 128], BF16, kind="Internal").ap()

    with ExitStack() as st:
        S = Sched(nc, st)
        import os as _os
        ARENA_BYTES = int(_os.environ.get('KARENA', '207872'))
        arena = st.enter_context(nc.sbuf_tensor("arena", [128, ARENA_BYTES // 4], F32))
        mem_top = [0]

        def view_at(off, shape, dt):
            n = 1
            for s_ in shape: n *= s_
            nb = n * (2 if dt == BF16 else 4)
            assert off % 4 == 0 and nb % 4 == 0 and off + nb <= ARENA_BYTES, (off, nb)
            ap = arena[:, off // 4:(off + nb) // 4]
            if dt != F32: ap = ap.bitcast(dt)
            if len(shape) == 2:
                ap = ap.rearrange("p (a b) -> p a b", b=shape[1])
            elif len(shape) == 3:
                ap = ap.rearrange("p (a b c) -> p a b c", b=shape[1], c=shape[2])
            elif len(shape) == 4:
                ap = ap.rearrange("p (a b c d) -> p a b c d", b=shape[1], c=shape[2], d=shape[3])
            return ap

        def alloc(shape, dt):
            n = 1
            for s_ in shape: n *= s_
            nb = n * (2 if dt == BF16 else 4)
            nb = (nb + 31) // 32 * 32
            off = mem_top[0]; mem_top[0] += nb
            return view_at(off, shape, dt), off

        kint_t = st.enter_context(nc.sbuf_tensor("kint_t", [128, 32], I32))
        psum = [st.enter_context(nc.psum_tensor("ps%d" % i, [128, 512], F32)) for i in range(8)]
        psres = [Res("ps%d" % i, excl=True) for i in range(8)]
        pidx = [0]

        def bank():
            i = pidx[0]; pidx[0] = (i + 1) % 8
            return psum[i], psres[i]

        def op(eng, method, reads, writes, *args, **kw):
            return S.add(eng, lambda e: getattr(e, method)(*args, **kw), reads, writes)

        import os
        SKIP = os.environ.get("KSKIP", "")

        def dma(eng, out, in_, reads, writes, semres=None, slow=False, part_of=None):
            if slow and SKIP == "slow":
                return None
            if len(writes) == 1 and writes[0].multi is not None:
                nr_ = Res("c%d" % len(writes[0].multi)); writes[0].multi.append(nr_); writes = [nr_]
            if slow:
                return S.add(eng, lambda e: e.dma_start(out=out, in_=in_, allow_slow_non_contiguous=True), reads, writes, dma=True, semres=semres, part_of=part_of)
            return S.add(eng, lambda e: e.dma_start(out=out, in_=in_), reads, writes, dma=True, semres=semres, part_of=part_of)

        def mm(out, lhsT, rhs, start, stop, reads, writes):
            return S.add("pe", lambda e: e.matmul(out, lhsT, rhs, start=start, stop=stop), reads, writes)

        def tr(out, in_, ident, reads, writes):
            return S.add("pe", lambda e: e.transpose(out, in_, ident), reads, writes)

        h, _ = alloc([8, TMAX], F32); Rhh = [[Res("h%d_%d" % (b, t)) for t in range(3)] for b in range(8)]
        Rh_all = [r for rr_ in Rhh for r in rr_]
        cur_ci = [0]

        def rh(b, t0):
            for ti_, (a0, an) in enumerate(chunk_tiles(cur_ci[0])):
                if a0 <= t0 < a0 + an: return Rhh[b][ti_]
            raise AssertionError(t0)
        hn, off_hn = alloc([8, TMAX], BF16); Rhn = Res("hn")
        y3, off_y3 = alloc([16, TMAX], BF16); Ry3 = [Res("y3_%d" % b) for b in range(16)]
        NSLOT = int(_os.environ.get('KNSLOT', '5'))
        ring = []; Rring = []
        for i in range(NSLOT):
            v, o_ = alloc([4096], BF16); ring.append((v, o_)); Rring.append(Res("ring%d" % i))
        ridx = [0]
        WORK_BYTES = 49920
        work0 = mem_top[0]; mem_top[0] += WORK_BYTES
        stage = []; Rstage = []
        off_stage = mem_top[0]
        for i in range(2):
            v, _ = alloc([1024], F32); stage.append(v); Rstage.append(Res("stage%d" % i))
        sq = []; Rsq = []
        for i in range(2):
            v, _ = alloc([512], BF16); sq.append(v); Rsq.append(Res("sq%d" % i))
        rsb = []; Rrs = []
        for i in range(2):
            v, _ = alloc([512], F32); rsb.append(v); Rrs.append(Res("rs%d" % i))
        pg = SemGroup("params")
        identf, _ = alloc([128], F32); identb, _ = alloc([128], BF16); onesb, _ = alloc([128], BF16)
        maskf, _ = alloc([128], F32); invc, _ = alloc([16], F32)
        Rc = Res("consts"); Rc.multi = []
        gains, _ = alloc([5, 8], F32)
        bglu, _ = alloc([2, 8], F32)
        cw, _ = alloc([3, 16], F32); cb, _ = alloc([16], F32)
        pscale, _ = alloc([16], F32); pbg, _ = alloc([16], F32); pbs, _ = alloc([16], F32)
        Dg, _ = alloc([2, 64], F32)
        ArAr, _ = alloc([2, 32, 2], F32); AiS, _ = alloc([2, 32, 2], F32)
        A2rr, _ = alloc([2, 32, 2], F32); A2is, _ = alloc([2, 32, 2], F32)
        Rtab = [Res("tab0"), Res("tab1")]
        chist, _ = alloc([16, 2], F32); Rchist = Res("chist")
        phist, _ = alloc([16, 16], F32); Rphist = Res("phist")
        Scarry = None
        assert mem_top[0] <= ARENA_BYTES, mem_top[0]

        def wslot(nelem_shape):
            i = ridx[0]; ridx[0] = (i + 1) % NSLOT
            v, o_ = ring[i]
            return view_at(o_, nelem_shape, BF16), Rring[i]

        if _os.environ.get("KDUMP", ""):
            Rinit = Res("init")
            for i_ in range(0, ARENA_BYTES // 4, 8192):
                op("dve", "memset", [], [Rinit], arena[:, i_:min(i_ + 8192, ARENA_BYTES // 4)], 0.0)
            op("act", "copy", [Rinit], [Rinit], out=arena[:, 0:8], in_=arena[:, 0:8])
            op("pool", "tensor_copy", [Rinit], [Rinit], out=arena[:, 0:8], in_=arena[:, 0:8])
            S.add("pe", lambda e: e.matmul(psum[0][:, 0:8], arena[:, 0:128], arena[:, 0:8], start=True, stop=True), [Rinit], [psres[0]])
            S.add("sp", lambda e: e.dma_start(out=arena[:, 0:8], in_=arena[:, 8:16]), [Rinit], [Rinit], dma=True)
        dma("sp", identf, D["c_ident"], [], [Rc])
        dma("pool", identb, D["c_ident"], [], [Rc])
        dma("pool", onesb, D["c_ones"], [], [Rc])
        dma("sp", maskf, D["c_mask"], [], [Rc])
        dma("sp", invc, D["c_invc"], [], [Rc])
        for i, nm in enumerate(["norm0_g", "norm1_g", "norm2_g", "norm3_g", "final_g"]):
            dma("sp", gains[:, i, :], D[nm].rearrange("(b p) -> p b", p=128), [], [Rc], slow=True)
        for i, nm in enumerate(["l0_b_glu", "l3_b_glu"]):
            dma("sp", bglu[:, i, :], D[nm].rearrange("(b p) -> p b", p=128), [], [Rc], slow=True)
        for k in range(3):
            dma("sp", cw[:, k, :], D["l1_conv_w"][k].rearrange("(b p) -> p b", p=128), [], [Rc], slow=True)
        dma("sp", cb, D["l1_conv_b"].rearrange("(b p) -> p b", p=128), [], [Rc], slow=True)
        dma("sp", pscale, D["l2_scale"].rearrange("(b p) -> p b", p=128), [], [Rc], slow=True)
        for k_ in range(4):
            dma("sp", pbg[:, 4 * k_:4 * k_ + 4], D["l2_b_grp"][k_].rearrange("(b p) -> p b", p=128), [], [Rc], slow=True)
        for li, nm in enumerate(["l0_d_skip", "l3_d_skip"]):
            for tau in range(8):
                dma("sp", Dg[tau * 16:(tau + 1) * 16, li, :], D[nm].rearrange("(g i) -> i g", i=16), [], [Rc], slow=True)
        Rpbs = Res("pbs")
        op("dve", "tensor_tensor", [Rc], [Rpbs], out=pbs, in0=pbg, in1=pscale, op=ALU.mult)
        op("dve", "memset", [], [Rphist], phist, 0.0)
        op("dve", "memset", [], [Rchist], chist, 0.0)

        KSTOP = _os.environ.get("KSTOP", "")
        if KSTOP == "consts": S.frozen = True
        s5layers = [l for l in layers if l in (0, 3)]
        Rscr = {k: Res("scr%s%d" % k) for k in scr}
        if SKIP == "arena":
            pass

        PRO_RES = {}

        def s5_prologue(l):
            S.phase = "pro%d" % l
            li = 0 if l == 0 else 1
            pre = "l%d_" % l
            base = [0]

            def A(shape, dt=F32):
                n = 1
                for s_ in shape: n *= s_
                nb = (n * (2 if dt == BF16 else 4) + 31) // 32 * 32
                off = base[0]; base[0] += nb
                assert base[0] <= work0 + WORK_BYTES
                return view_at(off, shape, dt)
            Rp = PRO_RES

            def R(n):
                if n not in Rp: Rp[n] = Res("p_%s" % n)
                return Rp[n]
            lamR = A([64]); lamI = A([64]); ldt = A([1]); ldtb = A([64]); selG = A([2, 32]); selC = A([2, 64])
            dma("sp", lamR[0:64], D[pre + "lam_re"], [], [R("lamR")])
            dma("sp", lamI[0:64], D[pre + "lam_im"], [], [R("lamI")])
            dma("sp", ldt[0:64], D[pre + "log_dt"].rearrange("(g o) -> g o", o=1), [], [R("ldt")])
            dma("sp", selG[0:64], D["c_selG"], [], [R("selG")])
            dma("sp", selC, D["c_selC"], [], [R("selC")])
            op("dve", "tensor_copy", [R("ldt")], [R("ldtb")], out=ldtb[0:64], in_=ldt[0:64, 0:1].to_broadcast([64, 64]))
            pb_, pr_ = bank()
            for par in range(2):
                rows = slice(par * 64, par * 64 + 64)
                mm(pb_[rows, 0:32], lamR[0:64], selG[0:64, par, :], True, True, [R("lamR"), R("selG")], [pr_])
                mm(pb_[rows, 32:64], lamI[0:64], selG[0:64, par, :], True, True, [R("lamI"), R("selG")], [pr_])
                mm(pb_[rows, 64:96], ldtb[0:64], selG[0:64, par, :], True, True, [R("ldtb"), R("selG")], [pr_])
            sm = A([24, 32])
            Rsm = R("sm")
            lr, li_, ld, dt, x1, mag, ang, v_, kf, r_, m_, sn, cs, ar, ai, am1, den, t_, kr, ki = [sm[:, i, :] for i in range(20)]
            kint = kint_t[:, :]; _ = A([32], I32)
            op("act", "copy", [pr_], [Rsm], out=sm[:, 0:3, :], in_=pb_[:, 0:96].rearrange("p (a b) -> p a b", b=32))

            def dv(method, *a, **k):
                return op("dve", method, [Rsm], [Rsm], *a, **k)

            def ac(*a, **k):
                return op("act", "activation", [Rsm], [Rsm], *a, **k)
            ac(out=dt, in_=ld, func=AF.Exp)
            dv("tensor_tensor", out=x1, in0=lr, in1=dt, op=ALU.mult)
            ac(out=mag, in_=x1, func=AF.Exp)
            dv("tensor_tensor", out=ang, in0=li_, in1=dt, op=ALU.mult)
            ac(out=sn, in_=ang, func=AF.Sin, scale=1.0 / 8)
            ac(out=v_, in_=ang, func=AF.Sin, scale=1.0 / 16)
            dv("tensor_tensor", out=v_, in0=v_, in1=v_, op=ALU.mult)
            dv("tensor_scalar", out=cs, in0=v_, scalar1=-2.0, scalar2=1.0, op0=ALU.mult, op1=ALU.add)
            for _d in range(3):
                dv("tensor_tensor", out=kf, in0=cs, in1=cs, op=ALU.mult)
                dv("tensor_tensor", out=r_, in0=sn, in1=sn, op=ALU.mult)
                dv("scalar_tensor_tensor", out=sn, in0=cs, scalar=2.0, in1=sn, op0=ALU.mult, op1=ALU.mult)
                dv("tensor_tensor", out=cs, in0=kf, in1=r_, op=ALU.subtract)
            dv("tensor_tensor", out=ar, in0=mag, in1=cs, op=ALU.mult)
            dv("tensor_tensor", out=ai, in0=mag, in1=sn, op=ALU.mult)
            dv("tensor_scalar", out=am1, in0=ar, scalar1=-1.0, scalar2=None, op0=ALU.add)
            dv("tensor_tensor", out=den, in0=lr, in1=lr, op=ALU.mult)
            dv("tensor_tensor", out=t_, in0=li_, in1=li_, op=ALU.mult)
            dv("tensor_tensor", out=den, in0=den, in1=t_, op=ALU.add)
            dv("reciprocal", out=den, in_=den)
            dv("tensor_tensor", out=kr, in0=am1, in1=lr, op=ALU.mult)
            dv("tensor_tensor", out=t_, in0=ai, in1=li_, op=ALU.mult)
            dv("tensor_tensor", out=kr, in0=kr, in1=t_, op=ALU.add)
            dv("tensor_tensor", out=kr, in0=kr, in1=den, op=ALU.mult)
            dv("tensor_tensor", out=ki, in0=ai, in1=lr, op=ALU.mult)
            dv("tensor_tensor", out=t_, in0=am1, in1=li_, op=ALU.mult)
            dv("tensor_tensor", out=ki, in0=ki, in1=t_, op=ALU.subtract)
            dv("tensor_tensor", out=ki, in0=ki, in1=den, op=ALU.mult)
            EPr = A([32, 9]); EPi = A([32, 9]); ERr = A([32, 8]); ERi = A([32, 8])
            dv("memset", EPr[:, :, 0], 1.0); dv("memset", EPi[:, :, 0], 0.0)
            dv("tensor_copy", out=EPr[:, :, 1], in_=ar); dv("tensor_copy", out=EPi[:, :, 1], in_=ai)
            for q in range(2, 9):
                dv("tensor_tensor", out=EPr[:, :, q], in0=EPr[:, :, q - 1], in1=ar, op=ALU.mult)
                dv("tensor_tensor", out=t_, in0=EPi[:, :, q - 1], in1=ai, op=ALU.mult)
                dv("tensor_tensor", out=EPr[:, :, q], in0=EPr[:, :, q], in1=t_, op=ALU.subtract)
                dv("tensor_tensor", out=EPi[:, :, q], in0=EPr[:, :, q - 1], in1=ai, op=ALU.mult)
                dv("tensor_tensor", out=t_, in0=EPi[:, :, q - 1], in1=ar, op=ALU.mult)
                dv("tensor_tensor", out=EPi[:, :, q], in0=EPi[:, :, q], in1=t_, op=ALU.add)
            for tau in range(8):
                dv("tensor_copy", out=ERr[:, :, tau], in_=EPr[:, :, 7 - tau])
                dv("tensor_copy", out=ERi[:, :, tau], in_=EPi[:, :, 7 - tau])
            Ir = sm[:, 20, :]; Ii = sm[:, 21, :]; n8 = sm[:, 22, :]
            dv("tensor_tensor", out=n8, in0=EPr[:, :, 8], in1=EPr[:, :, 8], op=ALU.mult)
            dv("tensor_tensor", out=t_, in0=EPi[:, :, 8], in1=EPi[:, :, 8], op=ALU.mult)
            dv("tensor_tensor", out=n8, in0=n8, in1=t_, op=ALU.add)
            dv("reciprocal", out=n8, in_=n8)
            dv("tensor_tensor", out=Ir, in0=EPr[:, :, 8], in1=n8, op=ALU.mult)
            dv("scalar_tensor_tensor", out=Ii, in0=EPi[:, :, 8], scalar=-1.0, in1=n8, op0=ALU.mult, op1=ALU.mult)
            op("dve", "tensor_copy", [Rsm], [Rtab[li]], out=ArAr[:, li, :, 0], in_=EPr[:, :, 8])
            op("dve", "tensor_copy", [Rsm], [Rtab[li]], out=ArAr[:, li, :, 1], in_=EPr[:, :, 8])
            op("dve", "tensor_scalar", [Rsm], [Rtab[li]], out=AiS[:, li, :, 0], in0=EPi[:, :, 8], scalar1=-1.0, scalar2=None, op0=ALU.mult)
            op("dve", "tensor_copy", [Rsm], [Rtab[li]], out=AiS[:, li, :, 1], in_=EPi[:, :, 8])
            dv("tensor_tensor", out=kf, in0=EPr[:, :, 8], in1=EPr[:, :, 8], op=ALU.mult)
            dv("tensor_tensor", out=r_, in0=EPi[:, :, 8], in1=EPi[:, :, 8], op=ALU.mult)
            dv("tensor_tensor", out=kf, in0=kf, in1=r_, op=ALU.subtract)
            dv("scalar_tensor_tensor", out=r_, in0=EPr[:, :, 8], scalar=2.0, in1=EPi[:, :, 8], op0=ALU.mult, op1=ALU.mult)
            op("dve", "tensor_copy", [Rsm], [Rtab[li]], out=A2rr[:, li, :, 0], in_=kf)
            op("dve", "tensor_copy", [Rsm], [Rtab[li]], out=A2rr[:, li, :, 1], in_=kf)
            op("dve", "tensor_scalar", [Rsm], [Rtab[li]], out=A2is[:, li, :, 0], in0=r_, scalar1=-1.0, scalar2=None, op0=ALU.mult)
            op("dve", "tensor_copy", [Rsm], [Rtab[li]], out=A2is[:, li, :, 1], in_=r_)
            Bre = A([32, 16]); Bim = A([32, 16]); bbr = A([32, 16]); bbi = A([32, 16]); tb = A([32, 16])
            Cre = A([32, 16]); Cim = A([32, 16])
            for par in range(2):
                rows = slice(par * 64, par * 64 + 64)
                for nm, dst in (("b_re", Bre), ("b_im", Bim)):
                    src = D[pre + nm].rearrange("(g2 q) p i -> q p g2 i", q=2)[par]
                    dma("sp", dst[rows], src, [], [R("Bsrc")])
            Xt = A([8, 64])
            for nm, dst in (("c_re", Cre), ("c_im", Cim)):
                srcv = D[pre + nm].rearrange("(a gl) o p -> a (gl o) p", gl=8)
                for a in range(8):
                    dma("sp", Xt[:, a, :], srcv[a], [], [R("Xt%d" % a)])
                pb_, pr_ = bank()
                for a in range(8):
                    for par in range(2):
                        rows = slice(par * 64, par * 64 + 64)
                        mm(pb_[rows, a * 64:(a + 1) * 64], Xt[:, a, :], selC[:, par, :], True, True, [R("Xt%d" % a), R("selC")], [pr_])
                op("act", "copy", [pr_], [R("Csrc")], out=dst, in_=pb_[:, 0:512].rearrange("p (a b) -> p a b", b=16))
            RB = R("bb")
            krb = kr.unsqueeze(2).to_broadcast([128, 32, 16]); kib = ki.unsqueeze(2).to_broadcast([128, 32, 16])
            op("dve", "tensor_tensor", [Rsm, R("Bsrc")], [RB], out=bbr, in0=Bre, in1=krb, op=ALU.mult)
            op("dve", "tensor_tensor", [Rsm, R("Bsrc")], [RB], out=tb, in0=Bim, in1=kib, op=ALU.mult)
            op("dve", "tensor_tensor", [RB], [RB], out=bbr, in0=bbr, in1=tb, op=ALU.subtract)
            op("dve", "tensor_tensor", [Rsm, R("Bsrc")], [RB], out=bbi, in0=Bim, in1=krb, op=ALU.mult)
            op("dve", "tensor_tensor", [Rsm, R("Bsrc")], [RB], out=tb, in0=Bre, in1=kib, op=ALU.mult)
            op("dve", "tensor_tensor", [RB], [RB], out=bbi, in0=bbi, in1=tb, op=ALU.add)
            Bqr = A([16, 8, 16]); Bqi = A([16, 8, 16]); Bmr = A([16, 8, 16]); Bmi = A([16, 8, 16])
            C1r = A([16, 8, 16]); C1i = A([16, 8, 16]); t1 = A([16, 8, 16]); t2 = A([16, 8, 16])
            Tsb = A([32, 128], BF16); Bsb = A([32, 128], BF16); Csb = A([16, 2, 128], BF16)
            B2r = A([16, 8, 16]); B2i = A([16, 8, 16]); B2sb = A([32, 128], BF16)
            tmpT = [A([128]), A([128])]
            RT = [R("tmpT0"), R("tmpT1")]
            for hf in range(2):
                gs = slice(hf * 16, hf * 16 + 16)
                sh = [128, 16, 8, 16]
                RA = R("big")
                ErB = ERr[:, gs, :].unsqueeze(3).to_broadcast(sh); EiB = ERi[:, gs, :].unsqueeze(3).to_broadcast(sh)
                brB = bbr[:, gs, :].unsqueeze(2).to_broadcast(sh); biB = bbi[:, gs, :].unsqueeze(2).to_broadcast(sh)

                def big(eng, method, *a, **k):
                    return op(eng, method, [Rsm, RB, R("Csrc"), RA], [RA], *a, **k)
                big("dve", "tensor_tensor", out=Bqr, in0=ErB, in1=brB, op=ALU.mult)
                big("dve", "tensor_tensor", out=t1, in0=EiB, in1=biB, op=ALU.mult)
                big("dve", "tensor_tensor", out=Bqr, in0=Bqr, in1=t1, op=ALU.subtract)
                big("dve", "tensor_tensor", out=Bqi, in0=ErB, in1=biB, op=ALU.mult)
                big("dve", "tensor_tensor", out=t1, in0=EiB, in1=brB, op=ALU.mult)
                big("dve", "tensor_tensor", out=Bqi, in0=Bqi, in1=t1, op=ALU.add)
                A8r = EPr[:, gs, 8].unsqueeze(2).unsqueeze(3).to_broadcast(sh); A8i = EPi[:, gs, 8].unsqueeze(2).unsqueeze(3).to_broadcast(sh)
                big("dve", "tensor_tensor", out=t1, in0=Bqr, in1=A8r, op=ALU.mult)
                big("dve", "tensor_tensor", out=t2, in0=Bqi, in1=A8i, op=ALU.mult)
                big("dve", "tensor_tensor", out=B2r, in0=t1, in1=t2, op=ALU.subtract)
                big("dve", "tensor_tensor", out=t1, in0=Bqi, in1=A8r, op=ALU.mult)
                big("dve", "tensor_tensor", out=t2, in0=Bqr, in1=A8i, op=ALU.mult)
                big("dve", "tensor_tensor", out=B2i, in0=t1, in1=t2, op=ALU.add)
                IrB = Ir[:, gs].unsqueeze(2).unsqueeze(3).to_broadcast(sh); IiB = Ii[:, gs].unsqueeze(2).unsqueeze(3).to_broadcast(sh)
                big("dve", "tensor_tensor", out=Bmr, in0=Bqr, in1=IrB, op=ALU.mult)
                big("dve", "tensor_tensor", out=t1, in0=Bqi, in1=IiB, op=ALU.mult)
                big("dve", "tensor_tensor", out=Bmr, in0=Bmr, in1=t1, op=ALU.subtract)
                big("dve", "tensor_tensor", out=Bmi, in0=Bqi, in1=IrB, op=ALU.mult)
                big("dve", "tensor_tensor", out=t1, in0=Bqr, in1=IiB, op=ALU.mult)
                big("dve", "tensor_tensor", out=Bmi, in0=Bmi, in1=t1, op=ALU.add)
                E1r = EPr[:, gs, 1:9].unsqueeze(3).to_broadcast(sh); E1i = EPi[:, gs, 1:9].unsqueeze(3).to_broadcast(sh)
                crB = Cre[:, gs, :].unsqueeze(2).to_broadcast(sh); ciB = Cim[:, gs, :].unsqueeze(2).to_broadcast(sh)
                big("dve", "tensor_tensor", out=C1r, in0=E1r, in1=crB, op=ALU.mult)
                big("dve", "tensor_tensor", out=t1, in0=E1i, in1=ciB, op=ALU.mult)
                big("dve", "tensor_tensor", out=C1r, in0=C1r, in1=t1, op=ALU.subtract)
                big("dve", "tensor_tensor", out=t1, in0=E1i, in1=crB, op=ALU.mult)
                big("dve", "tensor_tensor", out=t2, in0=E1r, in1=ciB, op=ALU.mult)
                big("dve", "scalar_tensor_tensor", out=C1i, in0=t1, scalar=-1.0, in1=t2, op0=ALU.mult, op1=ALU.subtract)
                Rout = R("outsb")
                op("act", "copy", [RA], [Rout], out=Csb[:, :, 0, :], in_=C1r.rearrange("p a b c -> p a (b c)"))
                op("act", "copy", [RA], [Rout], out=Csb[:, :, 1, :], in_=C1i.rearrange("p a b c -> p a (b c)"))
                for gl in range(32):
                    g = hf * 32 + gl; g2l = gl // 2; par = gl % 2
                    rows = slice(par * 64, par * 64 + 64)
                    pb_, pr_ = bank()
                    mm(pb_[:, 0:128], Bmr[rows, g2l].rearrange("p a b -> p (a b)"), C1r[rows, g2l].rearrange("p a b -> p (a b)"), True, False, [RA], [pr_])
                    mm(pb_[:, 0:128], Bmi[rows, g2l].rearrange("p a b -> p (a b)"), C1i[rows, g2l].rearrange("p a b -> p (a b)"), False, True, [RA], [pr_])
                    tt = tmpT[gl % 2]; rt = RT[gl % 2]
                    op("dve", "tensor_tensor", [pr_, Rc], [rt], out=tt, in0=pb_[:, 0:128], in1=maskf, op=ALU.mult)
                    op("dve", "scalar_tensor_tensor", [rt, Rc], [Rout], out=Tsb[:, gl, :], in0=identf, scalar=Dg[:, li, g:g + 1], in1=tt, op0=ALU.mult, op1=ALU.add)
                    pb2, pr2 = bank()
                    tr(pb2[:, 0:64], Bqr[rows, g2l].rearrange("p a b -> p (a b)"), identf[rows, par * 64:par * 64 + 64], [RA, Rc], [pr2])
                    tr(pb2[:, 64:128], Bqi[rows, g2l].rearrange("p a b -> p (a b)"), identf[rows, par * 64:par * 64 + 64], [RA, Rc], [pr2])
                    op("act", "copy", [pr2], [Rout], out=Bsb[:, gl, :], in_=pb2[:, 0:128])
                    pb3, pr3 = bank()
                    tr(pb3[:, 0:64], B2r[rows, g2l].rearrange("p a b -> p (a b)"), identf[rows, par * 64:par * 64 + 64], [RA, Rc], [pr3])
                    tr(pb3[:, 64:128], B2i[rows, g2l].rearrange("p a b -> p (a b)"), identf[rows, par * 64:par * 64 + 64], [RA, Rc], [pr3])
                    op("act", "copy", [pr3], [Rout], out=B2sb[:, gl, :], in_=pb3[:, 0:128])
                cols = slice(hf * 4096, hf * 4096 + 4096)
                dma("sp", scr[("T", l)][:, cols], Tsb.rearrange("p a b -> p (a b)"), [Rout], [Rscr[("T", l)]])
                dma("sp", scr[("B", l)][:, cols], Bsb.rearrange("p a b -> p (a b)"), [Rout], [Rscr[("B", l)]])
                dma("sp", scr[("B2", l)][:, cols], B2sb.rearrange("p a b -> p (a b)"), [Rout], [Rscr[("B2", l)]])
                dma("sp", scr[("C", l)][:, cols], Csb.rearrange("p a b c -> p (a b c)"), [Rout], [Rscr[("C", l)]])

        for l in s5layers:
            s5_prologue(l)
        if KSTOP == "pro": S.frozen = True

        u_tm = view_at(work0, [8, 1024], BF16); Rutm = [Res("u_tm%d" % i) for i in range(8)]
        u_tmu = view_at(work0, [64, 8, 16], BF16)
        Sst = view_at(work0 + 16384, [32, 2, 131], F32); RSh = [Res("Sst0"), Res("Sst1")]
        Uv = view_at(off_y3 + 16640, [64, 128], BF16); RU = [Res("U%d" % i) for i in range(16)]
        yfm = view_at(off_y3 + 16640, [8, TMAX], BF16)
        Sb = view_at(off_stage, [16, 2, 128], BF16); RSb = Res("Sb")
        s5work = Rutm + RSh
        UA, _ = alloc([64, 2], BF16); RUA = Res("UA")
        yfmA, _ = alloc([8, 16], BF16); RyA = Res("yfmA")
        SbA, _ = alloc([32, 2, 2], BF16); RSbA = Res("SbA")
        scan_t1, _ = alloc([32, 2], F32); scan_t2, _ = alloc([32, 2], F32)
        scan_t1b, _ = alloc([32, 2], F32); scan_t2b, _ = alloc([32, 2], F32)
        Rscan1 = [Res("scan1_0"), Res("scan1_1")]; Rscan2 = [Res("scan2_0"), Res("scan2_1")]
        sig = []; Rsig = []
        for i in range(2):
            v, _ = alloc([512], F32); sig.append(v); Rsig.append(Res("sig%d" % i))
        carry, _ = alloc([2, 32, 2], F32); Rcarry = [Res("carry0"), Res("carry1")]
        assert mem_top[0] <= ARENA_BYTES, mem_top[0]
        o = work0
        tcg = [view_at(o + i * 2048, [512], F32) for i in range(2)]; o += 4096
        hcb = [view_at(o + i * 4192, [1048], F32) for i in range(2)]; o += 8384
        c1b = [view_at(o + i * 4160, [TMAX], F32) for i in range(2)]; o += 8320
        szb = [view_at(o + i * 2048, [512], F32) for i in range(2)]; o += 4096
        yvb = [view_at(o + i * 2048, [512], F32) for i in range(2)]; o += 4096
        Rtcg = [Res("tcg%d" % i) for i in range(2)]; Rhc = [Res("hc%d" % i) for i in range(2)]
        Rc1 = [Res("c1%d" % i) for i in range(2)]; Rsz = [Res("sz%d" % i) for i in range(2)]; Ryv = [Res("yv%d" % i) for i in range(2)]
        Rhch = [Res("hch%d" % i) for i in range(2)]
        convwork = Rtcg + Rhc + Rhch + Rc1 + Rsz + Ryv
        o = work0
        ubb = [view_at(o + i * 4224, [1056], F32) for i in range(2)]; o += 8448
        lvb = [view_at(o + i * 4224, [1056], F32) for i in range(4)]; o += 16896
        mixb = [view_at(o + i * 8320, [4, TMAX], BF16) for i in range(2)]; o += 16640
        yvp = [view_at(o + i * 2048, [512], F32) for i in range(2)]; o += 4096
        tfix = view_at(o, [16], F32); o += 64
        assert o <= work0 + WORK_BYTES
        Rub = [Res("ub%d" % i) for i in range(2)]; Rlv = [Res("lv%d" % i) for i in range(4)]
        Rmix = [Res("mix%d" % i) for i in range(2)]; Ryvp = [Res("yvp%d" % i) for i in range(2)]; Rtfix = Res("tfix")
        poolwork = Rub + Rlv + Rmix + Ryvp + [Rtfix]
        cur_work = [None]
        global _DBG_OFFS
        _DBG_OFFS = dict(off_hn=off_hn, off_y3=off_y3, work0=work0, off_stage=off_stage)
        S.realias(list(PRO_RES.values()), Rh_all + [Rhn] + Ry3 + Rring + s5work + convwork + poolwork)

        def use_work(kind):
            new = {"s5": s5work, "conv": convwork, "pool": poolwork}[kind]
            if cur_work[0] is not None and cur_work[0] is not new:
                S.realias(cur_work[0], new)
            cur_work[0] = new

        def load_w(eng, src_ap, shape, reads=()):
            v, r = wslot(shape)
            dma(eng, v, src_ap, list(reads), [r])
            return v, r

        def load_w_parts(eng, shape, partfn, nparts, reads=()):
            v, r = wslot(shape)
            prev = None
            for i in range(nparts):
                d_, s_ = partfn(v, i)
                prev = dma(eng, d_, s_, list(reads), [r], part_of=prev)
            return v, r

        def rmsnorm_stats(t0, tn, k):
            pb_, pr_ = bank()
            for b in range(8):
                j = b % 2
                op("act", "activation", [rh(b, t0)], [Rsq[j]], out=sq[j][:, 0:tn], in_=h[:, b, t0:t0 + tn], func=AF.Square)
                mm(pb_[:, 0:tn], onesb, sq[j][:, 0:tn], b == 0, b == 7, [Rsq[j], Rc], [pr_])
            op("act", "activation", [pr_], [Rrs[k]], out=rsb[k][:, 0:tn], in_=pb_[:, 0:tn], func=AF.Sqrt, bias=EPS, scale=1.0 / DM)
            op("dve", "reciprocal", [Rrs[k]], [Rrs[k]], out=rsb[k][:, 0:tn], in_=rsb[k][:, 0:tn])

        nrm_k = [0]

        def norm_to_hn(ci, gi):
            for (t0, tn) in chunk_tiles(ci):
                k = nrm_k[0]; nrm_k[0] ^= 1
                rmsnorm_stats(t0, tn, k)
                for b in range(8):
                    op("dve", "scalar_tensor_tensor", [rh(b, t0), Rrs[k], Rc], [Rhn], out=hn[:, b, t0:t0 + tn], in0=h[:, b, t0:t0 + tn],
                       scalar=gains[:, gi, b:b + 1], in1=rsb[k][:, 0:tn], op0=ALU.mult, op1=ALU.mult)

        def out_proj(ci, wname, nk):
            colw = 4096 // nk
            nslots = DM // colw
            slots = []
            for s_ in range(nslots):
                slots.append(load_w("pool", D[wname][:, s_ * colw:(s_ + 1) * colw].rearrange("(k p) c -> p k c", p=128), [nk, colw]))
            for (t0, tn) in chunk_tiles(ci):
                for s_ in range(nslots):
                    wv, wr = slots[s_]
                    for db in range(colw // 128):
                        dblk = s_ * (colw // 128) + db
                        pb_, pr_ = bank()
                        for k in range(nk):
                            mm(pb_[:, 0:tn], wv[:, k, db * 128:(db + 1) * 128], y3[:, k, t0:t0 + tn], k == 0, k == nk - 1, [wr, Ry3[k]], [pr_])
                        op("dve", "tensor_tensor", [pr_, rh(dblk, t0)], [rh(dblk, t0)], out=h[:, dblk, t0:t0 + tn], in0=pb_[:, 0:tn], in1=h[:, dblk, t0:t0 + tn], op=ALU.add)

        stg_i = [0]

        def load_chunk(ci):
            S.phase = "load"
            c0, T = CHUNKS[ci]
            tiles = []
            if ci == 0:
                tiles.append(("meta", 0, 16, 0))
                for j in range(8): tiles.append(("x", j * 128, 128, 16 + j * 128))
            else:
                for j in range(8): tiles.append(("x", c0 - NMETA + j * 128, 128, j * 128))
            for ti_, (kind, r0, nr, col) in enumerate(tiles):
                if KSTOP == "load%d" % ti_: S.frozen = True
                si = stg_i[0]; stg_i[0] ^= 1
                src = D["meta_tokens"] if kind == "meta" else D["x"][r0:r0 + nr, :]
                dma("sp", stage[si][0:nr, :], src, [], [Rstage[si]])
                for half in range(2):
                    pb_, pr_ = bank()
                    for q in range(4):
                        b = half * 4 + q
                        tr(pb_[:, q * 128:q * 128 + nr], stage[si][0:nr, b * 128:(b + 1) * 128], identf[0:nr, 0:nr], [Rstage[si], Rc], [pr_])
                    for q in range(4):
                        b = half * 4 + q
                        op("act" if half == 0 else "dve", "copy" if half == 0 else "tensor_copy", [pr_], [rh(b, col)],
                           out=h[:, b, col:col + nr], in_=pb_[:, q * 128:q * 128 + nr])

        def final_store(ci):
            S.phase = "final"
            c0, T = CHUNKS[ci]
            hf_, _o = None, None
            for (t0, tn) in chunk_tiles(ci):
                if ci == 0 and t0 == 0: continue
                k = nrm_k[0]; nrm_k[0] ^= 1
                rmsnorm_stats(t0, tn, k)
                if KSTOP == "stats": S.frozen = True
                hf = view_at(off_y3, [8, 512], F32)
                for b in range(8):
                    op("dve", "scalar_tensor_tensor", [rh(b, t0), Rrs[k], Rc], Ry3[0:8], out=hf[:, b, 0:tn], in0=h[:, b, t0:t0 + tn],
                       scalar=gains[:, 4, b:b + 1], in1=rsb[k][:, 0:tn], op0=ALU.mult, op1=ALU.mult)
                for sub in range(tn // 128):
                    si = stg_i[0]; stg_i[0] ^= 1
                    for half in range(2):
                        pb_, pr_ = bank()
                        for q in range(4):
                            b = half * 4 + q
                            tr(pb_[:, q * 128:(q + 1) * 128], hf[:, b, sub * 128:(sub + 1) * 128], identf, Ry3[0:8] + [Rc], [pr_])
                        op("act" if si == 0 else "dve", "copy" if si == 0 else "tensor_copy", [pr_], [Rstage[si]],
                           out=stage[si][:, half * 512:(half + 1) * 512], in_=pb_[:, 0:512])
                    row0 = c0 + t0 + sub * 128 - NMETA
                    dma("sp", out_d[row0:row0 + 128, :], stage[si], [Rstage[si]], [Res("outd")], semres=Rstage[si])

        def layer_conv(ci):
            c0, T = CHUNKS[ci]
            use_work("conv")
            norm_to_hn(ci, 1)
            for e in range(16):
                srcw = D["l1_w_in"].rearrange("(k p) (q c) -> p k q c", p=128, q=4)
                wv, wr = load_w_parts("pool", [8, 4, 128], lambda v, i, e=e, srcw=srcw: (v[:, :, i, :], srcw[:, :, i, e * 128:(e + 1) * 128]), 4)
                j = e % 2
                hc = hcb[j]; c1 = c1b[j]
                op("act", "copy", [Rchist], [Rhch[j]], out=hc[:, 0:2], in_=chist[:, e, :])
                for (t0, tn) in chunk_tiles(ci):
                    banks = []
                    for part in (1, 2, 0, 3):
                        pb_, pr_ = bank()
                        for b in range(8):
                            mm(pb_[:, 0:tn], wv[:, b, part, :], hn[:, b, t0:t0 + tn], b == 0, b == 7, [wr, Rhn], [pr_])
                        banks.append((pb_, pr_))
                    (pcg, rcg), (pv, rv), (pbg_, rbg), (pz, rz) = banks
                    op("act", "copy", [rcg], [Rtcg[j]], out=tcg[j][:, 0:tn], in_=pcg[:, 0:tn])
                    op("dve", "tensor_tensor", [Rtcg[j], rv], [Rhc[j]], out=hc[:, 2 + t0:2 + t0 + tn], in0=pv[:, 0:tn], in1=tcg[j][:, 0:tn], op=ALU.mult)
                    op("act", "activation", [rz], [Rsz[j]], out=szb[j][:, 0:tn], in_=pz[:, 0:tn], func=AF.Silu)
                    op("act", "activation", [Rhc[j], Rc], [Rc1[j]], out=c1[:, t0:t0 + tn], in_=hc[:, 2 + t0:2 + t0 + tn], func=AF.Identity,
                       bias=cb[:, e:e + 1], scale=cw[:, 2, e:e + 1])
                    op("dve", "scalar_tensor_tensor", [Rhc[j], Rhch[j], Rc1[j], Rc], [Rc1[j]], out=c1[:, t0:t0 + tn], in0=hc[:, 1 + t0:1 + t0 + tn],
                       scalar=cw[:, 1, e:e + 1], in1=c1[:, t0:t0 + tn], op0=ALU.mult, op1=ALU.add)
                    op("dve", "scalar_tensor_tensor", [Rhc[j], Rhch[j], Rc1[j], Rc], [Rc1[j]], out=c1[:, t0:t0 + tn], in0=hc[:, t0:t0 + tn],
                       scalar=cw[:, 0, e:e + 1], in1=c1[:, t0:t0 + tn], op0=ALU.mult, op1=ALU.add)
                    op("dve", "tensor_tensor", [Rc1[j], rbg], [Ryv[j]], out=yvb[j][:, 0:tn], in0=pbg_[:, 0:tn], in1=c1[:, t0:t0 + tn], op=ALU.mult)
                    op("dve", "tensor_tensor", [Ryv[j], Rsz[j]], [Ry3[e]], out=y3[:, e, t0:t0 + tn], in0=yvb[j][:, 0:tn], in1=szb[j][:, 0:tn], op=ALU.mult)
                op("act", "copy", [Rhc[j]], [Rchist], out=chist[:, e, :], in_=hc[:, T:T + 2])
            out_proj(ci, "l1_w_out", 16)

        def layer_pool(ci):
            c0, T = CHUNKS[ci]
            use_work("pool")
            norm_to_hn(ci, 2)
            grp_calls = []
            def GRP(k):
                mix = mixb[k % 2]; rmix = Rmix[k % 2]
                wv, wr = load_w("pool", D["l2_w_grp"][k].rearrange("(kk p) c -> p kk c", p=128), [4, 512])
                for eo in range(4):
                    e = 4 * k + eo
                    for (t0, tn) in chunk_tiles(ci):
                        pb_, pr_ = bank()
                        for ei in range(4):
                            mm(pb_[:, 0:tn], wv[:, ei, eo * 128:(eo + 1) * 128], mix[:, ei, t0:t0 + tn], ei == 0, ei == 3, [wr, rmix], [pr_])
                        jj = eo % 2
                        op("act", "activation", [pr_, Rc, Rpbs], [Ryvp[jj]], out=yvp[jj][:, 0:tn], in_=pb_[:, 0:tn], func=AF.Identity,
                           bias=pbs[:, e:e + 1], scale=pscale[:, e:e + 1])
                        op("dve", "tensor_tensor", [Ryvp[jj], Ry3[e]], [Ry3[e]], out=y3[:, e, t0:t0 + tn], in0=yvp[jj][:, 0:tn], in1=y3[:, e, t0:t0 + tn], op=ALU.mult)

            for k in range(4):
                w = 2 << k
                mix = mixb[k % 2]; rmix = Rmix[k % 2]
                for epair in range(2):
                    e0 = 4 * k + 2 * epair
                    srcw = D["l2_w_in"].rearrange("(kk p) (q c) -> p kk q c", p=128, q=2)
                    wv, wr = load_w_parts("pool", [8, 2, 256], lambda v, i, e0=e0, srcw=srcw: (v[:, :, i, :], srcw[:, :, i, e0 * 128:(e0 + 2) * 128]), 2)
                    for el in range(2):
                        e = e0 + el; ei = 2 * epair + el
                        j = e % 2
                        ub = ubb[j]
                        op("act", "copy", [Rphist], [Rub[j]], out=ub[:, 0:16], in_=phist[:, e, :])
                        for (t0, tn) in chunk_tiles(ci):
                            pb_, pr_ = bank()
                            for b in range(8):
                                mm(pb_[:, 0:tn], wv[:, b, 0, el * 128:(el + 1) * 128], hn[:, b, t0:t0 + tn], b == 0, b == 7, [wr, Rhn], [pr_])
                            op("act", "copy", [pr_], [Rub[j]], out=ub[:, 16 + t0:16 + t0 + tn], in_=pb_[:, 0:tn])
                            pb2, pr2 = bank()
                            for b in range(8):
                                mm(pb2[:, 0:tn], wv[:, b, 1, el * 128:(el + 1) * 128], hn[:, b, t0:t0 + tn], b == 0, b == 7, [wr, Rhn], [pr2])
                            op("act", "activation", [pr2], [Ry3[e]], out=y3[:, e, t0:t0 + tn], in_=pb2[:, 0:tn], func=AF.Silu)
                        op("act", "copy", [Rub[j]], [Rphist], out=phist[:, e, :], in_=ub[:, T:T + 16])
                        cur = ub; rcur = Rub[j]; lo = 0
                        NE = 16 + T
                        for lev in range(k + 1):
                            sft = 1 << lev
                            nxt = lvb[lev % 2 + 2 * j]
                            rn = Rlv[lev % 2 + 2 * j]
                            nlo = lo + sft
                            op("dve", "tensor_tensor", [rcur], [rn], out=nxt[:, nlo:NE], in0=cur[:, nlo:NE], in1=cur[:, nlo - sft:NE - sft], op=ALU.add)
                            cur = nxt; rcur = rn; lo = nlo
                        op("dve", "scalar_tensor_tensor", [rcur, Rub[j]], [rmix], out=mix[:, ei, 0:T], in0=cur[:, 16:16 + T], scalar=1.0 / w,
                           in1=ub[:, 16:16 + T], op0=ALU.mult, op1=ALU.subtract)
                        if ci == 0:
                            op("dve", "tensor_tensor", [rcur, Rc], [Rtfix], out=tfix[:, 0:w - 1], in0=cur[:, 16:16 + w - 1], in1=invc[:, 0:w - 1], op=ALU.mult)
                            op("dve", "tensor_tensor", [Rtfix, Rub[j]], [rmix], out=mix[:, ei, 0:w - 1], in0=tfix[:, 0:w - 1], in1=ub[:, 16:16 + w - 1], op=ALU.subtract)
                if k >= 1:
                    grp_calls.append(k - 1); GRP(k - 1)
            GRP(3)
            out_proj(ci, "l2_w_out", 16)

        def layer_s5_full(ci, l):
            c0, T = CHUNKS[ci]
            li = 0 if l == 0 else 1
            pre = "l%d_" % l
            S.phase = "s5norm"
            norm_to_hn(ci, l)
            pieces = chunk_pieces(ci)
            def zgate():
                S.phase = "s5zgate"
                for cs_ in range(2):
                    wv, wr = load_w("pool", D[pre + "w_in"][:, 1024 + cs_ * 512:1024 + (cs_ + 1) * 512].rearrange("(k p) c -> p k c", p=128), [8, 512])
                    for cbk in range(4):
                        cblk = cs_ * 4 + cbk
                        for (t0, tn) in chunk_tiles(ci):
                            pb_, pr_ = bank()
                            for b in range(8):
                                mm(pb_[:, 0:tn], wv[:, b, cbk * 128:(cbk + 1) * 128], hn[:, b, t0:t0 + tn], b == 0, b == 7, [wr, Rhn], [pr_])
                            op("act", "activation", [pr_], [Ry3[cblk]], out=y3[:, cblk, t0:t0 + tn], in_=pb_[:, 0:tn], func=AF.Silu)
            if KSTOP == "s5gate": S.frozen = True
            ycols = []; ncol = 0
            for (n0_, nn_) in pieces:
                ycols.append(ncol); ncol += nn_

            def piece_stage(pi_, n0, nn, stage):
                ycol = ycols[pi_]
                last_piece = (pi_ == len(pieces) - 1)
                small = nn < 128
                Uc = UA if small else Uv
                RUc = [RUA] * 16 if small else RU
                Sbc = SbA if small else Sb
                RSbc = RSbA if small else RSb
                if stage == 1:
                    S.phase = "s5u"
                    for half in range(2):
                        wv, wr = load_w("pool", D[pre + "w_in"][:, half * 512:(half + 1) * 512].rearrange("(k p) c -> p k c", p=128), [8, 512])
                        for tau in range(8):
                            pb_, pr_ = bank()
                            for b in range(8):
                                lhs = hn[:, b, 8 * n0:8 * (n0 + nn)].rearrange("p (n t) -> p n t", t=8)[:, :, tau]
                                mm(pb_[0:nn, :], lhs, wv[:, b, :], b == 0, b == 7, [wr, Rhn], [pr_])
                            op("act" if tau % 2 == 0 else "dve", "copy" if tau % 2 == 0 else "tensor_copy", [pr_], [Rutm[tau]],
                               out=u_tmu[0:nn, half * 32:(half + 1) * 32, tau, :], in_=pb_[0:nn, :].rearrange("p (g i) -> p g i", i=16))
                    if not small:
                        S.realias(Ry3[8:16], RU)
                    S.phase = "s5Utr"
                    for gq in range(16):
                        pb_, pr_ = bank()
                        pbb = pb_[:].bitcast(BF16)
                        for gl in range(4):
                            g = 4 * gq + gl
                            tr(pbb[:, gl * 128:gl * 128 + nn], u_tmu[0:nn, g].rearrange("p t i -> p (t i)"), identb[0:nn, 0:nn], Rutm + [Rc], [pr_])
                        op("dve" if gq % 2 == 0 else "act", "tensor_copy" if gq % 2 == 0 else "copy", [pr_], [RUc[gq]],
                           out=Uc[:, 4 * gq:4 * gq + 4, 0:nn], in_=pbb[:, 0:512].rearrange("p (g n) -> p g n", n=128)[:, :, 0:nn])
                    if KSTOP == "s5U" and last_piece: S.frozen = True
                    S.phase = "s5z"
                    for hf in range(2):
                        wv, wr = load_w("sp", scr[("B", l)][:, hf * 4096:(hf + 1) * 4096].rearrange("p (a b) -> p a b", b=128), [32, 128], reads=[Rscr[("B", l)]])
                        if nn > 1:
                            wv2, wr2 = load_w("sp", scr[("B2", l)][:, hf * 4096:(hf + 1) * 4096].rearrange("p (a b) -> p a b", b=128), [32, 128], reads=[Rscr[("B2", l)]])
                        for g2p in range(8):
                            pb_, pr_ = bank()
                            for a in range(2):
                                g2 = hf * 16 + g2p * 2 + a
                                for par in range(2):
                                    g = 2 * g2 + par; gl = g - hf * 32
                                    rows = slice(par * 64, par * 64 + 64)
                                    for ri in range(2):
                                        c_ = (a * 2 + ri) * 128
                                        mm(pb_[rows, c_:c_ + nn], wv[:, gl, ri * 64:(ri + 1) * 64], Uc[:, g, 0:nn], True, nn == 1, [wr, RUc[g // 4]], [pr_])
                                        if nn > 1:
                                            mm(pb_[rows, c_ + 1:c_ + nn], wv2[:, gl, ri * 64:(ri + 1) * 64], Uc[:, g, 0:nn - 1], False, True, [wr2, RUc[g // 4]], [pr_])
                            g2a = hf * 16 + g2p * 2
                            op("act", "copy", [pr_], RSh, out=Sst[:, g2a:g2a + 2, :, ycol + 1:ycol + 1 + nn],
                               in_=pb_[:, 0:512].rearrange("p (a r n) -> p a r n", a=2, r=2)[:, :, :, 0:nn])
                    if KSTOP == "s5z" and last_piece: S.frozen = True
                if stage == 2:
                    S.phase = "s5scan"
                    def chain(c, pcol, qcol, Trr, Tis, rd, wr_):
                        t1c = scan_t1 if c == 0 else scan_t1b; t2c = scan_t2 if c == 0 else scan_t2b
                        r1 = Rscan1[c]; r2 = Rscan2[c]
                        return [
                            lambda: op("dve", "tensor_tensor", rd + [Rtab[li]], [r2], out=t2c[:, :, 0], in0=Sst[:, :, 1, pcol], in1=Tis[:, li, :, 0], op=ALU.mult),
                            lambda: op("dve", "tensor_tensor", rd + [Rtab[li]], [r2], out=t2c[:, :, 1], in0=Sst[:, :, 0, pcol], in1=Tis[:, li, :, 1], op=ALU.mult),
                            lambda: op("dve", "tensor_tensor", rd + [Rtab[li]], [r1], out=t1c, in0=Sst[:, :, :, pcol], in1=Trr[:, li], op=ALU.mult),
                            lambda: op("dve", "tensor_tensor", [r2, wr_], [wr_], out=Sst[:, :, :, qcol], in0=Sst[:, :, :, qcol], in1=t2c, op=ALU.add),
                            lambda: op("dve", "tensor_tensor", [r1, wr_], [wr_], out=Sst[:, :, :, qcol], in0=Sst[:, :, :, qcol], in1=t1c, op=ALU.add),
                        ]
                    for f_ in chain(0, ycol, ycol + 1, ArAr, AiS, RSh, RSh[0]): f_()
                    n = 1
                    while n < nn:
                        if n + 1 < nn:
                            rdO = RSh if n == 1 else [RSh[1]]
                            rdE = RSh if n == 1 else [RSh[0]]
                            cO = chain(1, ycol + n - 1, ycol + n + 1, A2rr, A2is, rdO, RSh[1])
                            cE = chain(0, ycol + n, ycol + n + 2, A2rr, A2is, rdE, RSh[0])
                            for fo, fe in zip(cO, cE):
                                fo(); fe()
                            n += 2
                        else:
                            for f_ in chain(1, ycol + n - 1, ycol + n + 1, A2rr, A2is, RSh, RSh[1]): f_()
                            n += 1
                    if KSTOP == "s5scan" and last_piece: S.frozen = True
                if stage == 3:
                    S.phase = "s5Y"
                    if not small:
                        S.realias(Rstage, [RSb])
                    if small:
                        op("act", "copy", RSh, [RSbc], out=Sbc[:, :, :, 0:nn], in_=Sst[:, :, :, ycol:ycol + nn])
                    for hf in range(2):
                        if not small:
                            op("act", "copy", RSh, [RSbc], out=Sbc[:, :, :, 0:nn], in_=Sst[:, hf * 16:hf * 16 + 16, :, ycol:ycol + nn])
                        sbo = 0 if small else hf * 16
                        wT, rT = load_w("sp", scr[("T", l)][:, hf * 4096:(hf + 1) * 4096].rearrange("p (a b) -> p a b", b=128), [32, 128], reads=[Rscr[("T", l)]])
                        wC, rC = load_w_parts("sp", [16, 2, 128], lambda v, i, hf=hf: (v.rearrange("p a r b -> p (a r b)"), scr[("C", l)][:, hf * 4096:(hf + 1) * 4096]), 1, reads=[Rscr[("C", l)]])
                        for gq in range(8):
                            pb_, pr_ = bank()
                            for gl4 in range(4):
                                gl = gq * 4 + gl4; g = hf * 32 + gl; g2 = g // 2; par = g % 2
                                rows = slice(par * 64, par * 64 + 64)
                                o_ = pb_[0:nn, gl4 * 128:(gl4 + 1) * 128]
                                mm(o_, Uc[:, g, 0:nn], wT[:, gl, :], True, False, [RUc[g // 4], rT], [pr_])
                                mm(o_, Sbc[rows, g2 - sbo, 0, 0:nn], wC[rows, g2 - hf * 16, 0, :], False, False, [RSbc, rC], [pr_])
                                mm(o_, Sbc[rows, g2 - sbo, 1, 0:nn], wC[rows, g2 - hf * 16, 1, :], False, True, [RSbc, rC], [pr_])
                            gq_abs = hf * 8 + gq
                            op("act", "activation", [pr_], Rutm, out=u_tm[0:nn, :, 64 * gq_abs:64 * gq_abs + 64].rearrange("p j (g o) -> p g j o", g=4),
                               in_=pb_[0:nn, 0:512].rearrange("p (g j o) -> p g j o", g=4, j=8), func=AF.Gelu_apprx_tanh)
                    if not small:
                        S.realias([RSb], Rstage)
                    if KSTOP == "s5Y" and last_piece: S.frozen = True
                    S.phase = "s5ytr"
                    for cbk in range(8):
                        pb_, pr_ = bank()
                        pbb = pb_[:].bitcast(BF16)
                        for j in range(8):
                            tr(pbb[:, j * 128:j * 128 + nn], u_tm[0:nn, j, cbk * 128:(cbk + 1) * 128], identb[0:nn, 0:nn], [Rutm[j], Rc], [pr_])
                        ydst = yfmA if small else yfm
                        op("dve" if cbk % 2 == 0 else "act", "tensor_copy" if cbk % 2 == 0 else "copy", [pr_], [RyA if small else Ry3[8 + cbk]],
                           out=ydst[:, cbk, 8 * n0:8 * (n0 + nn)].rearrange("p (n j) -> p j n", j=8),
                           in_=pbb[:, 0:1024].rearrange("p (j n) -> p j n", n=128)[:, :, 0:nn])
                    if not small:
                        S.realias(RU, Ry3[8:16])

            for pi_, (n0, nn) in enumerate(pieces): piece_stage(pi_, n0, nn, 1)
            zgate()
            for pi_, (n0, nn) in enumerate(pieces): piece_stage(pi_, n0, nn, 2)
            for pi_, (n0, nn) in enumerate(pieces): piece_stage(pi_, n0, nn, 3)

            if KSTOP == "s5yfm": S.frozen = True
            op("dve", "tensor_copy", RSh, [Rcarry[li]], out=carry[:, li], in_=Sst[:, :, :, ncol])
            if len(pieces) > 1:
                op("dve", "tensor_copy", [RyA], Ry3[8:16], out=yfm[:, :, 0:16], in_=yfmA[:, :, 0:16])
            S.phase = "s5glu"
            for c_ in range(8):
                op("dve", "tensor_tensor", [Ry3[8 + c_], Ry3[c_]], [Ry3[c_]], out=y3[:, c_, 0:T], in0=yfm[:, c_, 0:T], in1=y3[:, c_, 0:T], op=ALU.mult)
            for cs_ in range(2):
                wv, wr = load_w("pool", D[pre + "w_glu"][:, cs_ * 512:(cs_ + 1) * 512].rearrange("(k p) c -> p k c", p=128), [8, 512])
                for cbk in range(4):
                    cblk = cs_ * 4 + cbk
                    for (t0, tn) in chunk_tiles(ci):
                        pb_, pr_ = bank()
                        for k in range(8):
                            mm(pb_[:, 0:tn], wv[:, k, cbk * 128:(cbk + 1) * 128], yfm[:, k, t0:t0 + tn], k == 0, k == 7, [wr, Ry3[8 + k]], [pr_])
                        jj = cblk % 2
                        op("act", "activation", [pr_, Rc], [Rsig[jj]], out=sig[jj][:, 0:tn], in_=pb_[:, 0:tn], func=AF.Sigmoid, bias=bglu[:, li, cblk:cblk + 1])
                        op("dve", "tensor_tensor", [Rsig[jj], Ry3[cblk]], [Ry3[cblk]], out=y3[:, cblk, t0:t0 + tn], in0=sig[jj][:, 0:tn], in1=y3[:, cblk, t0:t0 + tn], op=ALU.mult)
            S.phase = "s5out"
            out_proj(ci, pre + "w_out", 8)

        first_s5 = {0: True, 3: True}
        for ci in range(nchunks):
            cur_ci[0] = ci
            if ci == 1:
                S.realias(Rh_all, Rh_all)
            load_chunk(ci)
            if KSTOP == "load": S.frozen = True
            for l in layers:
                if l in (0, 3):
                    li = 0 if l == 0 else 1
                    use_work("s5")
                    if first_s5[l]:
                        op("dve", "memset", [], RSh, Sst[:, :, :, 0], 0.0)
                        first_s5[l] = False
                    else:
                        op("dve", "tensor_copy", [Rcarry[li]], RSh, out=Sst[:, :, :, 0], in_=carry[:, li])
                    layer_s5_full(ci, l)
                elif l == 1:
                    layer_conv(ci)
                else:
                    layer_pool(ci)
            final_store(ci)
        if _os.environ.get("KDUMP", ""):
            S.frozen = False
            regs = [(off_hn, 16640), (off_y3, 33280), (work0, WORK_BYTES), (off_stage, 8192)]
            if _os.environ["KDUMP"] != "1":
                regs = [tuple(int(v_) for v_ in t_.split(":")) for t_ in _os.environ["KDUMP"].split(",")]
            allres = Rh_all + [Rhn] + Ry3 + Rring + s5work + convwork + poolwork + Rstage + RU + [RSb] + RSh + Rutm
            ov = out_d.rearrange("(p r) c -> p (r c)", p=128)
            pos = 0
            Rdump = Res("dump")
            for (o_, n_) in regs:
                dma("sp", ov[:, pos // 4:(pos + n_) // 4], arena[:, o_ // 4:(o_ + n_) // 4], allres, [Rdump], semres=Res("dumpsem%d" % pos))
                pos += n_
        S.emit()
    return nc


_CACHE = {}


def kernel(**inputs):
    x = np.ascontiguousarray(inputs["x"], dtype=np.float32)
    B = x.shape[0]
    if "nc" not in _CACHE:
        _CACHE["nc"] = build_program()
    nc = _CACHE["nc"]
    consts = host_consts()
    shared = {n: np.ascontiguousarray(inputs[n], dtype=np.float32) for n in PARAM_NAMES}
    shared.update(consts)
    in_maps = []
    for b in range(B):
        m = dict(shared)
        m["x"] = x[b]
        in_maps.append(m)
    res = run_bass_kernel_spmd(nc, in_maps, core_ids=list(range(B)))
    out = np.stack([np.asarray(r["out"], dtype=np.float32) for r in res.results], axis=0)
    return out
```

```python
import math
from contextlib import ExitStack
import numpy as np
import concourse.bass as bass
import concourse.mybir as mybir
from concourse.bass_utils import run_bass_kernel_spmd

F32 = mybir.dt.float32
BF16 = mybir.dt.bfloat16
I32 = mybir.dt.int32
AF = mybir.ActivationFunctionType
ALU = mybir.AluOpType
P = 128
NMETA = 16
SEQ = 4096
DM = 1024
EPS = 1e-6
PI = math.pi


class Res:
    __slots__ = ("name", "w", "rs", "sem", "ndma", "grp", "multi", "excl")

    def __init__(self, name, grp=None, excl=False):
        self.name = name; self.w = None; self.rs = {}; self.sem = None; self.ndma = 0; self.grp = grp; self.multi = None
        self.excl = excl


class SemGroup:
    def __init__(self, name):
        self.name = name; self.sem = None; self.total = 0


class Op:
    __slots__ = ("eng", "fn", "deps", "dma", "semres", "needs_inc", "ev", "phase")

    def __init__(self, eng, fn, dma):
        self.eng = eng; self.fn = fn; self.deps = []; self.dma = dma; self.semres = None
        self.needs_inc = False; self.ev = None


class Sched:
    ENG = ("pe", "act", "dve", "pool", "sp")

    def __init__(self, nc, stack):
        self.nc = nc; self.stack = stack
        self.ops = {e: [] for e in self.ENG}
        self.all_dma = []
        self.nsem = 0

    def new_sem(self, name):
        self.nsem += 1
        return self.stack.enter_context(self.nc.semaphore(name))

    frozen = False
    phase = ""

    def add(self, eng, fn, reads=(), writes=(), dma=False, semres=None, part_of=None):
        if self.frozen:
            return None
        op = Op(eng, fn, dma)
        op.phase = self.phase
        deps = []
        rr = []
        for r in reads:
            if r.multi is not None: rr.extend(r.multi)
            else: rr.append(r)
        reads = rr
        for r in reads:
            if r.w is not None: deps.append(r.w)
            if r.excl:
                for k_, v_ in r.rs.items():
                    if k_ != eng: deps.append(v_)
        for w in writes:
            if w.w is not None: deps.append(w.w)
            deps.extend(w.rs.values())
        seen = set(); out = []
        for d in deps:
            if id(d) in seen or d is op or d is part_of: continue
            seen.add(id(d))
            if not d.dma and not dma and d.eng == eng:
                if eng == "pe" or not self.ops[eng] or self.ops[eng][-1] is not d:
                    continue
                self.n_adj = getattr(self, "n_adj", 0) + 1
            out.append(d); d.needs_inc = True
        op.deps = out
        for r in reads: r.rs[eng if not dma else ("dma", id(op))] = op
        for w in writes: w.w = op; w.rs = {}
        if dma:
            op.semres = semres if semres is not None else writes[0]
            self.all_dma.append(op)
            op.needs_inc = True
        self.ops[eng].append(op)
        return op

    def realias(self, old, new):
        users = []
        for o in old:
            if o.w is not None: users.append(o.w)
            users.extend(list(o.rs.values()))
        for n in new:
            for v in users: n.rs[("r", id(v))] = v

    def emit(self):
        nc = self.nc
        MAXV = 30000
        nes = 0
        for op in self.all_dma:
            r = op.semres
            if r.grp is not None: r.grp.total += 1
        for e in self.ENG:
            cur = None; cnt = 0
            for op in self.ops[e]:
                if op.dma:
                    r = op.semres
                    if r.grp is not None:
                        g = r.grp
                        if g.sem is None: g.sem = self.new_sem("g_" + g.name)
                        op.ev = (g.sem, 16 * g.total)
                    else:
                        if r.sem is None: r.sem = {}
                        if e not in r.sem: r.sem[e] = [self.new_sem("d_%s_%s" % (r.name, e)), 0]
                        r.sem[e][1] += 1
                        op.ev = (r.sem[e][0], 16 * r.sem[e][1])
                elif op.needs_inc:
                    if cur is None or cnt >= MAXV:
                        cur = self.new_sem("e_%s_%d" % (e, nes)); nes += 1; cnt = 0
                    cnt += 1
                    op.ev = (cur, cnt)
        last = {}
        for op in self.all_dma: last[op.ev[0].name] = op.ev
        final_waits = list(last.values())
        engobj = {"pe": "tensor", "act": "scalar", "dve": "vector", "pool": "gpsimd", "sp": "sync"}
        sched = self
        with nc.Block() as block:
            def mk(ename):
                def body(eng):
                    known = {}
                    for op in sched.ops[ename]:
                        for d in op.deps:
                            sem, val = d.ev
                            if known.get(sem.name, 0) >= val: continue
                            known[sem.name] = val
                            eng.wait_ge(sem, val)
                        ins = op.fn(eng)
                        if _DBG_TAGS is not None:
                            try: _DBG_TAGS[ins.ins.name] = (ename, op.phase)
                            except Exception: pass
                        if op.ev is not None:
                            ins.then_inc(op.ev[0], 16 if op.dma else 1)
                    if ename == "sp":
                        for sem, val in final_waits:
                            eng.wait_ge(sem, val)
                return body
            for ename in self.ENG:
                getattr(block, engobj[ename])(mk(ename))


_DBG_TAGS = None
_DBG_OFFS = None
CHUNKS = [(0, 1040), (1040, 1024), (2064, 1024), (3088, 1024)]
TMAX = 1040
PARAM_NAMES = [
    "meta_tokens", "norm0_g", "l0_w_in", "l0_lam_re", "l0_lam_im", "l0_log_dt", "l0_b_re", "l0_b_im",
    "l0_c_re", "l0_c_im", "l0_d_skip", "l0_w_glu", "l0_b_glu", "l0_w_out",
    "norm1_g", "l1_w_in", "l1_conv_w", "l1_conv_b", "l1_w_out",
    "norm2_g", "l2_w_in", "l2_w_grp", "l2_b_grp", "l2_scale", "l2_w_out",
    "norm3_g", "l3_w_in", "l3_lam_re", "l3_lam_im", "l3_log_dt", "l3_b_re", "l3_b_im",
    "l3_c_re", "l3_c_im", "l3_d_skip", "l3_w_glu", "l3_b_glu", "l3_w_out", "final_g"]
PARAM_SHAPES = {
    "meta_tokens": [16, 1024], "norm0_g": [1024], "l0_w_in": [1024, 2048], "l0_lam_re": [64, 64], "l0_lam_im": [64, 64],
    "l0_log_dt": [64], "l0_b_re": [64, 64, 16], "l0_b_im": [64, 64, 16], "l0_c_re": [64, 16, 64], "l0_c_im": [64, 16, 64],
    "l0_d_skip": [1024], "l0_w_glu": [1024, 1024], "l0_b_glu": [1024], "l0_w_out": [1024, 1024],
    "norm1_g": [1024], "l1_w_in": [1024, 8192], "l1_conv_w": [3, 2048], "l1_conv_b": [2048], "l1_w_out": [2048, 1024],
    "norm2_g": [1024], "l2_w_in": [1024, 4096], "l2_w_grp": [4, 512, 512], "l2_b_grp": [4, 512], "l2_scale": [2048],
    "l2_w_out": [2048, 1024], "norm3_g": [1024], "l3_w_in": [1024, 2048], "l3_lam_re": [64, 64], "l3_lam_im": [64, 64],
    "l3_log_dt": [64], "l3_b_re": [64, 64, 16], "l3_b_im": [64, 64, 16], "l3_c_re": [64, 16, 64], "l3_c_im": [64, 16, 64],
    "l3_d_skip": [1024], "l3_w_glu": [1024, 1024], "l3_b_glu": [1024], "l3_w_out": [1024, 1024], "final_g": [1024]}


def host_consts():
    c = {}
    c["c_ident"] = np.eye(128, dtype=np.float32)
    idx = np.arange(128)
    c["c_mask"] = (idx[:, None] // 16 <= idx[None, :] // 16).astype(np.float32)
    selC = np.zeros((128, 2, 64), np.float32)
    for gl in range(8):
        for o in range(16):
            selC[gl * 16 + o, gl % 2, (gl // 2) * 16 + o] = 1.0
    c["c_selC"] = selC
    selG = np.zeros((64, 2, 32), np.float32)
    for g in range(64):
        selG[g, g % 2, g // 2] = 1.0
    c["c_selG"] = selG
    c["c_invc"] = np.tile((1.0 / np.arange(1, 17, dtype=np.float32))[None, :], (128, 1)).astype(np.float32)
    c["c_ones"] = np.ones((128, 128), np.float32)
    return c


CONST_SHAPES = {"c_ident": [128, 128], "c_mask": [128, 128], "c_selC": [128, 2, 64], "c_selG": [64, 2, 32],
                "c_invc": [128, 16], "c_ones": [128, 128]}


def chunk_tiles(ci):
    return [(0, 16), (16, 512), (528, 512)] if ci == 0 else [(0, 512), (512, 512)]


def chunk_pieces(ci):
    return [(0, 2), (2, 128)] if ci == 0 else [(0, 128)]


def build_program(layers=(0, 1, 2, 3), nchunks=4):
    nc = bass.Bass("TRN2", target_bir_lowering=False)
    D = {}
    D["x"] = nc.dram_tensor("x", [SEQ, DM], F32, kind="ExternalInput").ap()
    for n in PARAM_NAMES:
        D[n] = nc.dram_tensor(n, PARAM_SHAPES[n], F32, kind="ExternalInput").ap()
    for n, s in CONST_SHAPES.items():
        D[n] = nc.dram_tensor(n, s, F32, kind="ExternalInput").ap()
    out_d = nc.dram_tensor("out", [SEQ, DM], F32, kind="ExternalOutput").ap()
    scr = {}
    for l in [l_ for l_ in (0, 3) if l_ in layers]:
        scr[("T", l)] = nc.dram_tensor("scrT%d" % l, [128, 64 * 128], BF16, kind="Internal").ap()
        scr[("B", l)] = nc.dram_tensor("scrB%d" % l, [128, 64 * 128], BF16, kind="Internal").ap()
        scr[("C", l)] = nc.dram_tensor("scrC%d" % l, [128, 64 * 128], BF16, kind="Internal").ap()
        scr[("B2", l)] = nc.dram_tensor("scrB2%d" % l, [128, 64 * 128], BF16, kind="Internal").ap()

    with ExitStack() as st:
        S = Sched(nc, st)
        import os as _os
        ARENA_BYTES = int(_os.environ.get('KARENA', '207872'))
        arena = st.enter_context(nc.sbuf_tensor("arena", [128, ARENA_BYTES // 4], F32))
        mem_top = [0]

        def view_at(off, shape, dt):
            n = 1
            for s_ in shape: n *= s_
            nb = n * (2 if dt == BF16 else 4)
            assert off % 4 == 0 and nb % 4 == 0 and off + nb <= ARENA_BYTES, (off, nb)
            ap = arena[:, off // 4:(off + nb) // 4]
            if dt != F32: ap = ap.bitcast(dt)
            if len(shape) == 2:
                ap = ap.rearrange("p (a b) -> p a b", b=shape[1])
            elif len(shape) == 3:
                ap = ap.rearrange("p (a b c) -> p a b c", b=shape[1], c=shape[2])
            elif len(shape) == 4:
                ap = ap.rearrange("p (a b c d) -> p a b c d", b=shape[1], c=shape[2], d=shape[3])
            return ap

        def alloc(shape, dt):
            n = 1
            for s_ in shape: n *= s_
            nb = n * (2 if dt == BF16 else 4)
            nb = (nb + 31) // 32 * 32
            off = mem_top[0]; mem_top[0] += nb
            return view_at(off, shape, dt), off

        kint_t = st.enter_context(nc.sbuf_tensor("kint_t", [128, 32], I32))
        psum = [st.enter_context(nc.psum_tensor("ps%d" % i, [128, 512], F32)) for i in range(8)]
        psres = [Res("ps%d" % i, excl=True) for i in range(8)]
        pidx = [0]

        def bank():
            i = pidx[0]; pidx[0] = (i + 1) % 8
            return psum[i], psres[i]

        def op(eng, method, reads, writes, *args, **kw):
            return S.add(eng, lambda e: getattr(e, method)(*args, **kw), reads, writes)

        import os
        SKIP = os.environ.get("KSKIP", "")

        def dma(eng, out, in_, reads, writes, semres=None, slow=False, part_of=None):
            if slow and SKIP == "slow":
                return None
            if len(writes) == 1 and writes[0].multi is not None:
                nr_ = Res("c%d" % len(writes[0].multi)); writes[0].multi.append(nr_); writes = [nr_]
            if slow:
                return S.add(eng, lambda e: e.dma_start(out=out, in_=in_, allow_slow_non_contiguous=True), reads, writes, dma=True, semres=semres, part_of=part_of)
            return S.add(eng, lambda e: e.dma_start(out=out, in_=in_), reads, writes, dma=True, semres=semres, part_of=part_of)

        def mm(out, lhsT, rhs, start, stop, reads, writes):
            return S.add("pe", lambda e: e.matmul(out, lhsT, rhs, start=start, stop=stop), reads, writes)

        def tr(out, in_, ident, reads, writes):
            return S.add("pe", lambda e: e.transpose(out, in_, ident), reads, writes)

        h, _ = alloc([8, TMAX], F32); Rhh = [[Res("h%d_%d" % (b, t)) for t in range(3)] for b in range(8)]
        Rh_all = [r for rr_ in Rhh for r in rr_]
        cur_ci = [0]

        def rh(b, t0):
            for ti_, (a0, an) in enumerate(chunk_tiles(cur_ci[0])):
                if a0 <= t0 < a0 + an: return Rhh[b][ti_]
            raise AssertionError(t0)
        hn, off_hn = alloc([8, TMAX], BF16); Rhn = Res("hn")
        y3, off_y3 = alloc([16, TMAX], BF16); Ry3 = [Res("y3_%d" % b) for b in range(16)]
        NSLOT = int(_os.environ.get('KNSLOT', '5'))
        ring = []; Rring = []
        for i in range(NSLOT):
            v, o_ = alloc([4096], BF16); ring.append((v, o_)); Rring.append(Res("ring%d" % i))
        ridx = [0]
        WORK_BYTES = 49920
        work0 = mem_top[0]; mem_top[0] += WORK_BYTES
        stage = []; Rstage = []
        off_stage = mem_top[0]
        for i in range(2):
            v, _ = alloc([1024], F32); stage.append(v); Rstage.append(Res("stage%d" % i))
        sq = []; Rsq = []
        for i in range(2):
            v, _ = alloc([512], BF16); sq.append(v); Rsq.append(Res("sq%d" % i))
        rsb = []; Rrs = []
        for i in range(2):
            v, _ = alloc([512], F32); rsb.append(v); Rrs.append(Res("rs%d" % i))
        pg = SemGroup("params")
        identf, _ = alloc([128], F32); identb, _ = alloc([128], BF16); onesb, _ = alloc([128], BF16)
        maskf, _ = alloc([128], F32); invc, _ = alloc([16], F32)
        Rc = Res("consts"); Rc.multi = []
        gains, _ = alloc([5, 8], F32)
        bglu, _ = alloc([2, 8], F32)
        cw, _ = alloc([3, 16], F32); cb, _ = alloc([16], F32)
        pscale, _ = alloc([16], F32); pbg, _ = alloc([16], F32); pbs, _ = alloc([16], F32)
        Dg, _ = alloc([2, 64], F32)
        ArAr, _ = alloc([2, 32, 2], F32); AiS, _ = alloc([2, 32, 2], F32)
        A2rr, _ = alloc([2, 32, 2], F32); A2is, _ = alloc([2, 32, 2], F32)
        Rtab = [Res("tab0"), Res("tab1")]
        chist, _ = alloc([16, 2], F32); Rchist = Res("chist")
        phist, _ = alloc([16, 16], F32); Rphist = Res("phist")
        Scarry = None
        assert mem_top[0] <= ARENA_BYTES, mem_top[0]

        def wslot(nelem_shape):
            i = ridx[0]; ridx[0] = (i + 1) % NSLOT
            v, o_ = ring[i]
            return view_at(o_, nelem_shape, BF16), Rring[i]

        if _os.environ.get("KDUMP", ""):
            Rinit = Res("init")
            for i_ in range(0, ARENA_BYTES // 4, 8192):
                op("dve", "memset", [], [Rinit], arena[:, i_:min(i_ + 8192, ARENA_BYTES // 4)], 0.0)
            op("act", "copy", [Rinit], [Rinit], out=arena[:, 0:8], in_=arena[:, 0:8])
            op("pool", "tensor_copy", [Rinit], [Rinit], out=arena[:, 0:8], in_=arena[:, 0:8])
            S.add("pe", lambda e: e.matmul(psum[0][:, 0:8], arena[:, 0:128], arena[:, 0:8], start=True, stop=True), [Rinit], [psres[0]])
            S.add("sp", lambda e: e.dma_start(out=arena[:, 0:8], in_=arena[:, 8:16]), [Rinit], [Rinit], dma=True)
        dma("sp", identf, D["c_ident"], [], [Rc])
        dma("pool", identb, D["c_ident"], [], [Rc])
        dma("pool", onesb, D["c_ones"], [], [Rc])
        dma("sp", maskf, D["c_mask"], [], [Rc])
        dma("sp", invc, D["c_invc"], [], [Rc])
        for i, nm in enumerate(["norm0_g", "norm1_g", "norm2_g", "norm3_g", "final_g"]):
            dma("sp", gains[:, i, :], D[nm].rearrange("(b p) -> p b", p=128), [], [Rc], slow=True)
        for i, nm in enumerate(["l0_b_glu", "l3_b_glu"]):
            dma("sp", bglu[:, i, :], D[nm].rearrange("(b p) -> p b", p=128), [], [Rc], slow=True)
        for k in range(3):
            dma("sp", cw[:, k, :], D["l1_conv_w"][k].rearrange("(b p) -> p b", p=128), [], [Rc], slow=True)
        dma("sp", cb, D["l1_conv_b"].rearrange("(b p) -> p b", p=128), [], [Rc], slow=True)
        dma("sp", pscale, D["l2_scale"].rearrange("(b p) -> p b", p=128), [], [Rc], slow=True)
        for k_ in range(4):
            dma("sp", pbg[:, 4 * k_:4 * k_ + 4], D["l2_b_grp"][k_].rearrange("(b p) -> p b", p=128), [], [Rc], slow=True)
        for li, nm in enumerate(["l0_d_skip", "l3_d_skip"]):
            for tau in range(8):
                dma("sp", Dg[tau * 16:(tau + 1) * 16, li, :], D[nm].rearrange("(g i) -> i g", i=16), [], [Rc], slow=True)
        Rpbs = Res("pbs")
        op("dve", "tensor_tensor", [Rc], [Rpbs], out=pbs, in0=pbg, in1=pscale, op=ALU.mult)
        op("dve", "memset", [], [Rphist], phist, 0.0)
        op("dve", "memset", [], [Rchist], chist, 0.0)

        KSTOP = _os.environ.get("KSTOP", "")
        if KSTOP == "consts": S.frozen = True
        s5layers = [l for l in layers if l in (0, 3)]
        Rscr = {k: Res("scr%s%d" % k) for k in scr}
        if SKIP == "arena":
            pass

        PRO_RES = {}

        def s5_prologue(l):
            S.phase = "pro%d" % l
            li = 0 if l == 0 else 1
            pre = "l%d_" % l
            base = [0]

            def A(shape, dt=F32):
                n = 1
                for s_ in shape: n *= s_
                nb = (n * (2 if dt == BF16 else 4) + 31) // 32 * 32
                off = base[0]; base[0] += nb
                assert base[0] <= work0 + WORK_BYTES
                return view_at(off, shape, dt)
            Rp = PRO_RES

            def R(n):
                if n not in Rp: Rp[n] = Res("p_%s" % n)
                return Rp[n]
            lamR = A([64]); lamI = A([64]); ldt = A([1]); ldtb = A([64]); selG = A([2, 32]); selC = A([2, 64])
            dma("sp", lamR[0:64], D[pre + "lam_re"], [], [R("lamR")])
            dma("sp", lamI[0:64], D[pre + "lam_im"], [], [R("lamI")])
            dma("sp", ldt[0:64], D[pre + "log_dt"].rearrange("(g o) -> g o", o=1), [], [R("ldt")])
            dma("sp", selG[0:64], D["c_selG"], [], [R("selG")])
            dma("sp", selC, D["c_selC"], [], [R("selC")])
            op("dve", "tensor_copy", [R("ldt")], [R("ldtb")], out=ldtb[0:64], in_=ldt[0:64, 0:1].to_broadcast([64, 64]))
            pb_, pr_ = bank()
            for par in range(2):
                rows = slice(par * 64, par * 64 + 64)
                mm(pb_[rows, 0:32], lamR[0:64], selG[0:64, par, :], True, True, [R("lamR"), R("selG")], [pr_])
                mm(pb_[rows, 32:64], lamI[0:64], selG[0:64, par, :], True, True, [R("lamI"), R("selG")], [pr_])
                mm(pb_[rows, 64:96], ldtb[0:64], selG[0:64, par, :], True, True, [R("ldtb"), R("selG")], [pr_])
            sm = A([24, 32])
            Rsm = R("sm")
            lr, li_, ld, dt, x1, mag, ang, v_, kf, r_, m_, sn, cs, ar, ai, am1, den, t_, kr, ki = [sm[:, i, :] for i in range(20)]
            kint = kint_t[:, :]; _ = A([32], I32)
            op("act", "copy", [pr_], [Rsm], out=sm[:, 0:3, :], in_=pb_[:, 0:96].rearrange("p (a b) -> p a b", b=32))

            def dv(method, *a, **k):
                return op("dve", method, [Rsm], [Rsm], *a, **k)

            def ac(*a, **k):
                return op("act", "activation", [Rsm], [Rsm], *a, **k)
            ac(out=dt, in_=ld, func=AF.Exp)
            dv("tensor_tensor", out=x1, in0=lr, in1=dt, op=ALU.mult)
            ac(out=mag, in_=x1, func=AF.Exp)
            dv("tensor_tensor", out=ang, in0=li_, in1=dt, op=ALU.mult)
            ac(out=sn, in_=ang, func=AF.Sin, scale=1.0 / 8)
            ac(out=v_, in_=ang, func=AF.Sin, scale=1.0 / 16)
            dv("tensor_tensor", out=v_, in0=v_, in1=v_, op=ALU.mult)
            dv("tensor_scalar", out=cs, in0=v_, scalar1=-2.0, scalar2=1.0, op0=ALU.mult, op1=ALU.add)
            for _d in range(3):
                dv("tensor_tensor", out=kf, in0=cs, in1=cs, op=ALU.mult)
                dv("tensor_tensor", out=r_, in0=sn, in1=sn, op=ALU.mult)
                dv("scalar_tensor_tensor", out=sn, in0=cs, scalar=2.0, in1=sn, op0=ALU.mult, op1=ALU.mult)
                dv("tensor_tensor", out=cs, in0=kf, in1=r_, op=ALU.subtract)
            dv("tensor_tensor", out=ar, in0=mag, in1=cs, op=ALU.mult)
            dv("tensor_tensor", out=ai, in0=mag, in1=sn, op=ALU.mult)
            dv("tensor_scalar", out=am1, in0=ar, scalar1=-1.0, scalar2=None, op0=ALU.add)
            dv("tensor_tensor", out=den, in0=lr, in1=lr, op=ALU.mult)
            dv("tensor_tensor", out=t_, in0=li_, in1=li_, op=ALU.mult)
            dv("tensor_tensor", out=den, in0=den, in1=t_, op=ALU.add)
            dv("reciprocal", out=den, in_=den)
            dv("tensor_tensor", out=kr, in0=am1, in1=lr, op=ALU.mult)
            dv("tensor_tensor", out=t_, in0=ai, in1=li_, op=ALU.mult)
            dv("tensor_tensor", out=kr, in0=kr, in1=t_, op=ALU.add)
            dv("tensor_tensor", out=kr, in0=kr, in1=den, op=ALU.mult)
            dv("tensor_tensor", out=ki, in0=ai, in1=lr, op=ALU.mult)
            dv("tensor_tensor", out=t_, in0=am1, in1=li_, op=ALU.mult)
            dv("tensor_tensor", out=ki, in0=ki, in1=t_, op=ALU.subtract)
            dv("tensor_tensor", out=ki, in0=ki, in1=den, op=ALU.mult)
            EPr = A([32, 9]); EPi = A([32, 9]); ERr = A([32, 8]); ERi = A([32, 8])
            dv("memset", EPr[:, :, 0], 1.0); dv("memset", EPi[:, :, 0], 0.0)
            dv("tensor_copy", out=EPr[:, :, 1], in_=ar); dv("tensor_copy", out=EPi[:, :, 1], in_=ai)
            for q in range(2, 9):
                dv("tensor_tensor", out=EPr[:, :, q], in0=EPr[:, :, q - 1], in1=ar, op=ALU.mult)
                dv("tensor_tensor", out=t_, in0=EPi[:, :, q - 1], in1=ai, op=ALU.mult)
                dv("tensor_tensor", out=EPr[:, :, q], in0=EPr[:, :, q], in1=t_, op=ALU.subtract)
                dv("tensor_tensor", out=EPi[:, :, q], in0=EPr[:, :, q - 1], in1=ai, op=ALU.mult)
                dv("tensor_tensor", out=t_, in0=EPi[:, :, q - 1], in1=ar, op=ALU.mult)
                dv("tensor_tensor", out=EPi[:, :, q], in0=EPi[:, :, q], in1=t_, op=ALU.add)
            for tau in range(8):
                dv("tensor_copy", out=ERr[:, :, tau], in_=EPr[:, :, 7 - tau])
                dv("tensor_copy", out=ERi[:, :, tau], in_=EPi[:, :, 7 - tau])
            Ir = sm[:, 20, :]; Ii = sm[:, 21, :]; n8 = sm[:, 22, :]
            dv("tensor_tensor", out=n8, in0=EPr[:, :, 8], in1=EPr[:, :, 8], op=ALU.mult)
            dv("tensor_tensor", out=t_, in0=EPi[:, :, 8], in1=EPi[:, :, 8], op=ALU.mult)
            dv("tensor_tensor", out=n8, in0=n8, in1=t_, op=ALU.add)
            dv("reciprocal", out=n8, in_=n8)
            dv("tensor_tensor", out=Ir, in0=EPr[:, :, 8], in1=n8, op=ALU.mult)
            dv("scalar_tensor_tensor", out=Ii, in0=EPi[:, :, 8], scalar=-1.0, in1=n8, op0=ALU.mult, op1=ALU.mult)
            op("dve", "tensor_copy", [Rsm], [Rtab[li]], out=ArAr[:, li, :, 0], in_=EPr[:, :, 8])
            op("dve", "tensor_copy", [Rsm], [Rtab[li]], out=ArAr[:, li, :, 1], in_=EPr[:, :, 8])
            op("dve", "tensor_scalar", [Rsm], [Rtab[li]], out=AiS[:, li, :, 0], in0=EPi[:, :, 8], scalar1=-1.0, scalar2=None, op0=ALU.mult)
            op("dve", "tensor_copy", [Rsm], [Rtab[li]], out=AiS[:, li, :, 1], in_=EPi[:, :, 8])
            dv("tensor_tensor", out=kf, in0=EPr[:, :, 8], in1=EPr[:, :, 8], op=ALU.mult)
            dv("tensor_tensor", out=r_, in0=EPi[:, :, 8], in1=EPi[:, :, 8], op=ALU.mult)
            dv("tensor_tensor", out=kf, in0=kf, in1=r_, op=ALU.subtract)
            dv("scalar_tensor_tensor", out=r_, in0=EPr[:, :, 8], scalar=2.0, in1=EPi[:, :, 8], op0=ALU.mult, op1=ALU.mult)
            op("dve", "tensor_copy", [Rsm], [Rtab[li]], out=A2rr[:, li, :, 0], in_=kf)
            op("dve", "tensor_copy", [Rsm], [Rtab[li]], out=A2rr[:, li, :, 1], in_=kf)
            op("dve", "tensor_scalar", [Rsm], [Rtab[li]], out=A2is[:, li, :, 0], in0=r_, scalar1=-1.0, scalar2=None, op0=ALU.mult)
            op("dve", "tensor_copy", [Rsm], [Rtab[li]], out=A2is[:, li, :, 1], in_=r_)
            Bre = A([32, 16]); Bim = A([32, 16]); bbr = A([32, 16]); bbi = A([32, 16]); tb = A([32, 16])
            Cre = A([32, 16]); Cim = A([32, 16])
            for par in range(2):
                rows = slice(par * 64, par * 64 + 64)
                for nm, dst in (("b_re", Bre), ("b_im", Bim)):
                    src = D[pre + nm].rearrange("(g2 q) p i -> q p g2 i", q=2)[par]
                    dma("sp", dst[rows], src, [], [R("Bsrc")])
            Xt = A([8, 64])
            for nm, dst in (("c_re", Cre), ("c_im", Cim)):
                srcv = D[pre + nm].rearrange("(a gl) o p -> a (gl o) p", gl=8)
                for a in range(8):
                    dma("sp", Xt[:, a, :], srcv[a], [], [R("Xt%d" % a)])
                pb_, pr_ = bank()
                for a in range(8):
                    for par in range(2):
                        rows = slice(par * 64, par * 64 + 64)
                        mm(pb_[rows, a * 64:(a + 1) * 64], Xt[:, a, :], selC[:, par, :], True, True, [R("Xt%d" % a), R("selC")], [pr_])
                op("act", "copy", [pr_], [R("Csrc")], out=dst, in_=pb_[:, 0:512].rearrange("p (a b) -> p a b", b=16))
            RB = R("bb")
            krb = kr.unsqueeze(2).to_broadcast([128, 32, 16]); kib = ki.unsqueeze(2).to_broadcast([128, 32, 16])
            op("dve", "tensor_tensor", [Rsm, R("Bsrc")], [RB], out=bbr, in0=Bre, in1=krb, op=ALU.mult)
            op("dve", "tensor_tensor", [Rsm, R("Bsrc")], [RB], out=tb, in0=Bim, in1=kib, op=ALU.mult)
            op("dve", "tensor_tensor", [RB], [RB], out=bbr, in0=bbr, in1=tb, op=ALU.subtract)
            op("dve", "tensor_tensor", [Rsm, R("Bsrc")], [RB], out=bbi, in0=Bim, in1=krb, op=ALU.mult)
            op("dve", "tensor_tensor", [Rsm, R("Bsrc")], [RB], out=tb, in0=Bre, in1=kib, op=ALU.mult)
            op("dve", "tensor_tensor", [RB], [RB], out=bbi, in0=bbi, in1=tb, op=ALU.add)
            Bqr = A([16, 8, 16]); Bqi = A([16, 8, 16]); Bmr = A([16, 8, 16]); Bmi = A([16, 8, 16])
            C1r = A([16, 8, 16]); C1i = A([16, 8, 16]); t1 = A([16, 8, 16]); t2 = A([16, 8, 16])
            Tsb = A([32, 128], BF16); Bsb = A([32, 128], BF16); Csb = A([16, 2, 128], BF16)
            B2r = A([16, 8, 16]); B2i = A([16, 8, 16]); B2sb = A([32, 128], BF16)
            tmpT = [A([128]) for _ in range(4)]
            RT = [R("tmpT%d" % i_) for i_ in range(4)]
            for hf in range(2):
                gs = slice(hf * 16, hf * 16 + 16)
                sh = [128, 16, 8, 16]
                RA = R("big")
                ErB = ERr[:, gs, :].unsqueeze(3).to_broadcast(sh); EiB = ERi[:, gs, :].unsqueeze(3).to_broadcast(sh)
                brB = bbr[:, gs, :].unsqueeze(2).to_broadcast(sh); biB = bbi[:, gs, :].unsqueeze(2).to_broadcast(sh)

                def big(eng, method, *a, **k):
                    return op(eng, method, [Rsm, RB, R("Csrc"), RA], [RA], *a, **k)
                big("dve", "tensor_tensor", out=Bqr, in0=ErB, in1=brB, op=ALU.mult)
                big("dve", "tensor_tensor", out=t1, in0=EiB, in1=biB, op=ALU.mult)
                big("dve", "tensor_tensor", out=Bqr, in0=Bqr, in1=t1, op=ALU.subtract)
                big("dve", "tensor_tensor", out=Bqi, in0=ErB, in1=biB, op=ALU.mult)
                big("dve", "tensor_tensor", out=t1, in0=EiB, in1=brB, op=ALU.mult)
                big("dve", "tensor_tensor", out=Bqi, in0=Bqi, in1=t1, op=ALU.add)
                A8r = EPr[:, gs, 8].unsqueeze(2).unsqueeze(3).to_broadcast(sh); A8i = EPi[:, gs, 8].unsqueeze(2).unsqueeze(3).to_broadcast(sh)
                big("dve", "tensor_tensor", out=t1, in0=Bqr, in1=A8r, op=ALU.mult)
                big("dve", "tensor_tensor", out=t2, in0=Bqi, in1=A8i, op=ALU.mult)
                big("dve", "tensor_tensor", out=B2r, in0=t1, in1=t2, op=ALU.subtract)
                big("dve", "tensor_tensor", out=t1, in0=Bqi, in1=A8r, op=ALU.mult)
                big("dve", "tensor_tensor", out=t2, in0=Bqr, in1=A8i, op=ALU.mult)
                big("dve", "tensor_tensor", out=B2i, in0=t1, in1=t2, op=ALU.add)
                IrB = Ir[:, gs].unsqueeze(2).unsqueeze(3).to_broadcast(sh); IiB = Ii[:, gs].unsqueeze(2).unsqueeze(3).to_broadcast(sh)
                big("dve", "tensor_tensor", out=Bmr, in0=Bqr, in1=IrB, op=ALU.mult)
                big("dve", "tensor_tensor", out=t1, in0=Bqi, in1=IiB, op=ALU.mult)
                big("dve", "tensor_tensor", out=Bmr, in0=Bmr, in1=t1, op=ALU.subtract)
                big("dve", "tensor_tensor", out=Bmi, in0=Bqi, in1=IrB, op=ALU.mult)
                big("dve", "tensor_tensor", out=t1, in0=Bqr, in1=IiB, op=ALU.mult)
                big("dve", "tensor_tensor", out=Bmi, in0=Bmi, in1=t1, op=ALU.add)
                E1r = EPr[:, gs, 1:9].unsqueeze(3).to_broadcast(sh); E1i = EPi[:, gs, 1:9].unsqueeze(3).to_broadcast(sh)
                crB = Cre[:, gs, :].unsqueeze(2).to_broadcast(sh); ciB = Cim[:, gs, :].unsqueeze(2).to_broadcast(sh)
                big("dve", "tensor_tensor", out=C1r, in0=E1r, in1=crB, op=ALU.mult)
                big("dve", "tensor_tensor", out=t1, in0=E1i, in1=ciB, op=ALU.mult)
                big("dve", "tensor_tensor", out=C1r, in0=C1r, in1=t1, op=ALU.subtract)
                big("dve", "tensor_tensor", out=t1, in0=E1i, in1=crB, op=ALU.mult)
                big("dve", "tensor_tensor", out=t2, in0=E1r, in1=ciB, op=ALU.mult)
                big("dve", "scalar_tensor_tensor", out=C1i, in0=t1, scalar=-1.0, in1=t2, op0=ALU.mult, op1=ALU.subtract)
                Rout = R("outsb"); RoutT = R("outsbT")
                op("act", "copy", [RA], [Rout], out=Csb[:, :, 0, :], in_=C1r.rearrange("p a b c -> p a (b c)"))
                op("act", "copy", [RA], [Rout], out=Csb[:, :, 1, :], in_=C1i.rearrange("p a b c -> p a (b c)"))
                for gl in range(32):
                    g = hf * 32 + gl; g2l = gl // 2; par = gl % 2
                    rows = slice(par * 64, par * 64 + 64)
                    pb_, pr_ = bank()
                    mm(pb_[:, 0:128], Bmr[rows, g2l].rearrange("p a b -> p (a b)"), C1r[rows, g2l].rearrange("p a b -> p (a b)"), True, False, [RA], [pr_])
                    mm(pb_[:, 0:128], Bmi[rows, g2l].rearrange("p a b -> p (a b)"), C1i[rows, g2l].rearrange("p a b -> p (a b)"), False, True, [RA], [pr_])
                    tt = tmpT[gl % 4]; rt = RT[gl % 4]
                    op("dve", "tensor_tensor", [pr_, Rc], [rt], out=tt, in0=pb_[:, 0:128], in1=maskf, op=ALU.mult)
                    op("dve", "scalar_tensor_tensor", [rt, Rc], [RoutT], out=Tsb[:, gl, :], in0=identf, scalar=Dg[:, li, g:g + 1], in1=tt, op0=ALU.mult, op1=ALU.add)
                    pb2, pr2 = bank()
                    tr(pb2[:, 0:64], Bqr[rows, g2l].rearrange("p a b -> p (a b)"), identf[rows, par * 64:par * 64 + 64], [RA, Rc], [pr2])
                    tr(pb2[:, 64:128], Bqi[rows, g2l].rearrange("p a b -> p (a b)"), identf[rows, par * 64:par * 64 + 64], [RA, Rc], [pr2])
                    op("act", "copy", [pr2], [Rout], out=Bsb[:, gl, :], in_=pb2[:, 0:128])
                    pb3, pr3 = bank()
                    tr(pb3[:, 0:64], B2r[rows, g2l].rearrange("p a b -> p (a b)"), identf[rows, par * 64:par * 64 + 64], [RA, Rc], [pr3])
                    tr(pb3[:, 64:128], B2i[rows, g2l].rearrange("p a b -> p (a b)"), identf[rows, par * 64:par * 64 + 64], [RA, Rc], [pr3])
                    op("act", "copy", [pr3], [Rout], out=B2sb[:, gl, :], in_=pb3[:, 0:128])
                cols = slice(hf * 4096, hf * 4096 + 4096)
                dma("sp", scr[("T", l)][:, cols], Tsb.rearrange("p a b -> p (a b)"), [RoutT], [Rscr[("T", l)]])
                dma("sp", scr[("B", l)][:, cols], Bsb.rearrange("p a b -> p (a b)"), [Rout], [Rscr[("B", l)]])
                dma("sp", scr[("B2", l)][:, cols], B2sb.rearrange("p a b -> p (a b)"), [Rout], [Rscr[("B2", l)]])
                dma("sp", scr[("C", l)][:, cols], Csb.rearrange("p a b c -> p (a b c)"), [Rout], [Rscr[("C", l)]])

        for l in s5layers:
            s5_prologue(l)
        if KSTOP == "pro": S.frozen = True

        u_tm = view_at(work0, [8, 1024], BF16); Rutm = [Res("u_tm%d" % i) for i in range(8)]
        u_tmu = view_at(work0, [64, 8, 16], BF16)
        Sst = view_at(work0 + 16384, [32, 2, 131], F32); RSh = [Res("Sst0"), Res("Sst1")]
        Uv = view_at(off_y3 + 16640, [64, 128], BF16); RU = [Res("U%d" % i) for i in range(16)]
        yfm = view_at(off_y3 + 16640, [8, TMAX], BF16)
        Sb = view_at(off_stage, [16, 2, 128], BF16); RSb = Res("Sb")
        s5work = Rutm + RSh
        UA, _ = alloc([64, 2], BF16); RUA = Res("UA")
        yfmA, _ = alloc([8, 16], BF16); RyA = Res("yfmA")
        SbA, _ = alloc([32, 2, 2], BF16); RSbA = Res("SbA")
        scan_t1, _ = alloc([32, 2], F32); scan_t2, _ = alloc([32, 2], F32)
        scan_t1b, _ = alloc([32, 2], F32); scan_t2b, _ = alloc([32, 2], F32)
        Rscan1 = [Res("scan1_0"), Res("scan1_1")]; Rscan2 = [Res("scan2_0"), Res("scan2_1")]
        sig = []; Rsig = []
        for i in range(2):
            v, _ = alloc([512], F32); sig.append(v); Rsig.append(Res("sig%d" % i))
        carry, _ = alloc([2, 32, 2], F32); Rcarry = [Res("carry0"), Res("carry1")]
        assert mem_top[0] <= ARENA_BYTES, mem_top[0]
        o = work0
        tcg = [view_at(o + i * 2048, [512], F32) for i in range(2)]; o += 4096
        hcb = [view_at(o + i * 4192, [1048], F32) for i in range(2)]; o += 8384
        c1b = [view_at(o + i * 4160, [TMAX], F32) for i in range(2)]; o += 8320
        szb = [view_at(o + i * 2048, [512], F32) for i in range(2)]; o += 4096
        yvb = [view_at(o + i * 2048, [512], F32) for i in range(2)]; o += 4096
        Rtcg = [Res("tcg%d" % i) for i in range(2)]; Rhc = [Res("hc%d" % i) for i in range(2)]
        Rc1 = [Res("c1%d" % i) for i in range(2)]; Rsz = [Res("sz%d" % i) for i in range(2)]; Ryv = [Res("yv%d" % i) for i in range(2)]
        Rhch = [Res("hch%d" % i) for i in range(2)]
        convwork = Rtcg + Rhc + Rhch + Rc1 + Rsz + Ryv
        o = work0
        ubb = [view_at(o + i * 4224, [1056], F32) for i in range(2)]; o += 8448
        lvb = [view_at(o + i * 4224, [1056], F32) for i in range(4)]; o += 16896
        mixb = [view_at(o + i * 8320, [4, TMAX], BF16) for i in range(2)]; o += 16640
        yvp = [view_at(o + i * 2048, [512], F32) for i in range(2)]; o += 4096
        tfix = view_at(o, [16], F32); o += 64
        assert o <= work0 + WORK_BYTES
        Rub = [Res("ub%d" % i) for i in range(2)]; Rlv = [Res("lv%d" % i) for i in range(4)]
        Rmix = [Res("mix%d" % i) for i in range(2)]; Ryvp = [Res("yvp%d" % i) for i in range(2)]; Rtfix = Res("tfix")
        poolwork = Rub + Rlv + Rmix + Ryvp + [Rtfix]
        cur_work = [None]
        global _DBG_OFFS
        _DBG_OFFS = dict(off_hn=off_hn, off_y3=off_y3, work0=work0, off_stage=off_stage)
        S.realias(list(PRO_RES.values()), Rh_all + [Rhn] + Ry3 + Rring + s5work + convwork + poolwork)

        def use_work(kind):
            new = {"s5": s5work, "conv": convwork, "pool": poolwork}[kind]
            if cur_work[0] is not None and cur_work[0] is not new:
                S.realias(cur_work[0], new)
            cur_work[0] = new

        def load_w(eng, src_ap, shape, reads=()):
            v, r = wslot(shape)
            dma(eng, v, src_ap, list(reads), [r])
            return v, r

        def load_w_parts(eng, shape, partfn, nparts, reads=()):
            v, r = wslot(shape)
            prev = None
            for i in range(nparts):
                d_, s_ = partfn(v, i)
                prev = dma(eng, d_, s_, list(reads), [r], part_of=prev)
            return v, r

        def rmsnorm_stats(t0, tn, k):
            pb_, pr_ = bank()
            for b in range(8):
                j = b % 2
                op("act", "activation", [rh(b, t0)], [Rsq[j]], out=sq[j][:, 0:tn], in_=h[:, b, t0:t0 + tn], func=AF.Square)
                mm(pb_[:, 0:tn], onesb, sq[j][:, 0:tn], b == 0, b == 7, [Rsq[j], Rc], [pr_])
            op("act", "activation", [pr_], [Rrs[k]], out=rsb[k][:, 0:tn], in_=pb_[:, 0:tn], func=AF.Sqrt, bias=EPS, scale=1.0 / DM)
            op("dve", "reciprocal", [Rrs[k]], [Rrs[k]], out=rsb[k][:, 0:tn], in_=rsb[k][:, 0:tn])

        nrm_k = [0]

        def norm_to_hn(ci, gi):
            for (t0, tn) in chunk_tiles(ci):
                k = nrm_k[0]; nrm_k[0] ^= 1
                rmsnorm_stats(t0, tn, k)
                for b in range(8):
                    op("dve", "scalar_tensor_tensor", [rh(b, t0), Rrs[k], Rc], [Rhn], out=hn[:, b, t0:t0 + tn], in0=h[:, b, t0:t0 + tn],
                       scalar=gains[:, gi, b:b + 1], in1=rsb[k][:, 0:tn], op0=ALU.mult, op1=ALU.mult)

        def out_proj(ci, wname, nk):
            colw = 4096 // nk
            nslots = DM // colw
            slots = []
            for s_ in range(nslots):
                slots.append(load_w("pool", D[wname][:, s_ * colw:(s_ + 1) * colw].rearrange("(k p) c -> p k c", p=128), [nk, colw]))
            for (t0, tn) in chunk_tiles(ci):
                for s_ in range(nslots):
                    wv, wr = slots[s_]
                    for db in range(colw // 128):
                        dblk = s_ * (colw // 128) + db
                        pb_, pr_ = bank()
                        for k in range(nk):
                            mm(pb_[:, 0:tn], wv[:, k, db * 128:(db + 1) * 128], y3[:, k, t0:t0 + tn], k == 0, k == nk - 1, [wr, Ry3[k]], [pr_])
                        op("dve", "tensor_tensor", [pr_, rh(dblk, t0)], [rh(dblk, t0)], out=h[:, dblk, t0:t0 + tn], in0=pb_[:, 0:tn], in1=h[:, dblk, t0:t0 + tn], op=ALU.add)

        stg_i = [0]

        def load_chunk(ci):
            S.phase = "load"
            c0, T = CHUNKS[ci]
            tiles = []
            if ci == 0:
                tiles.append(("meta", 0, 16, 0))
                for j in range(8): tiles.append(("x", j * 128, 128, 16 + j * 128))
            else:
                for j in range(8): tiles.append(("x", c0 - NMETA + j * 128, 128, j * 128))
            for ti_, (kind, r0, nr, col) in enumerate(tiles):
                if KSTOP == "load%d" % ti_: S.frozen = True
                si = stg_i[0]; stg_i[0] ^= 1
                src = D["meta_tokens"] if kind == "meta" else D["x"][r0:r0 + nr, :]
                dma("sp", stage[si][0:nr, :], src, [], [Rstage[si]])
                for half in range(2):
                    pb_, pr_ = bank()
                    for q in range(4):
                        b = half * 4 + q
                        tr(pb_[:, q * 128:q * 128 + nr], stage[si][0:nr, b * 128:(b + 1) * 128], identf[0:nr, 0:nr], [Rstage[si], Rc], [pr_])
                    for q in range(4):
                        b = half * 4 + q
                        op("act" if half == 0 else "dve", "copy" if half == 0 else "tensor_copy", [pr_], [rh(b, col)],
                           out=h[:, b, col:col + nr], in_=pb_[:, q * 128:q * 128 + nr])

        def final_store(ci):
            S.phase = "final"
            c0, T = CHUNKS[ci]
            hf_, _o = None, None
            for (t0, tn) in chunk_tiles(ci):
                if ci == 0 and t0 == 0: continue
                k = nrm_k[0]; nrm_k[0] ^= 1
                rmsnorm_stats(t0, tn, k)
                if KSTOP == "stats": S.frozen = True
                hf = view_at(off_y3, [8, 512], F32)
                for b in range(8):
                    op("dve", "scalar_tensor_tensor", [rh(b, t0), Rrs[k], Rc], Ry3[0:8], out=hf[:, b, 0:tn], in0=h[:, b, t0:t0 + tn],
                       scalar=gains[:, 4, b:b + 1], in1=rsb[k][:, 0:tn], op0=ALU.mult, op1=ALU.mult)
                for sub in range(tn // 128):
                    si = stg_i[0]; stg_i[0] ^= 1
                    for half in range(2):
                        pb_, pr_ = bank()
                        for q in range(4):
                            b = half * 4 + q
                            tr(pb_[:, q * 128:(q + 1) * 128], hf[:, b, sub * 128:(sub + 1) * 128], identf, Ry3[0:8] + [Rc], [pr_])
                        op("act" if si == 0 else "dve", "copy" if si == 0 else "tensor_copy", [pr_], [Rstage[si]],
                           out=stage[si][:, half * 512:(half + 1) * 512], in_=pb_[:, 0:512])
                    row0 = c0 + t0 + sub * 128 - NMETA
                    dma("sp", out_d[row0:row0 + 128, :], stage[si], [Rstage[si]], [Res("outd")], semres=Rstage[si])

        def layer_conv(ci):
            c0, T = CHUNKS[ci]
            use_work("conv")
            norm_to_hn(ci, 1)
            for e in range(16):
                srcw = D["l1_w_in"].rearrange("(k p) (q c) -> p k q c", p=128, q=4)
                wv, wr = load_w_parts("pool", [8, 4, 128], lambda v, i, e=e, srcw=srcw: (v[:, :, i, :], srcw[:, :, i, e * 128:(e + 1) * 128]), 4)
                j = e % 2
                hc = hcb[j]; c1 = c1b[j]
                op("act", "copy", [Rchist], [Rhch[j]], out=hc[:, 0:2], in_=chist[:, e, :])
                for (t0, tn) in chunk_tiles(ci):
                    banks = []
                    for part in (1, 2, 0, 3):
                        pb_, pr_ = bank()
                        for b in range(8):
                            mm(pb_[:, 0:tn], wv[:, b, part, :], hn[:, b, t0:t0 + tn], b == 0, b == 7, [wr, Rhn], [pr_])
                        banks.append((pb_, pr_))
                    (pcg, rcg), (pv, rv), (pbg_, rbg), (pz, rz) = banks
                    op("act", "copy", [rcg], [Rtcg[j]], out=tcg[j][:, 0:tn], in_=pcg[:, 0:tn])
                    op("dve", "tensor_tensor", [Rtcg[j], rv], [Rhc[j]], out=hc[:, 2 + t0:2 + t0 + tn], in0=pv[:, 0:tn], in1=tcg[j][:, 0:tn], op=ALU.mult)
                    op("act", "activation", [rz], [Rsz[j]], out=szb[j][:, 0:tn], in_=pz[:, 0:tn], func=AF.Silu)
                    op("act", "activation", [Rhc[j], Rc], [Rc1[j]], out=c1[:, t0:t0 + tn], in_=hc[:, 2 + t0:2 + t0 + tn], func=AF.Identity,
                       bias=cb[:, e:e + 1], scale=cw[:, 2, e:e + 1])
                    op("dve", "scalar_tensor_tensor", [Rhc[j], Rhch[j], Rc1[j], Rc], [Rc1[j]], out=c1[:, t0:t0 + tn], in0=hc[:, 1 + t0:1 + t0 + tn],
                       scalar=cw[:, 1, e:e + 1], in1=c1[:, t0:t0 + tn], op0=ALU.mult, op1=ALU.add)
                    op("dve", "scalar_tensor_tensor", [Rhc[j], Rhch[j], Rc1[j], Rc], [Rc1[j]], out=c1[:, t0:t0 + tn], in0=hc[:, t0:t0 + tn],
                       scalar=cw[:, 0, e:e + 1], in1=c1[:, t0:t0 + tn], op0=ALU.mult, op1=ALU.add)
                    op("dve", "tensor_tensor", [Rc1[j], rbg], [Ryv[j]], out=yvb[j][:, 0:tn], in0=pbg_[:, 0:tn], in1=c1[:, t0:t0 + tn], op=ALU.mult)
                    op("dve", "tensor_tensor", [Ryv[j], Rsz[j]], [Ry3[e]], out=y3[:, e, t0:t0 + tn], in0=yvb[j][:, 0:tn], in1=szb[j][:, 0:tn], op=ALU.mult)
                op("act", "copy", [Rhc[j]], [Rchist], out=chist[:, e, :], in_=hc[:, T:T + 2])
            out_proj(ci, "l1_w_out", 16)

        def layer_pool(ci):
            c0, T = CHUNKS[ci]
            use_work("pool")
            norm_to_hn(ci, 2)
            grp_calls = []
            def GRP(k):
                mix = mixb[k % 2]; rmix = Rmix[k % 2]
                wv, wr = load_w("pool", D["l2_w_grp"][k].rearrange("(kk p) c -> p kk c", p=128), [4, 512])
                for eo in range(4):
                    e = 4 * k + eo
                    for (t0, tn) in chunk_tiles(ci):
                        pb_, pr_ = bank()
                        for ei in range(4):
                            mm(pb_[:, 0:tn], wv[:, ei, eo * 128:(eo + 1) * 128], mix[:, ei, t0:t0 + tn], ei == 0, ei == 3, [wr, rmix], [pr_])
                        jj = eo % 2
                        op("act", "activation", [pr_, Rc, Rpbs], [Ryvp[jj]], out=yvp[jj][:, 0:tn], in_=pb_[:, 0:tn], func=AF.Identity,
                           bias=pbs[:, e:e + 1], scale=pscale[:, e:e + 1])
                        op("dve", "tensor_tensor", [Ryvp[jj], Ry3[e]], [Ry3[e]], out=y3[:, e, t0:t0 + tn], in0=yvp[jj][:, 0:tn], in1=y3[:, e, t0:t0 + tn], op=ALU.mult)

            for k in range(4):
                w = 2 << k
                mix = mixb[k % 2]; rmix = Rmix[k % 2]
                for epair in range(2):
                    e0 = 4 * k + 2 * epair
                    srcw = D["l2_w_in"].rearrange("(kk p) (q c) -> p kk q c", p=128, q=2)
                    wv, wr = load_w_parts("pool", [8, 2, 256], lambda v, i, e0=e0, srcw=srcw: (v[:, :, i, :], srcw[:, :, i, e0 * 128:(e0 + 2) * 128]), 2)
                    for el in range(2):
                        e = e0 + el; ei = 2 * epair + el
                        j = e % 2
                        ub = ubb[j]
                        op("act", "copy", [Rphist], [Rub[j]], out=ub[:, 0:16], in_=phist[:, e, :])
                        for (t0, tn) in chunk_tiles(ci):
                            pb_, pr_ = bank()
                            for b in range(8):
                                mm(pb_[:, 0:tn], wv[:, b, 0, el * 128:(el + 1) * 128], hn[:, b, t0:t0 + tn], b == 0, b == 7, [wr, Rhn], [pr_])
                            op("act", "copy", [pr_], [Rub[j]], out=ub[:, 16 + t0:16 + t0 + tn], in_=pb_[:, 0:tn])
                            pb2, pr2 = bank()
                            for b in range(8):
                                mm(pb2[:, 0:tn], wv[:, b, 1, el * 128:(el + 1) * 128], hn[:, b, t0:t0 + tn], b == 0, b == 7, [wr, Rhn], [pr2])
                            op("act", "activation", [pr2], [Ry3[e]], out=y3[:, e, t0:t0 + tn], in_=pb2[:, 0:tn], func=AF.Silu)
                        op("act", "copy", [Rub[j]], [Rphist], out=phist[:, e, :], in_=ub[:, T:T + 16])
                        cur = ub; rcur = Rub[j]; lo = 0
                        NE = 16 + T
                        for lev in range(k + 1):
                            sft = 1 << lev
                            nxt = lvb[lev % 2 + 2 * j]
                            rn = Rlv[lev % 2 + 2 * j]
                            nlo = lo + sft
                            op("dve", "tensor_tensor", [rcur], [rn], out=nxt[:, nlo:NE], in0=cur[:, nlo:NE], in1=cur[:, nlo - sft:NE - sft], op=ALU.add)
                            cur = nxt; rcur = rn; lo = nlo
                        op("dve", "scalar_tensor_tensor", [rcur, Rub[j]], [rmix], out=mix[:, ei, 0:T], in0=cur[:, 16:16 + T], scalar=1.0 / w,
                           in1=ub[:, 16:16 + T], op0=ALU.mult, op1=ALU.subtract)
                        if ci == 0:
                            op("dve", "tensor_tensor", [rcur, Rc], [Rtfix], out=tfix[:, 0:w - 1], in0=cur[:, 16:16 + w - 1], in1=invc[:, 0:w - 1], op=ALU.mult)
                            op("dve", "tensor_tensor", [Rtfix, Rub[j]], [rmix], out=mix[:, ei, 0:w - 1], in0=tfix[:, 0:w - 1], in1=ub[:, 16:16 + w - 1], op=ALU.subtract)
                if k >= 1:
                    grp_calls.append(k - 1); GRP(k - 1)
            GRP(3)
            out_proj(ci, "l2_w_out", 16)

        def layer_s5_full(ci, l):
            c0, T = CHUNKS[ci]
            li = 0 if l == 0 else 1
            pre = "l%d_" % l
            S.phase = "s5norm"
            norm_to_hn(ci, l)
            pieces = chunk_pieces(ci)
            def zgate():
                S.phase = "s5zgate"
                for cs_ in range(2):
                    wv, wr = load_w("pool", D[pre + "w_in"][:, 1024 + cs_ * 512:1024 + (cs_ + 1) * 512].rearrange("(k p) c -> p k c", p=128), [8, 512])
                    for cbk in range(4):
                        cblk = cs_ * 4 + cbk
                        for (t0, tn) in chunk_tiles(ci):
                            pb_, pr_ = bank()
                            for b in range(8):
                                mm(pb_[:, 0:tn], wv[:, b, cbk * 128:(cbk + 1) * 128], hn[:, b, t0:t0 + tn], b == 0, b == 7, [wr, Rhn], [pr_])
                            op("act", "activation", [pr_], [Ry3[cblk]], out=y3[:, cblk, t0:t0 + tn], in_=pb_[:, 0:tn], func=AF.Silu)
            if KSTOP == "s5gate": S.frozen = True
            ycols = []; ncol = 0
            for (n0_, nn_) in pieces:
                ycols.append(ncol); ncol += nn_

            def piece_stage(pi_, n0, nn, stage):
                ycol = ycols[pi_]
                last_piece = (pi_ == len(pieces) - 1)
                small = nn < 128
                Uc = UA if small else Uv
                RUc = [RUA] * 16 if small else RU
                Sbc = SbA if small else Sb
                RSbc = RSbA if small else RSb
                if stage == 1:
                    S.phase = "s5u"
                    for half in range(2):
                        wv, wr = load_w("pool", D[pre + "w_in"][:, half * 512:(half + 1) * 512].rearrange("(k p) c -> p k c", p=128), [8, 512])
                        for tau in range(8):
                            pb_, pr_ = bank()
                            for b in range(8):
                                lhs = hn[:, b, 8 * n0:8 * (n0 + nn)].rearrange("p (n t) -> p n t", t=8)[:, :, tau]
                                mm(pb_[0:nn, :], lhs, wv[:, b, :], b == 0, b == 7, [wr, Rhn], [pr_])
                            op("act" if tau % 2 == 0 else "dve", "copy" if tau % 2 == 0 else "tensor_copy", [pr_], [Rutm[tau]],
                               out=u_tmu[0:nn, half * 32:(half + 1) * 32, tau, :], in_=pb_[0:nn, :].rearrange("p (g i) -> p g i", i=16))
                    if not small:
                        S.realias(Ry3[8:16], RU)
                    S.phase = "s5Utr"
                    for gq in range(16):
                        pb_, pr_ = bank()
                        pbb = pb_[:].bitcast(BF16)
                        for gl in range(4):
                            g = 4 * gq + gl
                            tr(pbb[:, gl * 128:gl * 128 + nn], u_tmu[0:nn, g].rearrange("p t i -> p (t i)"), identb[0:nn, 0:nn], Rutm + [Rc], [pr_])
                        op("dve" if gq % 2 == 0 else "act", "tensor_copy" if gq % 2 == 0 else "copy", [pr_], [RUc[gq]],
                           out=Uc[:, 4 * gq:4 * gq + 4, 0:nn], in_=pbb[:, 0:512].rearrange("p (g n) -> p g n", n=128)[:, :, 0:nn])
                    if KSTOP == "s5U" and last_piece: S.frozen = True
                    S.phase = "s5z"
                    for hf in range(2):
                        wv, wr = load_w("sp", scr[("B", l)][:, hf * 4096:(hf + 1) * 4096].rearrange("p (a b) -> p a b", b=128), [32, 128], reads=[Rscr[("B", l)]])
                        if nn > 1:
                            wv2, wr2 = load_w("sp", scr[("B2", l)][:, hf * 4096:(hf + 1) * 4096].rearrange("p (a b) -> p a b", b=128), [32, 128], reads=[Rscr[("B2", l)]])
                        for g2p in range(8):
                            pb_, pr_ = bank()
                            for a in range(2):
                                g2 = hf * 16 + g2p * 2 + a
                                for par in range(2):
                                    g = 2 * g2 + par; gl = g - hf * 32
                                    rows = slice(par * 64, par * 64 + 64)
                                    for ri in range(2):
                                        c_ = (a * 2 + ri) * 128
                                        mm(pb_[rows, c_:c_ + nn], wv[:, gl, ri * 64:(ri + 1) * 64], Uc[:, g, 0:nn], True, nn == 1, [wr, RUc[g // 4]], [pr_])
                                        if nn > 1:
                                            mm(pb_[rows, c_ + 1:c_ + nn], wv2[:, gl, ri * 64:(ri + 1) * 64], Uc[:, g, 0:nn - 1], False, True, [wr2, RUc[g // 4]], [pr_])
                            g2a = hf * 16 + g2p * 2
                            op("act", "copy", [pr_], RSh, out=Sst[:, g2a:g2a + 2, :, ycol + 1:ycol + 1 + nn],
                               in_=pb_[:, 0:512].rearrange("p (a r n) -> p a r n", a=2, r=2)[:, :, :, 0:nn])
                    if KSTOP == "s5z" and last_piece: S.frozen = True
                if stage == 2:
                    S.phase = "s5scan"
                    def chain(c, pcol, qcol, Trr, Tis, rd, wr_):
                        t1c = scan_t1 if c == 0 else scan_t1b; t2c = scan_t2 if c == 0 else scan_t2b
                        r1 = Rscan1[c]; r2 = Rscan2[c]
                        return [
                            lambda: op("dve", "tensor_tensor", rd + [Rtab[li]], [r2], out=t2c[:, :, 0], in0=Sst[:, :, 1, pcol], in1=Tis[:, li, :, 0], op=ALU.mult),
                            lambda: op("dve", "tensor_tensor", rd + [Rtab[li]], [r2], out=t2c[:, :, 1], in0=Sst[:, :, 0, pcol], in1=Tis[:, li, :, 1], op=ALU.mult),
                            lambda: op("dve", "tensor_tensor", rd + [Rtab[li]], [r1], out=t1c, in0=Sst[:, :, :, pcol], in1=Trr[:, li], op=ALU.mult),
                            lambda: op("dve", "tensor_tensor", [r2, wr_], [wr_], out=Sst[:, :, :, qcol], in0=Sst[:, :, :, qcol], in1=t2c, op=ALU.add),
                            lambda: op("dve", "tensor_tensor", [r1, wr_], [wr_], out=Sst[:, :, :, qcol], in0=Sst[:, :, :, qcol], in1=t1c, op=ALU.add),
                        ]
                    for f_ in chain(0, ycol, ycol + 1, ArAr, AiS, RSh, RSh[0]): f_()
                    n = 1
                    while n < nn:
                        if n + 1 < nn:
                            rdO = RSh if n == 1 else [RSh[1]]
                            rdE = RSh if n == 1 else [RSh[0]]
                            cO = chain(1, ycol + n - 1, ycol + n + 1, A2rr, A2is, rdO, RSh[1])
                            cE = chain(0, ycol + n, ycol + n + 2, A2rr, A2is, rdE, RSh[0])
                            for fo, fe in zip(cO, cE):
                                fo(); fe()
                            n += 2
                        else:
                            for f_ in chain(1, ycol + n - 1, ycol + n + 1, A2rr, A2is, RSh, RSh[1]): f_()
                            n += 1
                    if KSTOP == "s5scan" and last_piece: S.frozen = True
                if stage == 3:
                    S.phase = "s5Y"
                    if not small:
                        S.realias(Rstage, [RSb])
                    if small:
                        op("act", "copy", RSh, [RSbc], out=Sbc[:, :, :, 0:nn], in_=Sst[:, :, :, ycol:ycol + nn])
                    for hf in range(2):
                        if not small:
                            op("dve", "tensor_copy", RSh, [RSbc], out=Sbc[:, :, :, 0:nn], in_=Sst[:, hf * 16:hf * 16 + 16, :, ycol:ycol + nn])
                        sbo = 0 if small else hf * 16
                        wT, rT = load_w("sp", scr[("T", l)][:, hf * 4096:(hf + 1) * 4096].rearrange("p (a b) -> p a b", b=128), [32, 128], reads=[Rscr[("T", l)]])
                        wC, rC = load_w_parts("sp", [16, 2, 128], lambda v, i, hf=hf: (v.rearrange("p a r b -> p (a r b)"), scr[("C", l)][:, hf * 4096:(hf + 1) * 4096]), 1, reads=[Rscr[("C", l)]])
                        for gq in range(8):
                            pb_, pr_ = bank()
                            for gl4 in range(4):
                                gl = gq * 4 + gl4; g = hf * 32 + gl; g2 = g // 2; par = g % 2
                                rows = slice(par * 64, par * 64 + 64)
                                o_ = pb_[0:nn, gl4 * 128:(gl4 + 1) * 128]
                                mm(o_, Uc[:, g, 0:nn], wT[:, gl, :], True, False, [RUc[g // 4], rT], [pr_])
                                mm(o_, Sbc[rows, g2 - sbo, 0, 0:nn], wC[rows, g2 - hf * 16, 0, :], False, False, [RSbc, rC], [pr_])
                                mm(o_, Sbc[rows, g2 - sbo, 1, 0:nn], wC[rows, g2 - hf * 16, 1, :], False, True, [RSbc, rC], [pr_])
                            gq_abs = hf * 8 + gq
                            op("act", "activation", [pr_], Rutm, out=u_tm[0:nn, :, 64 * gq_abs:64 * gq_abs + 64].rearrange("p j (g o) -> p g j o", g=4),
                               in_=pb_[0:nn, 0:512].rearrange("p (g j o) -> p g j o", g=4, j=8), func=AF.Gelu_apprx_tanh)
                    if not small:
                        S.realias([RSb], Rstage)
                    if KSTOP == "s5Y" and last_piece: S.frozen = True
                    S.phase = "s5ytr"
                    for cbk in range(8):
                        pb_, pr_ = bank()
                        pbb = pb_[:].bitcast(BF16)
                        for j in range(8):
                            tr(pbb[:, j * 128:j * 128 + nn], u_tm[0:nn, j, cbk * 128:(cbk + 1) * 128], identb[0:nn, 0:nn], [Rutm[j], Rc], [pr_])
                        ydst = yfmA if small else yfm
                        op("dve" if cbk % 2 == 0 else "act", "tensor_copy" if cbk % 2 == 0 else "copy", [pr_], [RyA if small else Ry3[8 + cbk]],
                           out=ydst[:, cbk, 8 * n0:8 * (n0 + nn)].rearrange("p (n j) -> p j n", j=8),
                           in_=pbb[:, 0:1024].rearrange("p (j n) -> p j n", n=128)[:, :, 0:nn])
                    if not small:
                        S.realias(RU, Ry3[8:16])

            for pi_, (n0, nn) in enumerate(pieces): piece_stage(pi_, n0, nn, 1)
            zgate()
            for pi_, (n0, nn) in enumerate(pieces): piece_stage(pi_, n0, nn, 2)
            for pi_, (n0, nn) in enumerate(pieces): piece_stage(pi_, n0, nn, 3)

            if KSTOP == "s5yfm": S.frozen = True
            op("dve", "tensor_copy", RSh, [Rcarry[li]], out=carry[:, li], in_=Sst[:, :, :, ncol])
            if len(pieces) > 1:
                op("dve", "tensor_copy", [RyA], Ry3[8:16], out=yfm[:, :, 0:16], in_=yfmA[:, :, 0:16])
            S.phase = "s5glu"
            for c_ in range(8):
                op("dve", "tensor_tensor", [Ry3[8 + c_], Ry3[c_]], [Ry3[c_]], out=y3[:, c_, 0:T], in0=yfm[:, c_, 0:T], in1=y3[:, c_, 0:T], op=ALU.mult)
            for cs_ in range(2):
                wv, wr = load_w("pool", D[pre + "w_glu"][:, cs_ * 512:(cs_ + 1) * 512].rearrange("(k p) c -> p k c", p=128), [8, 512])
                for cbk in range(4):
                    cblk = cs_ * 4 + cbk
                    for (t0, tn) in chunk_tiles(ci):
                        pb_, pr_ = bank()
                        for k in range(8):
                            mm(pb_[:, 0:tn], wv[:, k, cbk * 128:(cbk + 1) * 128], yfm[:, k, t0:t0 + tn], k == 0, k == 7, [wr, Ry3[8 + k]], [pr_])
                        jj = cblk % 2
                        op("act", "activation", [pr_, Rc], [Rsig[jj]], out=sig[jj][:, 0:tn], in_=pb_[:, 0:tn], func=AF.Sigmoid, bias=bglu[:, li, cblk:cblk + 1])
                        op("dve", "tensor_tensor", [Rsig[jj], Ry3[cblk]], [Ry3[cblk]], out=y3[:, cblk, t0:t0 + tn], in0=sig[jj][:, 0:tn], in1=y3[:, cblk, t0:t0 + tn], op=ALU.mult)
            S.phase = "s5out"
            out_proj(ci, pre + "w_out", 8)

        first_s5 = {0: True, 3: True}
        for ci in range(nchunks):
            cur_ci[0] = ci
            if ci == 1:
                S.realias(Rh_all, Rh_all)
            load_chunk(ci)
            if KSTOP == "load": S.frozen = True
            for l in layers:
                if l in (0, 3):
                    li = 0 if l == 0 else 1
                    use_work("s5")
                    if first_s5[l]:
                        op("dve", "memset", [], RSh, Sst[:, :, :, 0], 0.0)
                        first_s5[l] = False
                    else:
                        op("dve", "tensor_copy", [Rcarry[li]], RSh, out=Sst[:, :, :, 0], in_=carry[:, li])
                    layer_s5_full(ci, l)
                elif l == 1:
                    layer_conv(ci)
                else:
                    layer_pool(ci)
            final_store(ci)
        if _os.environ.get("KDUMP", ""):
            S.frozen = False
            regs = [(off_hn, 16640), (off_y3, 33280), (work0, WORK_BYTES), (off_stage, 8192)]
            if _os.environ["KDUMP"] != "1":
                regs = [tuple(int(v_) for v_ in t_.split(":")) for t_ in _os.environ["KDUMP"].split(",")]
            allres = Rh_all + [Rhn] + Ry3 + Rring + s5work + convwork + poolwork + Rstage + RU + [RSb] + RSh + Rutm
            ov = out_d.rearrange("(p r) c -> p (r c)", p=128)
            pos = 0
            Rdump = Res("dump")
            for (o_, n_) in regs:
                dma("sp", ov[:, pos // 4:(pos + n_) // 4], arena[:, o_ // 4:(o_ + n_) // 4], allres, [Rdump], semres=Res("dumpsem%d" % pos))
                pos += n_
        S.emit()
    return nc


_CACHE = {}


def kernel(**inputs):
    x = np.ascontiguousarray(inputs["x"], dtype=np.float32)
    B = x.shape[0]
    if "nc" not in _CACHE:
        _CACHE["nc"] = build_program()
    nc = _CACHE["nc"]
    consts = host_consts()
    shared = {n: np.ascontiguousarray(inputs[n], dtype=np.float32) for n in PARAM_NAMES}
    shared.update(consts)
    in_maps = []
    for b in range(B):
        m = dict(shared)
        m["x"] = x[b]
        in_maps.append(m)
    res = run_bass_kernel_spmd(nc, in_maps, core_ids=list(range(B)))
    out = np.stack([np.asarray(r["out"], dtype=np.float32) for r in res.results], axis=0)
    return out
```

```python
import math
from contextlib import ExitStack
import numpy as np
import concourse.bass as bass
import concourse.mybir as mybir
from concourse.bass_utils import run_bass_kernel_spmd

F32 = mybir.dt.float32
BF16 = mybir.dt.bfloat16
I32 = mybir.dt.int32
AF = mybir.ActivationFunctionType
ALU = mybir.AluOpType
P = 128
NMETA = 16
SEQ = 4096
DM = 1024
EPS = 1e-6
PI = math.pi


class Res:
    __slots__ = ("name", "w", "rs", "sem", "ndma", "grp", "multi", "excl")

    def __init__(self, name, grp=None, excl=False):
        self.name = name; self.w = None; self.rs = {}; self.sem = None; self.ndma = 0; self.grp = grp; self.multi = None
        self.excl = excl


class SemGroup:
    def __init__(self, name):
        self.name = name; self.sem = None; self.total = 0


class Op:
    __slots__ = ("eng", "fn", "deps", "dma", "semres", "needs_inc", "ev", "phase")

    def __init__(self, eng, fn, dma):
        self.eng = eng; self.fn = fn; self.deps = []; self.dma = dma; self.semres = None
        self.needs_inc = False; self.ev = None


class Sched:
    ENG = ("pe", "act", "dve", "pool", "sp")

    def __init__(self, nc, stack):
        self.nc = nc; self.stack = stack
        self.ops = {e: [] for e in self.ENG}
        self.all_dma = []
        self.nsem = 0

    def new_sem(self, name):
        self.nsem += 1
        return self.stack.enter_context(self.nc.semaphore(name))

    frozen = False
    phase = ""

    def add(self, eng, fn, reads=(), writes=(), dma=False, semres=None, part_of=None):
        if self.frozen:
            return None
        op = Op(eng, fn, dma)
        op.phase = self.phase
        deps = []
        rr = []
        for r in reads:
            if r.multi is not None: rr.extend(r.multi)
            else: rr.append(r)
        reads = rr
        for r in reads:
            if r.w is not None: deps.append(r.w)
            if r.excl:
                for k_, v_ in r.rs.items():
                    if k_ != eng: deps.append(v_)
        for w in writes:
            if w.w is not None: deps.append(w.w)
            deps.extend(w.rs.values())
        seen = set(); out = []
        for d in deps:
            if id(d) in seen or d is op or d is part_of: continue
            seen.add(id(d))
            if not d.dma and not dma and d.eng == eng:
                if eng == "pe" or not self.ops[eng] or self.ops[eng][-1] is not d:
                    continue
                self.n_adj = getattr(self, "n_adj", 0) + 1
            out.append(d); d.needs_inc = True
        op.deps = out
        for r in reads: r.rs[eng if not dma else ("dma", id(op))] = op
        for w in writes: w.w = op; w.rs = {}
        if dma:
            op.semres = semres if semres is not None else writes[0]
            self.all_dma.append(op)
            op.needs_inc = True
        self.ops[eng].append(op)
        return op

    def realias(self, old, new):
        users = []
        for o in old:
            if o.w is not None: users.append(o.w)
            users.extend(list(o.rs.values()))
        for n in new:
            for v in users: n.rs[("r", id(v))] = v

    def emit(self):
        nc = self.nc
        MAXV = 30000
        nes = 0
        for op in self.all_dma:
            r = op.semres
            if r.grp is not None: r.grp.total += 1
        for e in self.ENG:
            cur = None; cnt = 0
            for op in self.ops[e]:
                if op.dma:
                    r = op.semres
                    if r.grp is not None:
                        g = r.grp
                        if g.sem is None: g.sem = self.new_sem("g_" + g.name)
                        op.ev = (g.sem, 16 * g.total)
                    else:
                        if r.sem is None: r.sem = {}
                        if e not in r.sem: r.sem[e] = [self.new_sem("d_%s_%s" % (r.name, e)), 0]
                        r.sem[e][1] += 1
                        op.ev = (r.sem[e][0], 16 * r.sem[e][1])
                elif op.needs_inc:
                    if cur is None or cnt >= MAXV:
                        cur = self.new_sem("e_%s_%d" % (e, nes)); nes += 1; cnt = 0
                    cnt += 1
                    op.ev = (cur, cnt)
        last = {}
        for op in self.all_dma: last[op.ev[0].name] = op.ev
        final_waits = list(last.values())
        engobj = {"pe": "tensor", "act": "scalar", "dve": "vector", "pool": "gpsimd", "sp": "sync"}
        sched = self
        with nc.Block() as block:
            def mk(ename):
                def body(eng):
                    known = {}
                    for op in sched.ops[ename]:
                        for d in op.deps:
                            sem, val = d.ev
                            if known.get(sem.name, 0) >= val: continue
                            known[sem.name] = val
                            eng.wait_ge(sem, val)
                        ins = op.fn(eng)
                        if _DBG_TAGS is not None:
                            try: _DBG_TAGS[ins.ins.name] = (ename, op.phase)
                            except Exception: pass
                        if op.ev is not None:
                            ins.then_inc(op.ev[0], 16 if op.dma else 1)
                    if ename == "sp":
                        for sem, val in final_waits:
                            eng.wait_ge(sem, val)
                return body
            for ename in self.ENG:
                getattr(block, engobj[ename])(mk(ename))


_DBG_TAGS = None
_DBG_OFFS = None
CHUNKS = [(0, 1040), (1040, 1024), (2064, 1024), (3088, 1024)]
TMAX = 1040
PARAM_NAMES = [
    "meta_tokens", "norm0_g", "l0_w_in", "l0_lam_re", "l0_lam_im", "l0_log_dt", "l0_b_re", "l0_b_im",
    "l0_c_re", "l0_c_im", "l0_d_skip", "l0_w_glu", "l0_b_glu", "l0_w_out",
    "norm1_g", "l1_w_in", "l1_conv_w", "l1_conv_b", "l1_w_out",
    "norm2_g", "l2_w_in", "l2_w_grp", "l2_b_grp", "l2_scale", "l2_w_out",
    "norm3_g", "l3_w_in", "l3_lam_re", "l3_lam_im", "l3_log_dt", "l3_b_re", "l3_b_im",
    "l3_c_re", "l3_c_im", "l3_d_skip", "l3_w_glu", "l3_b_glu", "l3_w_out", "final_g"]
PARAM_SHAPES = {
    "meta_tokens": [16, 1024], "norm0_g": [1024], "l0_w_in": [1024, 2048], "l0_lam_re": [64, 64], "l0_lam_im": [64, 64],
    "l0_log_dt": [64], "l0_b_re": [64, 64, 16], "l0_b_im": [64, 64, 16], "l0_c_re": [64, 16, 64], "l0_c_im": [64, 16, 64],
    "l0_d_skip": [1024], "l0_w_glu": [1024, 1024], "l0_b_glu": [1024], "l0_w_out": [1024, 1024],
    "norm1_g": [1024], "l1_w_in": [1024, 8192], "l1_conv_w": [3, 2048], "l1_conv_b": [2048], "l1_w_out": [2048, 1024],
    "norm2_g": [1024], "l2_w_in": [1024, 4096], "l2_w_grp": [4, 512, 512], "l2_b_grp": [4, 512], "l2_scale": [2048],
    "l2_w_out": [2048, 1024], "norm3_g": [1024], "l3_w_in": [1024, 2048], "l3_lam_re": [64, 64], "l3_lam_im": [64, 64],
    "l3_log_dt": [64], "l3_b_re": [64, 64, 16], "l3_b_im": [64, 64, 16], "l3_c_re": [64, 16, 64], "l3_c_im": [64, 16, 64],
    "l3_d_skip": [1024], "l3_w_glu": [1024, 1024], "l3_b_glu": [1024], "l3_w_out": [1024, 1024], "final_g": [1024]}


def host_consts():
    c = {}
    c["c_ident"] = np.eye(128, dtype=np.float32)
    idx = np.arange(128)
    c["c_mask"] = (idx[:, None] // 16 <= idx[None, :] // 16).astype(np.float32)
    selC = np.zeros((128, 2, 64), np.float32)
    for gl in range(8):
        for o in range(16):
            selC[gl * 16 + o, gl % 2, (gl // 2) * 16 + o] = 1.0
    c["c_selC"] = selC
    selG = np.zeros((64, 2, 32), np.float32)
    for g in range(64):
        selG[g, g % 2, g // 2] = 1.0
    c["c_selG"] = selG
    c["c_invc"] = np.tile((1.0 / np.arange(1, 17, dtype=np.float32))[None, :], (128, 1)).astype(np.float32)
    c["c_ones"] = np.ones((128, 128), np.float32)
    return c


CONST_SHAPES = {"c_ident": [128, 128], "c_mask": [128, 128], "c_selC": [128, 2, 64], "c_selG": [64, 2, 32],
                "c_invc": [128, 16], "c_ones": [128, 128]}


def chunk_tiles(ci):
    return [(0, 16), (16, 512), (528, 512)] if ci == 0 else [(0, 512), (512, 512)]


def chunk_pieces(ci):
    return [(0, 2), (2, 128)] if ci == 0 else [(0, 128)]


def build_program(layers=(0, 1, 2, 3), nchunks=4):
    nc = bass.Bass("TRN2", target_bir_lowering=False)
    D = {}
    D["x"] = nc.dram_tensor("x", [SEQ, DM], F32, kind="ExternalInput").ap()
    for n in PARAM_NAMES:
        D[n] = nc.dram_tensor(n, PARAM_SHAPES[n], F32, kind="ExternalInput").ap()
    for n, s in CONST_SHAPES.items():
        D[n] = nc.dram_tensor(n, s, F32, kind="ExternalInput").ap()
    out_d = nc.dram_tensor("out", [SEQ, DM], F32, kind="ExternalOutput").ap()
    scr = {}
    for l in [l_ for l_ in (0, 3) if l_ in layers]:
        scr[("T", l)] = nc.dram_tensor("scrT%d" % l, [128, 64 * 128], BF16, kind="Internal").ap()
        scr[("B", l)] = nc.dram_tensor("scrB%d" % l, [128, 64 * 128], BF16, kind="Internal").ap()
        scr[("C", l)] = nc.dram_tensor("scrC%d" % l, [128, 64 * 128], BF16, kind="Internal").ap()
        scr[("B2", l)] = nc.dram_tensor("scrB2%d" % l, [128, 64 * 128], BF16, kind="Internal").ap()

    with ExitStack() as st:
        S = Sched(nc, st)
        import os as _os
        ARENA_BYTES = int(_os.environ.get('KARENA', '209920'))
        arena = st.enter_context(nc.sbuf_tensor("arena", [128, ARENA_BYTES // 4], F32))
        mem_top = [0]

        def view_at(off, shape, dt):
            n = 1
            for s_ in shape: n *= s_
            nb = n * (2 if dt == BF16 else 4)
            assert off % 4 == 0 and nb % 4 == 0 and off + nb <= ARENA_BYTES, (off, nb)
            ap = arena[:, off // 4:(off + nb) // 4]
            if dt != F32: ap = ap.bitcast(dt)
            if len(shape) == 2:
                ap = ap.rearrange("p (a b) -> p a b", b=shape[1])
            elif len(shape) == 3:
                ap = ap.rearrange("p (a b c) -> p a b c", b=shape[1], c=shape[2])
            elif len(shape) == 4:
                ap = ap.rearrange("p (a b c d) -> p a b c d", b=shape[1], c=shape[2], d=shape[3])
            return ap

        def alloc(shape, dt):
            n = 1
            for s_ in shape: n *= s_
            nb = n * (2 if dt == BF16 else 4)
            nb = (nb + 31) // 32 * 32
            off = mem_top[0]; mem_top[0] += nb
            return view_at(off, shape, dt), off

        kint_t = st.enter_context(nc.sbuf_tensor("kint_t", [128, 32], I32))
        psum = [st.enter_context(nc.psum_tensor("ps%d" % i, [128, 512], F32)) for i in range(8)]
        psres = [Res("ps%d" % i, excl=True) for i in range(8)]
        pidx = [0]

        def bank():
            i = pidx[0]; pidx[0] = (i + 1) % 8
            return psum[i], psres[i]

        def op(eng, method, reads, writes, *args, **kw):
            return S.add(eng, lambda e: getattr(e, method)(*args, **kw), reads, writes)

        import os
        SKIP = os.environ.get("KSKIP", "")

        def dma(eng, out, in_, reads, writes, semres=None, slow=False, part_of=None):
            if slow and SKIP == "slow":
                return None
            if len(writes) == 1 and writes[0].multi is not None:
                nr_ = Res("c%d" % len(writes[0].multi)); writes[0].multi.append(nr_); writes = [nr_]
            if slow:
                return S.add(eng, lambda e: e.dma_start(out=out, in_=in_, allow_slow_non_contiguous=True), reads, writes, dma=True, semres=semres, part_of=part_of)
            return S.add(eng, lambda e: e.dma_start(out=out, in_=in_), reads, writes, dma=True, semres=semres, part_of=part_of)

        def mm(out, lhsT, rhs, start, stop, reads, writes):
            return S.add("pe", lambda e: e.matmul(out, lhsT, rhs, start=start, stop=stop), reads, writes)

        def tr(out, in_, ident, reads, writes):
            return S.add("pe", lambda e: e.transpose(out, in_, ident), reads, writes)

        h, _ = alloc([8, TMAX], F32); Rhh = [[Res("h%d_%d" % (b, t)) for t in range(3)] for b in range(8)]
        Rh_all = [r for rr_ in Rhh for r in rr_]
        cur_ci = [0]

        def rh(b, t0):
            for ti_, (a0, an) in enumerate(chunk_tiles(cur_ci[0])):
                if a0 <= t0 < a0 + an: return Rhh[b][ti_]
            raise AssertionError(t0)
        hn, off_hn = alloc([8, TMAX], BF16); Rhn = Res("hn")
        y3, off_y3 = alloc([16, TMAX], BF16); Ry3 = [Res("y3_%d" % b) for b in range(16)]
        NSLOT = int(_os.environ.get('KNSLOT', '6'))
        ring = []; Rring = []
        for i in range(NSLOT):
            v, o_ = alloc([4096], BF16); ring.append((v, o_)); Rring.append(Res("ring%d" % i))
        ridx = [0]
        WORK_BYTES = 49920
        work0 = mem_top[0]; mem_top[0] += WORK_BYTES
        stage = []; Rstage = []
        off_stage = mem_top[0]
        for i in range(2):
            v, _ = alloc([1024], F32); stage.append(v); Rstage.append(Res("stage%d" % i))
        sq = []; Rsq = []
        for i in range(2):
            v, _ = alloc([512], BF16); sq.append(v); Rsq.append(Res("sq%d" % i))
        rsb = []; Rrs = []
        for i in range(2):
            v, _ = alloc([512], F32); rsb.append(v); Rrs.append(Res("rs%d" % i))
        pg = SemGroup("params")
        identf, _ = alloc([128], F32); identb, _ = alloc([128], BF16); onesb, _ = alloc([128], BF16)
        maskf, _ = alloc([128], F32); invc, _ = alloc([16], F32)
        Rc = Res("consts"); Rc.multi = []
        gains, _ = alloc([5, 8], F32)
        bglu, _ = alloc([2, 8], F32)
        cw, _ = alloc([3, 16], F32); cb, _ = alloc([16], F32)
        pscale, _ = alloc([16], F32); pbg, _ = alloc([16], F32); pbs, _ = alloc([16], F32)
        Dg, _ = alloc([2, 64], F32)
        ArAr, _ = alloc([2, 32, 2], F32); AiS, _ = alloc([2, 32, 2], F32)
        A2rr, _ = alloc([2, 32, 2], F32); A2is, _ = alloc([2, 32, 2], F32)
        Rtab = [Res("tab0"), Res("tab1")]
        chist, _ = alloc([16, 2], F32); Rchist = Res("chist")
        phist, _ = alloc([16, 16], F32); Rphist = Res("phist")
        Scarry = None
        assert mem_top[0] <= ARENA_BYTES, mem_top[0]

        def wslot(nelem_shape):
            i = ridx[0]; ridx[0] = (i + 1) % NSLOT
            v, o_ = ring[i]
            return view_at(o_, nelem_shape, BF16), Rring[i]

        if _os.environ.get("KDUMP", ""):
            Rinit = Res("init")
            for i_ in range(0, ARENA_BYTES // 4, 8192):
                op("dve", "memset", [], [Rinit], arena[:, i_:min(i_ + 8192, ARENA_BYTES // 4)], 0.0)
            op("act", "copy", [Rinit], [Rinit], out=arena[:, 0:8], in_=arena[:, 0:8])
            op("pool", "tensor_copy", [Rinit], [Rinit], out=arena[:, 0:8], in_=arena[:, 0:8])
            S.add("pe", lambda e: e.matmul(psum[0][:, 0:8], arena[:, 0:128], arena[:, 0:8], start=True, stop=True), [Rinit], [psres[0]])
            S.add("sp", lambda e: e.dma_start(out=arena[:, 0:8], in_=arena[:, 8:16]), [Rinit], [Rinit], dma=True)
        dma("sp", identf, D["c_ident"], [], [Rc])
        dma("pool", identb, D["c_ident"], [], [Rc])
        dma("pool", onesb, D["c_ones"], [], [Rc])
        dma("sp", maskf, D["c_mask"], [], [Rc])
        dma("sp", invc, D["c_invc"], [], [Rc])
        for i, nm in enumerate(["norm0_g", "norm1_g", "norm2_g", "norm3_g", "final_g"]):
            dma("sp", gains[:, i, :], D[nm].rearrange("(b p) -> p b", p=128), [], [Rc], slow=True)
        for i, nm in enumerate(["l0_b_glu", "l3_b_glu"]):
            dma("sp", bglu[:, i, :], D[nm].rearrange("(b p) -> p b", p=128), [], [Rc], slow=True)
        for k in range(3):
            dma("sp", cw[:, k, :], D["l1_conv_w"][k].rearrange("(b p) -> p b", p=128), [], [Rc], slow=True)
        dma("sp", cb, D["l1_conv_b"].rearrange("(b p) -> p b", p=128), [], [Rc], slow=True)
        dma("sp", pscale, D["l2_scale"].rearrange("(b p) -> p b", p=128), [], [Rc], slow=True)
        for k_ in range(4):
            dma("sp", pbg[:, 4 * k_:4 * k_ + 4], D["l2_b_grp"][k_].rearrange("(b p) -> p b", p=128), [], [Rc], slow=True)
        for li, nm in enumerate(["l0_d_skip", "l3_d_skip"]):
            for tau in range(8):
                dma("sp", Dg[tau * 16:(tau + 1) * 16, li, :], D[nm].rearrange("(g i) -> i g", i=16), [], [Rc], slow=True)
        Rpbs = Res("pbs")
        op("dve", "tensor_tensor", [Rc], [Rpbs], out=pbs, in0=pbg, in1=pscale, op=ALU.mult)
        op("dve", "memset", [], [Rphist], phist, 0.0)
        op("dve", "memset", [], [Rchist], chist, 0.0)

        KSTOP = _os.environ.get("KSTOP", "")
        if KSTOP == "consts": S.frozen = True
        s5layers = [l for l in layers if l in (0, 3)]
        Rscr = {k: Res("scr%s%d" % k) for k in scr}
        if SKIP == "arena":
            pass

        PRO_RES = {}

        def s5_prologue(l):
            S.phase = "pro%d" % l
            li = 0 if l == 0 else 1
            pre = "l%d_" % l
            base = [0]

            def A(shape, dt=F32):
                n = 1
                for s_ in shape: n *= s_
                nb = (n * (2 if dt == BF16 else 4) + 31) // 32 * 32
                off = base[0]; base[0] += nb
                assert base[0] <= work0 + WORK_BYTES
                return view_at(off, shape, dt)
            Rp = PRO_RES

            def R(n):
                if n not in Rp: Rp[n] = Res("p_%s" % n)
                return Rp[n]
            lamR = A([64]); lamI = A([64]); ldt = A([1]); ldtb = A([64]); selG = A([2, 32]); selC = A([2, 64])
            dma("sp", lamR[0:64], D[pre + "lam_re"], [], [R("lamR")])
            dma("sp", lamI[0:64], D[pre + "lam_im"], [], [R("lamI")])
            dma("sp", ldt[0:64], D[pre + "log_dt"].rearrange("(g o) -> g o", o=1), [], [R("ldt")])
            dma("sp", selG[0:64], D["c_selG"], [], [R("selG")])
            dma("sp", selC, D["c_selC"], [], [R("selC")])
            op("dve", "tensor_copy", [R("ldt")], [R("ldtb")], out=ldtb[0:64], in_=ldt[0:64, 0:1].to_broadcast([64, 64]))
            pb_, pr_ = bank()
            for par in range(2):
                rows = slice(par * 64, par * 64 + 64)
                mm(pb_[rows, 0:32], lamR[0:64], selG[0:64, par, :], True, True, [R("lamR"), R("selG")], [pr_])
                mm(pb_[rows, 32:64], lamI[0:64], selG[0:64, par, :], True, True, [R("lamI"), R("selG")], [pr_])
                mm(pb_[rows, 64:96], ldtb[0:64], selG[0:64, par, :], True, True, [R("ldtb"), R("selG")], [pr_])
            sm = A([24, 32])
            Rsm = R("sm")
            lr, li_, ld, dt, x1, mag, ang, v_, kf, r_, m_, sn, cs, ar, ai, am1, den, t_, kr, ki = [sm[:, i, :] for i in range(20)]
            kint = kint_t[:, :]; _ = A([32], I32)
            op("act", "copy", [pr_], [Rsm], out=sm[:, 0:3, :], in_=pb_[:, 0:96].rearrange("p (a b) -> p a b", b=32))

            def dv(method, *a, **k):
                return op("dve", method, [Rsm], [Rsm], *a, **k)

            def ac(*a, **k):
                return op("act", "activation", [Rsm], [Rsm], *a, **k)
            ac(out=dt, in_=ld, func=AF.Exp)
            dv("tensor_tensor", out=x1, in0=lr, in1=dt, op=ALU.mult)
            ac(out=mag, in_=x1, func=AF.Exp)
            dv("tensor_tensor", out=ang, in0=li_, in1=dt, op=ALU.mult)
            ac(out=sn, in_=ang, func=AF.Sin, scale=1.0 / 8)
            ac(out=v_, in_=ang, func=AF.Sin, scale=1.0 / 16)
            dv("tensor_tensor", out=v_, in0=v_, in1=v_, op=ALU.mult)
            dv("tensor_scalar", out=cs, in0=v_, scalar1=-2.0, scalar2=1.0, op0=ALU.mult, op1=ALU.add)
            for _d in range(3):
                dv("tensor_tensor", out=kf, in0=cs, in1=cs, op=ALU.mult)
                dv("tensor_tensor", out=r_, in0=sn, in1=sn, op=ALU.mult)
                dv("scalar_tensor_tensor", out=sn, in0=cs, scalar=2.0, in1=sn, op0=ALU.mult, op1=ALU.mult)
                dv("tensor_tensor", out=cs, in0=kf, in1=r_, op=ALU.subtract)
            dv("tensor_tensor", out=ar, in0=mag, in1=cs, op=ALU.mult)
            dv("tensor_tensor", out=ai, in0=mag, in1=sn, op=ALU.mult)
            dv("tensor_scalar", out=am1, in0=ar, scalar1=-1.0, scalar2=None, op0=ALU.add)
            dv("tensor_tensor", out=den, in0=lr, in1=lr, op=ALU.mult)
            dv("tensor_tensor", out=t_, in0=li_, in1=li_, op=ALU.mult)
            dv("tensor_tensor", out=den, in0=den, in1=t_, op=ALU.add)
            dv("reciprocal", out=den, in_=den)
            dv("tensor_tensor", out=kr, in0=am1, in1=lr, op=ALU.mult)
            dv("tensor_tensor", out=t_, in0=ai, in1=li_, op=ALU.mult)
            dv("tensor_tensor", out=kr, in0=kr, in1=t_, op=ALU.add)
            dv("tensor_tensor", out=kr, in0=kr, in1=den, op=ALU.mult)
            dv("tensor_tensor", out=ki, in0=ai, in1=lr, op=ALU.mult)
            dv("tensor_tensor", out=t_, in0=am1, in1=li_, op=ALU.mult)
            dv("tensor_tensor", out=ki, in0=ki, in1=t_, op=ALU.subtract)
            dv("tensor_tensor", out=ki, in0=ki, in1=den, op=ALU.mult)
            EPr = A([32, 9]); EPi = A([32, 9]); ERr = A([32, 8]); ERi = A([32, 8])
            dv("memset", EPr[:, :, 0], 1.0); dv("memset", EPi[:, :, 0], 0.0)
            dv("tensor_copy", out=EPr[:, :, 1], in_=ar); dv("tensor_copy", out=EPi[:, :, 1], in_=ai)
            for q in range(2, 9):
                dv("tensor_tensor", out=EPr[:, :, q], in0=EPr[:, :, q - 1], in1=ar, op=ALU.mult)
                dv("tensor_tensor", out=t_, in0=EPi[:, :, q - 1], in1=ai, op=ALU.mult)
                dv("tensor_tensor", out=EPr[:, :, q], in0=EPr[:, :, q], in1=t_, op=ALU.subtract)
                dv("tensor_tensor", out=EPi[:, :, q], in0=EPr[:, :, q - 1], in1=ai, op=ALU.mult)
                dv("tensor_tensor", out=t_, in0=EPi[:, :, q - 1], in1=ar, op=ALU.mult)
                dv("tensor_tensor", out=EPi[:, :, q], in0=EPi[:, :, q], in1=t_, op=ALU.add)
            for tau in range(8):
                dv("tensor_copy", out=ERr[:, :, tau], in_=EPr[:, :, 7 - tau])
                dv("tensor_copy", out=ERi[:, :, tau], in_=EPi[:, :, 7 - tau])
            Ir = sm[:, 20, :]; Ii = sm[:, 21, :]; n8 = sm[:, 22, :]
            dv("tensor_tensor", out=n8, in0=EPr[:, :, 8], in1=EPr[:, :, 8], op=ALU.mult)
            dv("tensor_tensor", out=t_, in0=EPi[:, :, 8], in1=EPi[:, :, 8], op=ALU.mult)
            dv("tensor_tensor", out=n8, in0=n8, in1=t_, op=ALU.add)
            dv("reciprocal", out=n8, in_=n8)
            dv("tensor_tensor", out=Ir, in0=EPr[:, :, 8], in1=n8, op=ALU.mult)
            dv("scalar_tensor_tensor", out=Ii, in0=EPi[:, :, 8], scalar=-1.0, in1=n8, op0=ALU.mult, op1=ALU.mult)
            op("dve", "tensor_copy", [Rsm], [Rtab[li]], out=ArAr[:, li, :, 0], in_=EPr[:, :, 8])
            op("dve", "tensor_copy", [Rsm], [Rtab[li]], out=ArAr[:, li, :, 1], in_=EPr[:, :, 8])
            op("dve", "tensor_scalar", [Rsm], [Rtab[li]], out=AiS[:, li, :, 0], in0=EPi[:, :, 8], scalar1=-1.0, scalar2=None, op0=ALU.mult)
            op("dve", "tensor_copy", [Rsm], [Rtab[li]], out=AiS[:, li, :, 1], in_=EPi[:, :, 8])
            dv("tensor_tensor", out=kf, in0=EPr[:, :, 8], in1=EPr[:, :, 8], op=ALU.mult)
            dv("tensor_tensor", out=r_, in0=EPi[:, :, 8], in1=EPi[:, :, 8], op=ALU.mult)
            dv("tensor_tensor", out=kf, in0=kf, in1=r_, op=ALU.subtract)
            dv("scalar_tensor_tensor", out=r_, in0=EPr[:, :, 8], scalar=2.0, in1=EPi[:, :, 8], op0=ALU.mult, op1=ALU.mult)
            op("dve", "tensor_copy", [Rsm], [Rtab[li]], out=A2rr[:, li, :, 0], in_=kf)
            op("dve", "tensor_copy", [Rsm], [Rtab[li]], out=A2rr[:, li, :, 1], in_=kf)
            op("dve", "tensor_scalar", [Rsm], [Rtab[li]], out=A2is[:, li, :, 0], in0=r_, scalar1=-1.0, scalar2=None, op0=ALU.mult)
            op("dve", "tensor_copy", [Rsm], [Rtab[li]], out=A2is[:, li, :, 1], in_=r_)
            Bre = A([32, 16]); Bim = A([32, 16]); bbr = A([32, 16]); bbi = A([32, 16]); tb = A([32, 16])
            Cre = A([32, 16]); Cim = A([32, 16])
            for par in range(2):
                rows = slice(par * 64, par * 64 + 64)
                for nm, dst in (("b_re", Bre), ("b_im", Bim)):
                    src = D[pre + nm].rearrange("(g2 q) p i -> q p g2 i", q=2)[par]
                    dma("sp", dst[rows], src, [], [R("Bsrc")])
            Xt = A([8, 64])
            for nm, dst in (("c_re", Cre), ("c_im", Cim)):
                srcv = D[pre + nm].rearrange("(a gl) o p -> a (gl o) p", gl=8)
                for a in range(8):
                    dma("sp", Xt[:, a, :], srcv[a], [], [R("Xt%d" % a)])
                pb_, pr_ = bank()
                for a in range(8):
                    for par in range(2):
                        rows = slice(par * 64, par * 64 + 64)
                        mm(pb_[rows, a * 64:(a + 1) * 64], Xt[:, a, :], selC[:, par, :], True, True, [R("Xt%d" % a), R("selC")], [pr_])
                op("act", "copy", [pr_], [R("Csrc")], out=dst, in_=pb_[:, 0:512].rearrange("p (a b) -> p a b", b=16))
            RB = R("bb")
            krb = kr.unsqueeze(2).to_broadcast([128, 32, 16]); kib = ki.unsqueeze(2).to_broadcast([128, 32, 16])
            op("dve", "tensor_tensor", [Rsm, R("Bsrc")], [RB], out=bbr, in0=Bre, in1=krb, op=ALU.mult)
            op("dve", "tensor_tensor", [Rsm, R("Bsrc")], [RB], out=tb, in0=Bim, in1=kib, op=ALU.mult)
            op("dve", "tensor_tensor", [RB], [RB], out=bbr, in0=bbr, in1=tb, op=ALU.subtract)
            op("dve", "tensor_tensor", [Rsm, R("Bsrc")], [RB], out=bbi, in0=Bim, in1=krb, op=ALU.mult)
            op("dve", "tensor_tensor", [Rsm, R("Bsrc")], [RB], out=tb, in0=Bre, in1=kib, op=ALU.mult)
            op("dve", "tensor_tensor", [RB], [RB], out=bbi, in0=bbi, in1=tb, op=ALU.add)
            Bqr = A([16, 8, 16]); Bqi = A([16, 8, 16]); Bmr = A([16, 8, 16]); Bmi = A([16, 8, 16])
            C1r = A([16, 8, 16]); C1i = A([16, 8, 16]); t1 = A([16, 8, 16]); t2 = A([16, 8, 16])
            Tsb = A([32, 128], BF16); Bsb = A([32, 128], BF16); Csb = A([16, 2, 128], BF16)
            B2r = A([16, 8, 16]); B2i = A([16, 8, 16]); B2sb = A([32, 128], BF16)
            tmpT = [A([128]) for _ in range(4)]
            RT = [R("tmpT%d" % i_) for i_ in range(4)]
            for hf in range(2):
                gs = slice(hf * 16, hf * 16 + 16)
                sh = [128, 16, 8, 16]
                RA = R("big")
                ErB = ERr[:, gs, :].unsqueeze(3).to_broadcast(sh); EiB = ERi[:, gs, :].unsqueeze(3).to_broadcast(sh)
                brB = bbr[:, gs, :].unsqueeze(2).to_broadcast(sh); biB = bbi[:, gs, :].unsqueeze(2).to_broadcast(sh)

                def big(eng, method, *a, **k):
                    return op(eng, method, [Rsm, RB, R("Csrc"), RA], [RA], *a, **k)
                big("dve", "tensor_tensor", out=Bqr, in0=ErB, in1=brB, op=ALU.mult)
                big("dve", "tensor_tensor", out=t1, in0=EiB, in1=biB, op=ALU.mult)
                big("dve", "tensor_tensor", out=Bqr, in0=Bqr, in1=t1, op=ALU.subtract)
                big("dve", "tensor_tensor", out=Bqi, in0=ErB, in1=biB, op=ALU.mult)
                big("dve", "tensor_tensor", out=t1, in0=EiB, in1=brB, op=ALU.mult)
                big("dve", "tensor_tensor", out=Bqi, in0=Bqi, in1=t1, op=ALU.add)
                A8r = EPr[:, gs, 8].unsqueeze(2).unsqueeze(3).to_broadcast(sh); A8i = EPi[:, gs, 8].unsqueeze(2).unsqueeze(3).to_broadcast(sh)
                big("dve", "tensor_tensor", out=t1, in0=Bqr, in1=A8r, op=ALU.mult)
                big("dve", "tensor_tensor", out=t2, in0=Bqi, in1=A8i, op=ALU.mult)
                big("dve", "tensor_tensor", out=B2r, in0=t1, in1=t2, op=ALU.subtract)
                big("dve", "tensor_tensor", out=t1, in0=Bqi, in1=A8r, op=ALU.mult)
                big("dve", "tensor_tensor", out=t2, in0=Bqr, in1=A8i, op=ALU.mult)
                big("dve", "tensor_tensor", out=B2i, in0=t1, in1=t2, op=ALU.add)
                IrB = Ir[:, gs].unsqueeze(2).unsqueeze(3).to_broadcast(sh); IiB = Ii[:, gs].unsqueeze(2).unsqueeze(3).to_broadcast(sh)
                big("dve", "tensor_tensor", out=Bmr, in0=Bqr, in1=IrB, op=ALU.mult)
                big("dve", "tensor_tensor", out=t1, in0=Bqi, in1=IiB, op=ALU.mult)
                big("dve", "tensor_tensor", out=Bmr, in0=Bmr, in1=t1, op=ALU.subtract)
                big("dve", "tensor_tensor", out=Bmi, in0=Bqi, in1=IrB, op=ALU.mult)
                big("dve", "tensor_tensor", out=t1, in0=Bqr, in1=IiB, op=ALU.mult)
                big("dve", "tensor_tensor", out=Bmi, in0=Bmi, in1=t1, op=ALU.add)
                E1r = EPr[:, gs, 1:9].unsqueeze(3).to_broadcast(sh); E1i = EPi[:, gs, 1:9].unsqueeze(3).to_broadcast(sh)
                crB = Cre[:, gs, :].unsqueeze(2).to_broadcast(sh); ciB = Cim[:, gs, :].unsqueeze(2).to_broadcast(sh)
                big("dve", "tensor_tensor", out=C1r, in0=E1r, in1=crB, op=ALU.mult)
                big("dve", "tensor_tensor", out=t1, in0=E1i, in1=ciB, op=ALU.mult)
                big("dve", "tensor_tensor", out=C1r, in0=C1r, in1=t1, op=ALU.subtract)
                big("dve", "tensor_tensor", out=t1, in0=E1i, in1=crB, op=ALU.mult)
                big("dve", "tensor_tensor", out=t2, in0=E1r, in1=ciB, op=ALU.mult)
                big("dve", "scalar_tensor_tensor", out=C1i, in0=t1, scalar=-1.0, in1=t2, op0=ALU.mult, op1=ALU.subtract)
                Rout = R("outsb"); RoutT = R("outsbT")
                op("act", "copy", [RA], [Rout], out=Csb[:, :, 0, :], in_=C1r.rearrange("p a b c -> p a (b c)"))
                op("act", "copy", [RA], [Rout], out=Csb[:, :, 1, :], in_=C1i.rearrange("p a b c -> p a (b c)"))
                for gl in range(32):
                    g = hf * 32 + gl; g2l = gl // 2; par = gl % 2
                    rows = slice(par * 64, par * 64 + 64)
                    pb_, pr_ = bank()
                    mm(pb_[:, 0:128], Bmr[rows, g2l].rearrange("p a b -> p (a b)"), C1r[rows, g2l].rearrange("p a b -> p (a b)"), True, False, [RA], [pr_])
                    mm(pb_[:, 0:128], Bmi[rows, g2l].rearrange("p a b -> p (a b)"), C1i[rows, g2l].rearrange("p a b -> p (a b)"), False, True, [RA], [pr_])
                    tt = tmpT[gl % 4]; rt = RT[gl % 4]
                    op("dve", "tensor_tensor", [pr_, Rc], [rt], out=tt, in0=pb_[:, 0:128], in1=maskf, op=ALU.mult)
                    op("dve", "scalar_tensor_tensor", [rt, Rc], [RoutT], out=Tsb[:, gl, :], in0=identf, scalar=Dg[:, li, g:g + 1], in1=tt, op0=ALU.mult, op1=ALU.add)
                    pb2, pr2 = bank()
                    tr(pb2[:, 0:64], Bqr[rows, g2l].rearrange("p a b -> p (a b)"), identf[rows, par * 64:par * 64 + 64], [RA, Rc], [pr2])
                    tr(pb2[:, 64:128], Bqi[rows, g2l].rearrange("p a b -> p (a b)"), identf[rows, par * 64:par * 64 + 64], [RA, Rc], [pr2])
                    op("act", "copy", [pr2], [Rout], out=Bsb[:, gl, :], in_=pb2[:, 0:128])
                    pb3, pr3 = bank()
                    tr(pb3[:, 0:64], B2r[rows, g2l].rearrange("p a b -> p (a b)"), identf[rows, par * 64:par * 64 + 64], [RA, Rc], [pr3])
                    tr(pb3[:, 64:128], B2i[rows, g2l].rearrange("p a b -> p (a b)"), identf[rows, par * 64:par * 64 + 64], [RA, Rc], [pr3])
                    op("act", "copy", [pr3], [Rout], out=B2sb[:, gl, :], in_=pb3[:, 0:128])
                cols = slice(hf * 4096, hf * 4096 + 4096)
                dma("sp", scr[("T", l)][:, cols], Tsb.rearrange("p a b -> p (a b)"), [RoutT], [Rscr[("T", l)]])
                dma("sp", scr[("B", l)][:, cols], Bsb.rearrange("p a b -> p (a b)"), [Rout], [Rscr[("B", l)]])
                dma("sp", scr[("B2", l)][:, cols], B2sb.rearrange("p a b -> p (a b)"), [Rout], [Rscr[("B2", l)]])
                dma("sp", scr[("C", l)][:, cols], Csb.rearrange("p a b c -> p (a b c)"), [Rout], [Rscr[("C", l)]])

        for l in s5layers:
            s5_prologue(l)
        if KSTOP == "pro": S.frozen = True

        u_tm = view_at(work0, [8, 1024], BF16); Rutm = [Res("u_tm%d" % i) for i in range(8)]
        u_tmu = view_at(work0, [64, 8, 16], BF16)
        Sst = view_at(work0 + 16384, [32, 2, 131], F32); RSh = [Res("Sst0"), Res("Sst1")]
        Uv = view_at(off_y3 + 16640, [64, 128], BF16); RU = [Res("U%d" % i) for i in range(16)]
        yfm = view_at(off_y3 + 16640, [8, TMAX], BF16)
        Sb = view_at(off_stage, [16, 2, 128], BF16); RSb = Res("Sb")
        s5work = Rutm + RSh
        UA, _ = alloc([64, 2], BF16); RUA = Res("UA")
        yfmA, _ = alloc([8, 16], BF16); RyA = Res("yfmA")
        SbA, _ = alloc([32, 2, 2], BF16); RSbA = Res("SbA")
        scan_t1, _ = alloc([32, 2], F32); scan_t2, _ = alloc([32, 2], F32)
        scan_t1b, _ = alloc([32, 2], F32); scan_t2b, _ = alloc([32, 2], F32)
        Rscan1 = [Res("scan1_0"), Res("scan1_1")]; Rscan2 = [Res("scan2_0"), Res("scan2_1")]
        sig = []; Rsig = []
        for i in range(2):
            v, _ = alloc([512], F32); sig.append(v); Rsig.append(Res("sig%d" % i))
        carry, _ = alloc([2, 32, 2], F32); Rcarry = [Res("carry0"), Res("carry1")]
        assert mem_top[0] <= ARENA_BYTES, mem_top[0]
        o = work0
        tcg = [view_at(o + i * 2048, [512], F32) for i in range(2)]; o += 4096
        hcb = [view_at(o + i * 4192, [1048], F32) for i in range(2)]; o += 8384
        c1b = [view_at(o + i * 4160, [TMAX], F32) for i in range(2)]; o += 8320
        szb = [view_at(o + i * 2048, [512], F32) for i in range(2)]; o += 4096
        yvb = [view_at(o + i * 2048, [512], F32) for i in range(2)]; o += 4096
        Rtcg = [Res("tcg%d" % i) for i in range(2)]; Rhc = [Res("hc%d" % i) for i in range(2)]
        Rc1 = [Res("c1%d" % i) for i in range(2)]; Rsz = [Res("sz%d" % i) for i in range(2)]; Ryv = [Res("yv%d" % i) for i in range(2)]
        Rhch = [Res("hch%d" % i) for i in range(2)]
        convwork = Rtcg + Rhc + Rhch + Rc1 + Rsz + Ryv
        o = work0
        ubb = [view_at(o + i * 4224, [1056], F32) for i in range(2)]; o += 8448
        lvb = [view_at(o + i * 4224, [1056], F32) for i in range(4)]; o += 16896
        mixb = [view_at(o + i * 8320, [4, TMAX], BF16) for i in range(2)]; o += 16640
        yvp = [view_at(o + i * 2048, [512], F32) for i in range(2)]; o += 4096
        tfix = view_at(o, [16], F32); o += 64
        assert o <= work0 + WORK_BYTES
        Rub = [Res("ub%d" % i) for i in range(2)]; Rlv = [Res("lv%d" % i) for i in range(4)]
        Rmix = [Res("mix%d" % i) for i in range(2)]; Ryvp = [Res("yvp%d" % i) for i in range(2)]; Rtfix = Res("tfix")
        poolwork = Rub + Rlv + Rmix + Ryvp + [Rtfix]
        cur_work = [None]
        global _DBG_OFFS
        _DBG_OFFS = dict(off_hn=off_hn, off_y3=off_y3, work0=work0, off_stage=off_stage)
        S.realias(list(PRO_RES.values()), Rh_all + [Rhn] + Ry3 + Rring + s5work + convwork + poolwork)

        def use_work(kind):
            new = {"s5": s5work, "conv": convwork, "pool": poolwork}[kind]
            if cur_work[0] is not None and cur_work[0] is not new:
                S.realias(cur_work[0], new)
            cur_work[0] = new

        def load_w(eng, src_ap, shape, reads=()):
            v, r = wslot(shape)
            dma(eng, v, src_ap, list(reads), [r])
            return v, r

        def load_w_parts(eng, shape, partfn, nparts, reads=()):
            v, r = wslot(shape)
            prev = None
            for i in range(nparts):
                d_, s_ = partfn(v, i)
                prev = dma(eng, d_, s_, list(reads), [r], part_of=prev)
            return v, r

        def rmsnorm_stats(t0, tn, k):
            pb_, pr_ = bank()
            for b in range(8):
                j = b % 2
                op("act", "activation", [rh(b, t0)], [Rsq[j]], out=sq[j][:, 0:tn], in_=h[:, b, t0:t0 + tn], func=AF.Square)
                mm(pb_[:, 0:tn], onesb, sq[j][:, 0:tn], b == 0, b == 7, [Rsq[j], Rc], [pr_])
            op("act", "activation", [pr_], [Rrs[k]], out=rsb[k][:, 0:tn], in_=pb_[:, 0:tn], func=AF.Sqrt, bias=EPS, scale=1.0 / DM)
            op("dve", "reciprocal", [Rrs[k]], [Rrs[k]], out=rsb[k][:, 0:tn], in_=rsb[k][:, 0:tn])

        nrm_k = [0]

        def norm_to_hn(ci, gi):
            for (t0, tn) in chunk_tiles(ci):
                k = nrm_k[0]; nrm_k[0] ^= 1
                rmsnorm_stats(t0, tn, k)
                for b in range(8):
                    op("dve", "scalar_tensor_tensor", [rh(b, t0), Rrs[k], Rc], [Rhn], out=hn[:, b, t0:t0 + tn], in0=h[:, b, t0:t0 + tn],
                       scalar=gains[:, gi, b:b + 1], in1=rsb[k][:, 0:tn], op0=ALU.mult, op1=ALU.mult)

        def out_proj(ci, wname, nk):
            colw = 4096 // nk
            nslots = DM // colw
            slots = []
            for s_ in range(nslots):
                slots.append(load_w("pool", D[wname][:, s_ * colw:(s_ + 1) * colw].rearrange("(k p) c -> p k c", p=128), [nk, colw]))
            for (t0, tn) in chunk_tiles(ci):
                for s_ in range(nslots):
                    wv, wr = slots[s_]
                    for db in range(colw // 128):
                        dblk = s_ * (colw // 128) + db
                        pb_, pr_ = bank()
                        for k in range(nk):
                            mm(pb_[:, 0:tn], wv[:, k, db * 128:(db + 1) * 128], y3[:, k, t0:t0 + tn], k == 0, k == nk - 1, [wr, Ry3[k]], [pr_])
                        op("dve", "tensor_tensor", [pr_, rh(dblk, t0)], [rh(dblk, t0)], out=h[:, dblk, t0:t0 + tn], in0=pb_[:, 0:tn], in1=h[:, dblk, t0:t0 + tn], op=ALU.add)

        stg_i = [0]

        def load_chunk(ci):
            S.phase = "load"
            c0, T = CHUNKS[ci]
            tiles = []
            if ci == 0:
                tiles.append(("meta", 0, 16, 0))
                for j in range(8): tiles.append(("x", j * 128, 128, 16 + j * 128))
            else:
                for j in range(8): tiles.append(("x", c0 - NMETA + j * 128, 128, j * 128))
            for ti_, (kind, r0, nr, col) in enumerate(tiles):
                if KSTOP == "load%d" % ti_: S.frozen = True
                si = stg_i[0]; stg_i[0] ^= 1
                src = D["meta_tokens"] if kind == "meta" else D["x"][r0:r0 + nr, :]
                dma("sp", stage[si][0:nr, :], src, [], [Rstage[si]])
                for half in range(2):
                    pb_, pr_ = bank()
                    for q in range(4):
                        b = half * 4 + q
                        tr(pb_[:, q * 128:q * 128 + nr], stage[si][0:nr, b * 128:(b + 1) * 128], identf[0:nr, 0:nr], [Rstage[si], Rc], [pr_])
                    for q in range(4):
                        b = half * 4 + q
                        op("act" if half == 0 else "dve", "copy" if half == 0 else "tensor_copy", [pr_], [rh(b, col)],
                           out=h[:, b, col:col + nr], in_=pb_[:, q * 128:q * 128 + nr])

        def final_store(ci):
            S.phase = "final"
            c0, T = CHUNKS[ci]
            hf_, _o = None, None
            for (t0, tn) in chunk_tiles(ci):
                if ci == 0 and t0 == 0: continue
                k = nrm_k[0]; nrm_k[0] ^= 1
                rmsnorm_stats(t0, tn, k)
                if KSTOP == "stats": S.frozen = True
                hf = view_at(off_y3, [8, 512], F32)
                for b in range(8):
                    op("dve", "scalar_tensor_tensor", [rh(b, t0), Rrs[k], Rc], Ry3[0:8], out=hf[:, b, 0:tn], in0=h[:, b, t0:t0 + tn],
                       scalar=gains[:, 4, b:b + 1], in1=rsb[k][:, 0:tn], op0=ALU.mult, op1=ALU.mult)
                for sub in range(tn // 128):
                    si = stg_i[0]; stg_i[0] ^= 1
                    for half in range(2):
                        pb_, pr_ = bank()
                        for q in range(4):
                            b = half * 4 + q
                            tr(pb_[:, q * 128:(q + 1) * 128], hf[:, b, sub * 128:(sub + 1) * 128], identf, Ry3[0:8] + [Rc], [pr_])
                        op("act" if si == 0 else "dve", "copy" if si == 0 else "tensor_copy", [pr_], [Rstage[si]],
                           out=stage[si][:, half * 512:(half + 1) * 512], in_=pb_[:, 0:512])
                    row0 = c0 + t0 + sub * 128 - NMETA
                    dma("sp", out_d[row0:row0 + 128, :], stage[si], [Rstage[si]], [Res("outd")], semres=Rstage[si])

        def layer_conv(ci):
            c0, T = CHUNKS[ci]
            use_work("conv")
            norm_to_hn(ci, 1)
            for e in range(16):
                srcw = D["l1_w_in"].rearrange("(k p) (q c) -> p k q c", p=128, q=4)
                wv, wr = load_w_parts("pool", [8, 4, 128], lambda v, i, e=e, srcw=srcw: (v[:, :, i, :], srcw[:, :, i, e * 128:(e + 1) * 128]), 4)
                j = e % 2
                hc = hcb[j]; c1 = c1b[j]
                op("act", "copy", [Rchist], [Rhch[j]], out=hc[:, 0:2], in_=chist[:, e, :])
                for (t0, tn) in chunk_tiles(ci):
                    banks = []
                    for part in (1, 2, 0, 3):
                        pb_, pr_ = bank()
                        for b in range(8):
                            mm(pb_[:, 0:tn], wv[:, b, part, :], hn[:, b, t0:t0 + tn], b == 0, b == 7, [wr, Rhn], [pr_])
                        banks.append((pb_, pr_))
                    (pcg, rcg), (pv, rv), (pbg_, rbg), (pz, rz) = banks
                    op("act", "copy", [rcg], [Rtcg[j]], out=tcg[j][:, 0:tn], in_=pcg[:, 0:tn])
                    op("dve", "tensor_tensor", [Rtcg[j], rv], [Rhc[j]], out=hc[:, 2 + t0:2 + t0 + tn], in0=pv[:, 0:tn], in1=tcg[j][:, 0:tn], op=ALU.mult)
                    op("act", "activation", [rz], [Rsz[j]], out=szb[j][:, 0:tn], in_=pz[:, 0:tn], func=AF.Silu)
                    op("act", "activation", [Rhc[j], Rc], [Rc1[j]], out=c1[:, t0:t0 + tn], in_=hc[:, 2 + t0:2 + t0 + tn], func=AF.Identity,
                       bias=cb[:, e:e + 1], scale=cw[:, 2, e:e + 1])
                    op("dve", "scalar_tensor_tensor", [Rhc[j], Rhch[j], Rc1[j], Rc], [Rc1[j]], out=c1[:, t0:t0 + tn], in0=hc[:, 1 + t0:1 + t0 + tn],
                       scalar=cw[:, 1, e:e + 1], in1=c1[:, t0:t0 + tn], op0=ALU.mult, op1=ALU.add)
                    op("dve", "scalar_tensor_tensor", [Rhc[j], Rhch[j], Rc1[j], Rc], [Rc1[j]], out=c1[:, t0:t0 + tn], in0=hc[:, t0:t0 + tn],
                       scalar=cw[:, 0, e:e + 1], in1=c1[:, t0:t0 + tn], op0=ALU.mult, op1=ALU.add)
                    op("dve", "tensor_tensor", [Rc1[j], rbg], [Ryv[j]], out=yvb[j][:, 0:tn], in0=pbg_[:, 0:tn], in1=c1[:, t0:t0 + tn], op=ALU.mult)
                    op("dve", "tensor_tensor", [Ryv[j], Rsz[j]], [Ry3[e]], out=y3[:, e, t0:t0 + tn], in0=yvb[j][:, 0:tn], in1=szb[j][:, 0:tn], op=ALU.mult)
                op("act", "copy", [Rhc[j]], [Rchist], out=chist[:, e, :], in_=hc[:, T:T + 2])
            out_proj(ci, "l1_w_out", 16)

        def layer_pool(ci):
            c0, T = CHUNKS[ci]
            use_work("pool")
            norm_to_hn(ci, 2)
            grp_calls = []
            def GRP(k):
                mix = mixb[k % 2]; rmix = Rmix[k % 2]
                wv, wr = load_w("pool", D["l2_w_grp"][k].rearrange("(kk p) c -> p kk c", p=128), [4, 512])
                for eo in range(4):
                    e = 4 * k + eo
                    for (t0, tn) in chunk_tiles(ci):
                        pb_, pr_ = bank()
                        for ei in range(4):
                            mm(pb_[:, 0:tn], wv[:, ei, eo * 128:(eo + 1) * 128], mix[:, ei, t0:t0 + tn], ei == 0, ei == 3, [wr, rmix], [pr_])
                        jj = eo % 2
                        op("act", "activation", [pr_, Rc, Rpbs], [Ryvp[jj]], out=yvp[jj][:, 0:tn], in_=pb_[:, 0:tn], func=AF.Identity,
                           bias=pbs[:, e:e + 1], scale=pscale[:, e:e + 1])
                        op("dve", "tensor_tensor", [Ryvp[jj], Ry3[e]], [Ry3[e]], out=y3[:, e, t0:t0 + tn], in0=yvp[jj][:, 0:tn], in1=y3[:, e, t0:t0 + tn], op=ALU.mult)

            for k in range(4):
                w = 2 << k
                mix = mixb[k % 2]; rmix = Rmix[k % 2]
                for epair in range(2):
                    e0 = 4 * k + 2 * epair
                    srcw = D["l2_w_in"].rearrange("(kk p) (q c) -> p kk q c", p=128, q=2)
                    wv, wr = load_w_parts("pool", [8, 2, 256], lambda v, i, e0=e0, srcw=srcw: (v[:, :, i, :], srcw[:, :, i, e0 * 128:(e0 + 2) * 128]), 2)
                    for el in range(2):
                        e = e0 + el; ei = 2 * epair + el
                        j = e % 2
                        ub = ubb[j]
                        op("act", "copy", [Rphist], [Rub[j]], out=ub[:, 0:16], in_=phist[:, e, :])
                        for (t0, tn) in chunk_tiles(ci):
                            pb_, pr_ = bank()
                            for b in range(8):
                                mm(pb_[:, 0:tn], wv[:, b, 0, el * 128:(el + 1) * 128], hn[:, b, t0:t0 + tn], b == 0, b == 7, [wr, Rhn], [pr_])
                            op("act", "copy", [pr_], [Rub[j]], out=ub[:, 16 + t0:16 + t0 + tn], in_=pb_[:, 0:tn])
                            pb2, pr2 = bank()
                            for b in range(8):
                                mm(pb2[:, 0:tn], wv[:, b, 1, el * 128:(el + 1) * 128], hn[:, b, t0:t0 + tn], b == 0, b == 7, [wr, Rhn], [pr2])
                            op("act", "activation", [pr2], [Ry3[e]], out=y3[:, e, t0:t0 + tn], in_=pb2[:, 0:tn], func=AF.Silu)
                        op("act", "copy", [Rub[j]], [Rphist], out=phist[:, e, :], in_=ub[:, T:T + 16])
                        cur = ub; rcur = Rub[j]; lo = 0
                        NE = 16 + T
                        for lev in range(k + 1):
                            sft = 1 << lev
                            nxt = lvb[lev % 2 + 2 * j]
                            rn = Rlv[lev % 2 + 2 * j]
                            nlo = lo + sft
                            op("dve", "tensor_tensor", [rcur], [rn], out=nxt[:, nlo:NE], in0=cur[:, nlo:NE], in1=cur[:, nlo - sft:NE - sft], op=ALU.add)
                            cur = nxt; rcur = rn; lo = nlo
                        op("dve", "scalar_tensor_tensor", [rcur, Rub[j]], [rmix], out=mix[:, ei, 0:T], in0=cur[:, 16:16 + T], scalar=1.0 / w,
                           in1=ub[:, 16:16 + T], op0=ALU.mult, op1=ALU.subtract)
                        if ci == 0:
                            op("dve", "tensor_tensor", [rcur, Rc], [Rtfix], out=tfix[:, 0:w - 1], in0=cur[:, 16:16 + w - 1], in1=invc[:, 0:w - 1], op=ALU.mult)
                            op("dve", "tensor_tensor", [Rtfix, Rub[j]], [rmix], out=mix[:, ei, 0:w - 1], in0=tfix[:, 0:w - 1], in1=ub[:, 16:16 + w - 1], op=ALU.subtract)
                if k >= 1:
                    grp_calls.append(k - 1); GRP(k - 1)
            GRP(3)
            out_proj(ci, "l2_w_out", 16)

        def layer_s5_full(ci, l):
            c0, T = CHUNKS[ci]
            li = 0 if l == 0 else 1
            pre = "l%d_" % l
            S.phase = "s5norm"
            norm_to_hn(ci, l)
            pieces = chunk_pieces(ci)
            def zgate():
                S.phase = "s5zgate"
                for cs_ in range(2):
                    wv, wr = load_w("pool", D[pre + "w_in"][:, 1024 + cs_ * 512:1024 + (cs_ + 1) * 512].rearrange("(k p) c -> p k c", p=128), [8, 512])
                    for cbk in range(4):
                        cblk = cs_ * 4 + cbk
                        for (t0, tn) in chunk_tiles(ci):
                            pb_, pr_ = bank()
                            for b in range(8):
                                mm(pb_[:, 0:tn], wv[:, b, cbk * 128:(cbk + 1) * 128], hn[:, b, t0:t0 + tn], b == 0, b == 7, [wr, Rhn], [pr_])
                            op("act", "activation", [pr_], [Ry3[cblk]], out=y3[:, cblk, t0:t0 + tn], in_=pb_[:, 0:tn], func=AF.Silu)
            if KSTOP == "s5gate": S.frozen = True
            ycols = []; ncol = 0
            for (n0_, nn_) in pieces:
                ycols.append(ncol); ncol += nn_

            def piece_stage(pi_, n0, nn, stage):
                ycol = ycols[pi_]
                last_piece = (pi_ == len(pieces) - 1)
                small = nn < 128
                Uc = UA if small else Uv
                RUc = [RUA] * 16 if small else RU
                Sbc = SbA if small else Sb
                RSbc = RSbA if small else RSb
                if stage == 1:
                    S.phase = "s5u"
                    for half in range(2):
                        wv, wr = load_w("pool", D[pre + "w_in"][:, half * 512:(half + 1) * 512].rearrange("(k p) c -> p k c", p=128), [8, 512])
                        for tau in range(8):
                            pb_, pr_ = bank()
                            for b in range(8):
                                lhs = hn[:, b, 8 * n0:8 * (n0 + nn)].rearrange("p (n t) -> p n t", t=8)[:, :, tau]
                                mm(pb_[0:nn, :], lhs, wv[:, b, :], b == 0, b == 7, [wr, Rhn], [pr_])
                            op("act" if tau % 2 == 0 else "dve", "copy" if tau % 2 == 0 else "tensor_copy", [pr_], [Rutm[tau]],
                               out=u_tmu[0:nn, half * 32:(half + 1) * 32, tau, :], in_=pb_[0:nn, :].rearrange("p (g i) -> p g i", i=16))
                    if not small:
                        S.realias(Ry3[8:16], RU)
                    S.phase = "s5Utr"
                    for gq in range(16):
                        pb_, pr_ = bank()
                        pbb = pb_[:].bitcast(BF16)
                        for gl in range(4):
                            g = 4 * gq + gl
                            tr(pbb[:, gl * 128:gl * 128 + nn], u_tmu[0:nn, g].rearrange("p t i -> p (t i)"), identb[0:nn, 0:nn], Rutm + [Rc], [pr_])
                        op("dve" if gq % 2 == 0 else "act", "tensor_copy" if gq % 2 == 0 else "copy", [pr_], [RUc[gq]],
                           out=Uc[:, 4 * gq:4 * gq + 4, 0:nn], in_=pbb[:, 0:512].rearrange("p (g n) -> p g n", n=128)[:, :, 0:nn])
                    if KSTOP == "s5U" and last_piece: S.frozen = True
                    S.phase = "s5z"
                    for hf in range(2):
                        wv, wr = load_w("sp", scr[("B", l)][:, hf * 4096:(hf + 1) * 4096].rearrange("p (a b) -> p a b", b=128), [32, 128], reads=[Rscr[("B", l)]])
                        if nn > 1:
                            wv2, wr2 = load_w("sp", scr[("B2", l)][:, hf * 4096:(hf + 1) * 4096].rearrange("p (a b) -> p a b", b=128), [32, 128], reads=[Rscr[("B2", l)]])
                        for g2p in range(8):
                            pb_, pr_ = bank()
                            for a in range(2):
                                g2 = hf * 16 + g2p * 2 + a
                                for par in range(2):
                                    g = 2 * g2 + par; gl = g - hf * 32
                                    rows = slice(par * 64, par * 64 + 64)
                                    for ri in range(2):
                                        c_ = (a * 2 + ri) * 128
                                        mm(pb_[rows, c_:c_ + nn], wv[:, gl, ri * 64:(ri + 1) * 64], Uc[:, g, 0:nn], True, nn == 1, [wr, RUc[g // 4]], [pr_])
                                        if nn > 1:
                                            mm(pb_[rows, c_ + 1:c_ + nn], wv2[:, gl, ri * 64:(ri + 1) * 64], Uc[:, g, 0:nn - 1], False, True, [wr2, RUc[g // 4]], [pr_])
                            g2a = hf * 16 + g2p * 2
                            op("act", "copy", [pr_], RSh, out=Sst[:, g2a:g2a + 2, :, ycol + 1:ycol + 1 + nn],
                               in_=pb_[:, 0:512].rearrange("p (a r n) -> p a r n", a=2, r=2)[:, :, :, 0:nn])
                    if KSTOP == "s5z" and last_piece: S.frozen = True
                if stage == 2:
                    S.phase = "s5scan"
                    def chain(c, pcol, qcol, Trr, Tis, rd, wr_):
                        t1c = scan_t1 if c == 0 else scan_t1b; t2c = scan_t2 if c == 0 else scan_t2b
                        r1 = Rscan1[c]; r2 = Rscan2[c]
                        return [
                            lambda: op("dve", "tensor_tensor", rd + [Rtab[li]], [r2], out=t2c[:, :, 0], in0=Sst[:, :, 1, pcol], in1=Tis[:, li, :, 0], op=ALU.mult),
                            lambda: op("dve", "tensor_tensor", rd + [Rtab[li]], [r2], out=t2c[:, :, 1], in0=Sst[:, :, 0, pcol], in1=Tis[:, li, :, 1], op=ALU.mult),
                            lambda: op("dve", "tensor_tensor", rd + [Rtab[li]], [r1], out=t1c, in0=Sst[:, :, :, pcol], in1=Trr[:, li], op=ALU.mult),
                            lambda: op("dve", "tensor_tensor", [r2, wr_], [wr_], out=Sst[:, :, :, qcol], in0=Sst[:, :, :, qcol], in1=t2c, op=ALU.add),
                            lambda: op("dve", "tensor_tensor", [r1, wr_], [wr_], out=Sst[:, :, :, qcol], in0=Sst[:, :, :, qcol], in1=t1c, op=ALU.add),
                        ]
                    for f_ in chain(0, ycol, ycol + 1, ArAr, AiS, RSh, RSh[0]): f_()
                    n = 1
                    while n < nn:
                        if n + 1 < nn:
                            rdO = RSh if n == 1 else [RSh[1]]
                            rdE = RSh if n == 1 else [RSh[0]]
                            cO = chain(1, ycol + n - 1, ycol + n + 1, A2rr, A2is, rdO, RSh[1])
                            cE = chain(0, ycol + n, ycol + n + 2, A2rr, A2is, rdE, RSh[0])
                            for fo, fe in zip(cO, cE):
                                fo(); fe()
                            n += 2
                        else:
                            for f_ in chain(1, ycol + n - 1, ycol + n + 1, A2rr, A2is, RSh, RSh[1]): f_()
                            n += 1
                    if KSTOP == "s5scan" and last_piece: S.frozen = True
                if stage == 3:
                    S.phase = "s5Y"
                    if not small:
                        S.realias(Rstage, [RSb])
                    if small:
                        op("act", "copy", RSh, [RSbc], out=Sbc[:, :, :, 0:nn], in_=Sst[:, :, :, ycol:ycol + nn])
                    for hf in range(2):
                        if not small:
                            op("dve", "tensor_copy", RSh, [RSbc], out=Sbc[:, :, :, 0:nn], in_=Sst[:, hf * 16:hf * 16 + 16, :, ycol:ycol + nn])
                        sbo = 0 if small else hf * 16
                        wT, rT = load_w("sp", scr[("T", l)][:, hf * 4096:(hf + 1) * 4096].rearrange("p (a b) -> p a b", b=128), [32, 128], reads=[Rscr[("T", l)]])
                        wC, rC = load_w_parts("sp", [16, 2, 128], lambda v, i, hf=hf: (v.rearrange("p a r b -> p (a r b)"), scr[("C", l)][:, hf * 4096:(hf + 1) * 4096]), 1, reads=[Rscr[("C", l)]])
                        for gq in range(8):
                            pb_, pr_ = bank()
                            for gl4 in range(4):
                                gl = gq * 4 + gl4; g = hf * 32 + gl; g2 = g // 2; par = g % 2
                                rows = slice(par * 64, par * 64 + 64)
                                o_ = pb_[0:nn, gl4 * 128:(gl4 + 1) * 128]
                                mm(o_, Uc[:, g, 0:nn], wT[:, gl, :], True, False, [RUc[g // 4], rT], [pr_])
                                mm(o_, Sbc[rows, g2 - sbo, 0, 0:nn], wC[rows, g2 - hf * 16, 0, :], False, False, [RSbc, rC], [pr_])
                                mm(o_, Sbc[rows, g2 - sbo, 1, 0:nn], wC[rows, g2 - hf * 16, 1, :], False, True, [RSbc, rC], [pr_])
                            gq_abs = hf * 8 + gq
                            op("act", "activation", [pr_], Rutm, out=u_tm[0:nn, :, 64 * gq_abs:64 * gq_abs + 64].rearrange("p j (g o) -> p g j o", g=4),
                               in_=pb_[0:nn, 0:512].rearrange("p (g j o) -> p g j o", g=4, j=8), func=AF.Gelu_apprx_tanh)
                    if not small:
                        S.realias([RSb], Rstage)
                    if KSTOP == "s5Y" and last_piece: S.frozen = True
                    S.phase = "s5ytr"
                    for cbk in range(8):
                        pb_, pr_ = bank()
                        pbb = pb_[:].bitcast(BF16)
                        for j in range(8):
                            tr(pbb[:, j * 128:j * 128 + nn], u_tm[0:nn, j, cbk * 128:(cbk + 1) * 128], identb[0:nn, 0:nn], [Rutm[j], Rc], [pr_])
                        ydst = yfmA if small else yfm
                        op("dve" if cbk % 2 == 0 else "act", "tensor_copy" if cbk % 2 == 0 else "copy", [pr_], [RyA if small else Ry3[8 + cbk]],
                           out=ydst[:, cbk, 8 * n0:8 * (n0 + nn)].rearrange("p (n j) -> p j n", j=8),
                           in_=pbb[:, 0:1024].rearrange("p (j n) -> p j n", n=128)[:, :, 0:nn])
                    if not small:
                        S.realias(RU, Ry3[8:16])

            for pi_, (n0, nn) in enumerate(pieces): piece_stage(pi_, n0, nn, 1)
            zgate()
            for pi_, (n0, nn) in enumerate(pieces): piece_stage(pi_, n0, nn, 2)
            for pi_, (n0, nn) in enumerate(pieces): piece_stage(pi_, n0, nn, 3)

            if KSTOP == "s5yfm": S.frozen = True
            op("dve", "tensor_copy", RSh, [Rcarry[li]], out=carry[:, li], in_=Sst[:, :, :, ncol])
            if len(pieces) > 1:
                op("dve", "tensor_copy", [RyA], Ry3[8:16], out=yfm[:, :, 0:16], in_=yfmA[:, :, 0:16])
            S.phase = "s5glu"
            for c_ in range(8):
                op("dve", "tensor_tensor", [Ry3[8 + c_], Ry3[c_]], [Ry3[c_]], out=y3[:, c_, 0:T], in0=yfm[:, c_, 0:T], in1=y3[:, c_, 0:T], op=ALU.mult)
            for cs_ in range(2):
                wv, wr = load_w("pool", D[pre + "w_glu"][:, cs_ * 512:(cs_ + 1) * 512].rearrange("(k p) c -> p k c", p=128), [8, 512])
                for cbk in range(4):
                    cblk = cs_ * 4 + cbk
                    for (t0, tn) in chunk_tiles(ci):
                        pb_, pr_ = bank()
                        for k in range(8):
                            mm(pb_[:, 0:tn], wv[:, k, cbk * 128:(cbk + 1) * 128], yfm[:, k, t0:t0 + tn], k == 0, k == 7, [wr, Ry3[8 + k]], [pr_])
                        jj = cblk % 2
                        op("act", "activation", [pr_, Rc], [Rsig[jj]], out=sig[jj][:, 0:tn], in_=pb_[:, 0:tn], func=AF.Sigmoid, bias=bglu[:, li, cblk:cblk + 1])
                        op("dve", "tensor_tensor", [Rsig[jj], Ry3[cblk]], [Ry3[cblk]], out=y3[:, cblk, t0:t0 + tn], in0=sig[jj][:, 0:tn], in1=y3[:, cblk, t0:t0 + tn], op=ALU.mult)
            S.phase = "s5out"
            out_proj(ci, pre + "w_out", 8)

        first_s5 = {0: True, 3: True}
        for ci in range(nchunks):
            cur_ci[0] = ci
            if ci == 1:
                S.realias(Rh_all, Rh_all)
            load_chunk(ci)
            if KSTOP == "load": S.frozen = True
            for l in layers:
                if l in (0, 3):
                    li = 0 if l == 0 else 1
                    use_work("s5")
                    if first_s5[l]:
                        op("dve", "memset", [], RSh, Sst[:, :, :, 0], 0.0)
                        first_s5[l] = False
                    else:
                        op("dve", "tensor_copy", [Rcarry[li]], RSh, out=Sst[:, :, :, 0], in_=carry[:, li])
                    layer_s5_full(ci, l)
                elif l == 1:
                    layer_conv(ci)
                else:
                    layer_pool(ci)
            final_store(ci)
        if _os.environ.get("KDUMP", ""):
            S.frozen = False
            regs = [(off_hn, 16640), (off_y3, 33280), (work0, WORK_BYTES), (off_stage, 8192)]
            if _os.environ["KDUMP"] != "1":
                regs = [tuple(int(v_) for v_ in t_.split(":")) for t_ in _os.environ["KDUMP"].split(",")]
            allres = Rh_all + [Rhn] + Ry3 + Rring + s5work + convwork + poolwork + Rstage + RU + [RSb] + RSh + Rutm
            ov = out_d.rearrange("(p r) c -> p (r c)", p=128)
            pos = 0
            Rdump = Res("dump")
            for (o_, n_) in regs:
                dma("sp", ov[:, pos // 4:(pos + n_) // 4], arena[:, o_ // 4:(o_ + n_) // 4], allres, [Rdump], semres=Res("dumpsem%d" % pos))
                pos += n_
        S.emit()
    return nc


_CACHE = {}


def kernel(**inputs):
    x = np.ascontiguousarray(inputs["x"], dtype=np.float32)
    B = x.shape[0]
    if "nc" not in _CACHE:
        _CACHE["nc"] = build_program()
    nc = _CACHE["nc"]
    consts = host_consts()
    shared = {n: np.ascontiguousarray(inputs[n], dtype=np.float32) for n in PARAM_NAMES}
    shared.update(consts)
    in_maps = []
    for b in range(B):
        m = dict(shared)
        m["x"] = x[b]
        in_maps.append(m)
    res = run_bass_kernel_spmd(nc, in_maps, core_ids=list(range(B)))
    out = np.stack([np.asarray(r["out"], dtype=np.float32) for r in res.results], axis=0)
    return out
```
